# Optimizing a Trainium2 kernel written in Bass

```python
import math
import jax
import jax.numpy as jnp
from jax import lax
import numpy as np

D_MODEL = 4096
BATCH = 4
SEQ = 4096
DEPTH = 2

D_MIX = D_MODEL
EPS = 1e-6
ROPE_THETA = 10000.0

RET_HEADS = 4
RET_DV = D_MIX // 16
RET_DK = RET_DV // 2
RET_WIDTH = RET_HEADS * RET_DV
RET_CHUNK = 128

SSM_WIDTH = D_MIX // 4
SSM_GROUP = 16
SSM_GROUPS = SSM_WIDTH // SSM_GROUP
SSM_STATE = 64
SSM_DT_MIN = 1e-3
SSM_DT_MAX = 1e-1

SWA_HEAD_DIM = 64
SWA_Q_HEADS = (D_MIX // 4) // SWA_HEAD_DIM
SWA_KV_HEADS = max(1, SWA_Q_HEADS // 8)
SWA_WIDTH = SWA_Q_HEADS * SWA_HEAD_DIM
SWA_WINDOW = 128

POOL_WIDTH = D_MIX - RET_WIDTH - SSM_WIDTH - SWA_WIDTH
POOL_WINDOWS = (2, 4, 8, 16)
POOL_GROUP = POOL_WIDTH // len(POOL_WINDOWS)

IN_SPLITS = (RET_HEADS * RET_DK, RET_HEADS * RET_DK, RET_WIDTH, RET_WIDTH,
             SSM_WIDTH,
             SWA_WIDTH, SWA_KV_HEADS * SWA_HEAD_DIM, SWA_KV_HEADS * SWA_HEAD_DIM,
             POOL_WIDTH)
IN_COLS = sum(IN_SPLITS)
OUT_SPLITS = (RET_WIDTH, SSM_WIDTH, SWA_WIDTH, POOL_WIDTH)

PEER_N_KEYS = 128
PEER_EXPERTS = PEER_N_KEYS ** 2
PEER_HEADS = 8
PEER_TOPK = 16
PEER_QDIM = 256
PEER_TOKEN_BLOCK = 128

kernel_name = "hybrid_retnet_s5_swa_pool_peer_adaln"


def _split(a, sizes):
    return jnp.split(a, np.cumsum(sizes)[:-1].tolist(), axis=-1)


def rms_norm(x, g):
    xf = x.astype(jnp.float32)
    y = xf * lax.rsqrt(jnp.mean(xf * xf, axis=-1, keepdims=True) + EPS)
    return (y * g.astype(jnp.float32)).astype(x.dtype)


def rotary(x, positions, theta=ROPE_THETA):
    d = x.shape[-1]
    half = d // 2
    inv = theta ** (-jnp.arange(half, dtype=jnp.float32) * 2.0 / d)
    ang = positions.astype(jnp.float32)[..., None] * inv
    cos = jnp.cos(ang)[:, :, None, :]
    sin = jnp.sin(ang)[:, :, None, :]
    xf = x.astype(jnp.float32)
    x1, x2 = xf[..., :half], xf[..., half:]
    return jnp.concatenate([x1 * cos - x2 * sin, x2 * cos + x1 * sin], axis=-1).astype(x.dtype)


def retention(q, k, v, g, positions):
    B, S, H, dk = q.shape
    dv = v.shape[-1]
    L = RET_CHUNK
    n = S // L
    q = rotary(q, positions)
    k = rotary(k, positions) * (dk ** -0.5)
    log_g = jnp.log(1.0 - 2.0 ** (-5.0 - jnp.arange(H, dtype=jnp.float32)))
    idx = jnp.arange(L, dtype=jnp.float32)
    diff = idx[:, None] - idx[None, :]
    decay = jnp.where(diff >= 0, jnp.exp(jnp.maximum(diff, 0.0)[None] * log_g[:, None, None]), 0.0)
    w_k = jnp.exp((L - 1 - idx)[None, :] * log_g[:, None])
    w_q = jnp.exp((idx + 1)[None, :] * log_g[:, None])
    gamma_L = jnp.exp(L * log_g)[None, :, None, None]

    qc = q.reshape(B, n, L, H, dk)
    kc = k.reshape(B, n, L, H, dk)
    vc = v.reshape(B, n, L, H, dv)
    scores = jnp.einsum('bnlhd,bnmhd->bnhlm', qc, kc) * decay[None, None]
    o_in = jnp.einsum('bnhlm,bnmhe->bnlhe', scores, vc)
    kv = jnp.einsum('bnmhd,hm,bnmhe->bnhde', kc, w_k, vc)

    def step(R, kv_n):
        return gamma_L * R + kv_n, R

    _, r_prev = lax.scan(step, jnp.zeros((B, H, dk, dv), kv.dtype), jnp.moveaxis(kv, 1, 0))
    r_prev = jnp.moveaxis(r_prev, 0, 1)
    o_cross = jnp.einsum('bnlhd,hl,bnhde->bnlhe', qc, w_q, r_prev)

    o = (o_in + o_cross).reshape(B, S, H, dv).astype(jnp.float32)
    mu = jnp.mean(o, axis=-1, keepdims=True)
    var = jnp.mean(jnp.square(o - mu), axis=-1, keepdims=True)
    o = (o - mu) * lax.rsqrt(var + EPS)
    return (jax.nn.silu(g.astype(jnp.float32)) * o).reshape(B, S, H * dv).astype(g.dtype)


def s5_ssm(u, a_re, a_im, log_dt, b_re, b_im, c_re, c_im, d, w_glu, b_glu):
    B, S, W = u.shape
    f32 = jnp.float32
    uf = u.astype(f32)
    ug = uf.reshape(B, S, SSM_GROUPS, SSM_GROUP)
    ar = a_re.astype(f32)
    ai = a_im.astype(f32)
    dt = jnp.exp(log_dt.astype(f32))[:, None]
    mag = jnp.exp(ar * dt)
    ab_re = mag * jnp.cos(ai * dt)
    ab_im = mag * jnp.sin(ai * dt)
    den = ar * ar + ai * ai
    nr = ab_re - 1.0
    f_re = (nr * ar + ab_im * ai) / den
    f_im = (ab_im * ar - nr * ai) / den
    br = b_re.astype(f32)
    bi = b_im.astype(f32)
    bb_re = f_re[..., None] * br - f_im[..., None] * bi
    bb_im = f_re[..., None] * bi + f_im[..., None] * br
    x_re = jnp.einsum('bsgc,gpc->sbgp', ug, bb_re)
    x_im = jnp.einsum('bsgc,gpc->sbgp', ug, bb_im)
    a_seq_re = jnp.broadcast_to(ab_re[None, None], (S, 1) + ab_re.shape)
    a_seq_im = jnp.broadcast_to(ab_im[None, None], (S, 1) + ab_im.shape)

    def combine(e_i, e_j):
        ar_i, ai_i, br_i, bi_i = e_i
        ar_j, ai_j, br_j, bi_j = e_j
        return (ar_j * ar_i - ai_j * ai_i,
                ar_j * ai_i + ai_j * ar_i,
                ar_j * br_i - ai_j * bi_i + br_j,
                ar_j * bi_i + ai_j * br_i + bi_j)

    _, _, h_re, h_im = lax.associative_scan(combine, (a_seq_re, a_seq_im, x_re, x_im), axis=0)
    y = (jnp.einsum('sbgp,gcp->bsgc', h_re, c_re.astype(f32))
         - jnp.einsum('sbgp,gcp->bsgc', h_im, c_im.astype(f32)))
    y = y.reshape(B, S, W) + d.astype(f32) * uf
    y = jax.nn.gelu(y, approximate=False)
    y = y * jax.nn.sigmoid(y @ w_glu.astype(f32) + b_glu.astype(f32))
    return y.astype(u.dtype)


def sliding_window_attention(q, k, v, q_norm, k_norm, sinks, positions):
    B, S, Hq, d = q.shape
    Hkv = k.shape[2]
    G = Hq // Hkv
    W = SWA_WINDOW
    nb = S // W
    q = rotary(rms_norm(q, q_norm), positions)
    k = rotary(rms_norm(k, k_norm), positions)
    qb = q.reshape(B, nb, W, Hkv, G, d)
    pad = ((0, 0), (W, 0), (0, 0), (0, 0))
    kb = jnp.pad(k, pad).reshape(B, nb + 1, W, Hkv, d)
    vb = jnp.pad(v, pad).reshape(B, nb + 1, W, Hkv, d)
    k_band = jnp.concatenate([kb[:, :-1], kb[:, 1:]], axis=2)
    v_band = jnp.concatenate([vb[:, :-1], vb[:, 1:]], axis=2)
    scores = jnp.einsum('bnqhgd,bnkhd->bnhgqk', qb, k_band).astype(jnp.float32) * (d ** -0.5)
    qi = jnp.arange(W)[:, None] + W
    ki = jnp.arange(2 * W)[None, :]
    rel = qi - ki
    kglob = jnp.arange(nb)[:, None, None] * W - W + ki[None]
    mask = (rel >= 0)[None] & (rel < W)[None] & (kglob >= 0)
    scores = jnp.where(mask[None, :, None, None], scores, -1e30)
    sink = jnp.broadcast_to(sinks.astype(jnp.float32).reshape(Hkv, G)[None, None, :, :, None, None],
                            scores.shape[:-1] + (1,))
    probs = jax.nn.softmax(jnp.concatenate([scores, sink], axis=-1), axis=-1)[..., :-1]
    out = jnp.einsum('bnhgqk,bnkhd->bnqhgd', probs.astype(v.dtype), v_band)
    return out.reshape(B, S, Hq * d)


def multiscale_pool(u, pool_w, pool_scale):
    B, S, _ = u.shape
    nG = len(POOL_WINDOWS)
    uf = u.astype(jnp.float32).reshape(B, S, nG, POOL_GROUP)
    cs = jnp.pad(jnp.cumsum(uf, axis=1), ((0, 0), (1, 0), (0, 0), (0, 0)))
    t = jnp.arange(S)[:, None]
    win = jnp.array(POOL_WINDOWS, dtype=jnp.int32)[None, :]
    lo = jnp.maximum(t + 1 - win, 0)
    grp = jnp.arange(nG)[None, :]
    window_sum = cs[:, t + 1, grp] - cs[:, lo, grp]
    count = (t + 1 - lo).astype(jnp.float32)[None, :, :, None]
    pooled = window_sum / count - uf
    y = jnp.einsum('bsgc,gcd->bsgd', pooled, pool_w.astype(jnp.float32)).reshape(B, S, POOL_WIDTH)
    return (y * pool_scale.astype(jnp.float32)).astype(u.dtype)


def peer_ffn(h, w_query, sub_keys, u_tab, v_tab):
    B, S, D = h.shape
    T = B * S
    K = PEER_TOPK
    hf = h.reshape(T, D)
    q = (hf @ w_query).reshape(T, PEER_HEADS, 2, PEER_QDIM // 2)
    scores = jnp.einsum('thpd,pnd->thpn', q, sub_keys).astype(jnp.float32)
    top_s, top_i = lax.top_k(scores, K)
    cand_s = top_s[:, :, 0, :, None] + top_s[:, :, 1, None, :]
    cand_e = top_i[:, :, 0, :, None] * PEER_N_KEYS + top_i[:, :, 1, None, :]
    best_s, best_pos = lax.top_k(cand_s.reshape(T, PEER_HEADS, K * K), K)
    experts = jnp.take_along_axis(cand_e.reshape(T, PEER_HEADS, K * K), best_pos, axis=-1)
    gates = jax.nn.softmax(best_s, axis=-1)

    nblk = T // PEER_TOKEN_BLOCK
    HK = PEER_HEADS * K

    def block(args):
        xb, eb, gb = args
        ub = u_tab[eb]
        a = jnp.einsum('ted,td->te', ub, xb).astype(jnp.float32)
        a = gb * jax.nn.gelu(a, approximate=False)
        vb = v_tab[eb]
        return jnp.einsum('te,ted->td', a.astype(vb.dtype), vb)

    out = lax.map(block, (hf.reshape(nblk, PEER_TOKEN_BLOCK, D),
                          experts.reshape(nblk, PEER_TOKEN_BLOCK, HK),
                          gates.reshape(nblk, PEER_TOKEN_BLOCK, HK)))
    return out.reshape(B, S, D).astype(h.dtype)


def hybrid_layer(x, c, positions, ada_w, ada_b, norm1_g, norm2_g, w_in, w_out, out_norm_g,
                 ssm_a_re, ssm_a_im, ssm_log_dt, ssm_b_re, ssm_b_im, ssm_c_re, ssm_c_im,
                 ssm_d, ssm_w_glu, ssm_b_glu, attn_q_norm, attn_k_norm, attn_sinks,
                 pool_w, pool_scale, peer_w_query, peer_sub_keys, peer_u, peer_v):
    B, S, _ = x.shape
    mod = jax.nn.silu(c) @ ada_w + ada_b
    sh1, sc1, g1, sh2, sc2, g2 = [m[:, None, :] for m in jnp.split(mod, 6, axis=-1)]

    h = rms_norm(x, norm1_g) * (1.0 + sc1) + sh1
    proj = h @ w_in
    q_r, k_r, v_r, g_r, u_s, q_a, k_a, v_a, u_p = _split(proj, IN_SPLITS)
    ret = retention(q_r.reshape(B, S, RET_HEADS, RET_DK), k_r.reshape(B, S, RET_HEADS, RET_DK),
                    v_r.reshape(B, S, RET_HEADS, RET_DV), g_r.reshape(B, S, RET_HEADS, RET_DV), positions)
    ssm = s5_ssm(u_s, ssm_a_re, ssm_a_im, ssm_log_dt, ssm_b_re, ssm_b_im, ssm_c_re, ssm_c_im,
                 ssm_d, ssm_w_glu, ssm_b_glu)
    swa = sliding_window_attention(q_a.reshape(B, S, SWA_Q_HEADS, SWA_HEAD_DIM),
                                   k_a.reshape(B, S, SWA_KV_HEADS, SWA_HEAD_DIM),
                                   v_a.reshape(B, S, SWA_KV_HEADS, SWA_HEAD_DIM),
                                   attn_q_norm, attn_k_norm, attn_sinks, positions)
    pool = multiscale_pool(u_p, pool_w, pool_scale)
    gains = _split(out_norm_g, OUT_SPLITS)
    mix = jnp.concatenate([rms_norm(o, gg) for o, gg in zip((ret, ssm, swa, pool), gains)], axis=-1)
    x = x + g1 * (mix @ w_out)

    h = rms_norm(x, norm2_g) * (1.0 + sc2) + sh2
    x = x + g2 * peer_ffn(h, peer_w_query, peer_sub_keys, peer_u, peer_v)
    return x


def setup_inputs(seed: int = 0) -> dict:
    key = jax.random.key(seed)
    ks = jax.random.split(key, 32)
    f32 = jnp.float32
    L = DEPTH

    def normal(k, shape, scale):
        return jax.random.normal(k, shape, f32) * scale

    x = normal(ks[0], (BATCH, SEQ, D_MODEL), 1.0)
    c = normal(ks[1], (BATCH, D_MODEL), 1.0)
    offset = jax.random.randint(ks[2], (BATCH, 1), 0, 1024, dtype=jnp.int32)
    positions = offset + jnp.arange(SEQ, dtype=jnp.int32)[None, :]

    ada_w = normal(ks[3], (L, D_MODEL, 6 * D_MODEL), 0.5 * D_MODEL ** -0.5)
    ada_b = normal(ks[4], (L, 6 * D_MODEL), 0.01)
    norm1_g = 1.0 + normal(ks[5], (L, D_MODEL), 0.01)
    norm2_g = 1.0 + normal(ks[6], (L, D_MODEL), 0.01)
    w_in = normal(ks[7], (L, D_MODEL, IN_COLS), D_MODEL ** -0.5)
    w_out = normal(ks[8], (L, D_MIX, D_MODEL), D_MIX ** -0.5)
    out_norm_g = 1.0 + normal(ks[9], (L, D_MIX), 0.01)

    ssm_a_re = -0.5 + normal(ks[10], (L, SSM_GROUPS, SSM_STATE), 0.01)
    ssm_a_im = (jnp.pi * jnp.arange(SSM_STATE, dtype=f32))[None, None, :] + normal(ks[11], (L, SSM_GROUPS, SSM_STATE), 0.01)
    ssm_log_dt = jax.random.uniform(ks[12], (L, SSM_GROUPS), f32, math.log(SSM_DT_MIN), math.log(SSM_DT_MAX))
    b_scale = (2.0 * SSM_GROUP) ** -0.5
    c_scale = (2.0 * SSM_STATE) ** -0.5
    ssm_b_re = normal(ks[13], (L, SSM_GROUPS, SSM_STATE, SSM_GROUP), b_scale)
    ssm_b_im = normal(ks[14], (L, SSM_GROUPS, SSM_STATE, SSM_GROUP), b_scale)
    ssm_c_re = normal(ks[15], (L, SSM_GROUPS, SSM_GROUP, SSM_STATE), c_scale)
    ssm_c_im = normal(ks[16], (L, SSM_GROUPS, SSM_GROUP, SSM_STATE), c_scale)
    ssm_d = normal(ks[17], (L, SSM_WIDTH), 1.0)
    ssm_w_glu = normal(ks[18], (L, SSM_WIDTH, SSM_WIDTH), SSM_WIDTH ** -0.5)
    ssm_b_glu = normal(ks[19], (L, SSM_WIDTH), 0.01)

    attn_q_norm = 1.0 + normal(ks[20], (L, SWA_HEAD_DIM), 0.01)
    attn_k_norm = 1.0 + normal(ks[21], (L, SWA_HEAD_DIM), 0.01)
    attn_sinks = normal(ks[22], (L, SWA_Q_HEADS), 1.0)

    pool_w = normal(ks[23], (L, len(POOL_WINDOWS), POOL_GROUP, POOL_GROUP), POOL_GROUP ** -0.5)
    pool_scale = 1.0 + normal(ks[24], (L, POOL_WIDTH), 0.1)

    peer_w_query = normal(ks[25], (L, D_MODEL, PEER_HEADS * PEER_QDIM), D_MODEL ** -0.5)
    peer_sub_keys = normal(ks[26], (L, 2, PEER_N_KEYS, PEER_QDIM // 2), (PEER_QDIM // 2) ** -0.5)
    peer_u = normal(ks[27], (L, PEER_EXPERTS, D_MODEL), D_MODEL ** -0.5)
    peer_v = normal(ks[28], (L, PEER_EXPERTS, D_MODEL), PEER_HEADS ** -0.5)

    return {"x": x, "c": c, "positions": positions,
            "ada_w": ada_w, "ada_b": ada_b, "norm1_g": norm1_g, "norm2_g": norm2_g,
            "w_in": w_in, "w_out": w_out, "out_norm_g": out_norm_g,
            "ssm_a_re": ssm_a_re, "ssm_a_im": ssm_a_im, "ssm_log_dt": ssm_log_dt,
            "ssm_b_re": ssm_b_re, "ssm_b_im": ssm_b_im, "ssm_c_re": ssm_c_re, "ssm_c_im": ssm_c_im,
            "ssm_d": ssm_d, "ssm_w_glu": ssm_w_glu, "ssm_b_glu": ssm_b_glu,
            "attn_q_norm": attn_q_norm, "attn_k_norm": attn_k_norm, "attn_sinks": attn_sinks,
            "pool_w": pool_w, "pool_scale": pool_scale,
            "peer_w_query": peer_w_query, "peer_sub_keys": peer_sub_keys,
            "peer_u": peer_u, "peer_v": peer_v}


def reference(x, c, positions, ada_w, ada_b, norm1_g, norm2_g, w_in, w_out, out_norm_g,
              ssm_a_re, ssm_a_im, ssm_log_dt, ssm_b_re, ssm_b_im, ssm_c_re, ssm_c_im,
              ssm_d, ssm_w_glu, ssm_b_glu, attn_q_norm, attn_k_norm, attn_sinks,
              pool_w, pool_scale, peer_w_query, peer_sub_keys, peer_u, peer_v):
    for i in range(DEPTH):
        x = hybrid_layer(x, c, positions, ada_w[i], ada_b[i], norm1_g[i], norm2_g[i], w_in[i], w_out[i],
                         out_norm_g[i], ssm_a_re[i], ssm_a_im[i], ssm_log_dt[i], ssm_b_re[i], ssm_b_im[i],
                         ssm_c_re[i], ssm_c_im[i], ssm_d[i], ssm_w_glu[i], ssm_b_glu[i],
                         attn_q_norm[i], attn_k_norm[i], attn_sinks[i], pool_w[i], pool_scale[i],
                         peer_w_query[i], peer_sub_keys[i], peer_u[i], peer_v[i])
    return x
```

```python
import numpy as np
import concourse.bass as bass
import concourse.mybir as mybir

F32 = mybir.dt.float32
BF16 = mybir.dt.bfloat16
I32 = mybir.dt.int32
U32 = mybir.dt.uint32
ALU = mybir.AluOpType
AF = mybir.ActivationFunctionType
AX = mybir.AxisListType

ENGS = ("pe", "act", "dve", "pool", "sp")


class Prog:
    NDS = 12

    def __init__(self, nc, stack):
        self.nc = nc
        self.ops = []
        self.csem = {}
        self.ccnt = {}
        self.dsem = {}
        self.dcnt = {}
        for e in ENGS:
            if e != "sp":
                self.csem[e] = stack.enter_context(nc.semaphore("c_" + e))
                self.ccnt[e] = 0
        for e in ("sp", "act", "pool"):
            self.dsem[e] = [stack.enter_context(nc.semaphore("d_%s%d" % (e, i))) for i in range(self.NDS)]
            self.dcnt[e] = 0
        self.seen = {e: {} for e in ENGS}
        self.nstage = 0

    def add(self, eng, fn, r=(), w=(), dma=False):
        self.ops.append((eng, fn, tuple(r), tuple(w), dma))

    def dma(self, out, in_, r, w, q="sp", **kw):
        self.add(q, lambda e: e.dma_start(out=out, in_=in_, **kw), r, w, dma=True)

    def emit(self):
        nc = self.nc
        ops = self.ops
        self.ops = []
        n = len(ops)
        deps = [None] * n
        last_w = {}
        readers = {}
        for i, (eng, fn, r, w, dma) in enumerate(ops):
            d = set()
            for k in r:
                j = last_w.get(k)
                if j is not None:
                    d.add(j)
            for k in w:
                j = last_w.get(k)
                if j is not None:
                    d.add(j)
                rr = readers.get(k)
                if rr:
                    d.update(rr)
            for k in r:
                readers.setdefault(k, []).append(i)
            for k in w:
                last_w[k] = i
                readers[k] = []
            d.discard(i)
            if eng == "pe":
                d = {j for j in d if not (ops[j][0] == "pe" and not ops[j][4])}
            deps[i] = d
        hasdep = [False] * n
        for d in deps:
            for j in d:
                hasdep[j] = True
        lastc = {}
        for i, (eng, fn, r, w, dma) in enumerate(ops):
            if not dma and fn is not None:
                lastc[eng] = i
        for e, i in lastc.items():
            hasdep[i] = True
        sig = [None] * n
        prew = [None] * n
        for i, (eng, fn, r, w, dma) in enumerate(ops):
            if dma:
                k = self.dcnt[eng]
                self.dcnt[eng] = k + 1
                s = self.dsem[eng][k % self.NDS]
                v = 16 * (k // self.NDS + 1)
                sig[i] = (s, v, 16)
                if v > 16:
                    prew[i] = (s, v - 16)
            elif fn is not None and hasdep[i]:
                self.ccnt[eng] += 1
                sig[i] = (self.csem[eng], self.ccnt[eng], 1)
        byeng = {e: [] for e in ENGS}
        for i, op in enumerate(ops):
            byeng[op[0]].append(i)
        seen = self.seen

        def run(e, eobj):
            sn = seen[e]
            for i in byeng[e]:
                eng, fn, r, w, dma = ops[i]
                for j in sorted(deps[i]):
                    s, v, _ = sig[j]
                    if sn.get(id(s), 0) < v:
                        eobj.wait_ge(s, v)
                        sn[id(s)] = v
                if prew[i] is not None:
                    s, v = prew[i]
                    key = id(s)
                    if sn.get(key, 0) < v:
                        eobj.wait_ge(s, v)
                        sn[key] = v
                if fn is not None:
                    ins = fn(eobj)
                    if sig[i] is not None:
                        s, v, inc = sig[i]
                        ins.then_inc(s, inc)
            for ee in ENGS:
                if ee != "sp":
                    s = self.csem[ee]
                    v = self.ccnt[ee]
                    key = id(s)
                    if v > 0 and sn.get(key, 0) < v:
                        eobj.wait_ge(s, v)
                        sn[key] = v
            for q in self.dsem:
                k = self.dcnt[q]
                for si, s in enumerate(self.dsem[q]):
                    cnt = (k - si + self.NDS - 1) // self.NDS if k > si else 0
                    v = 16 * cnt
                    key = id(s)
                    if v > 0 and sn.get(key, 0) < v:
                        eobj.wait_ge(s, v)
                        sn[key] = v

        with nc.Block() as block:
            @block.sync
            def _(e):
                run("sp", e)

            @block.scalar
            def _(e):
                run("act", e)

            @block.vector
            def _(e):
                run("dve", e)

            @block.gpsimd
            def _(e):
                run("pool", e)

            @block.tensor
            def _(e):
                run("pe", e)
        self.nstage += 1
        return n


from contextlib import ExitStack
import math
import numpy as np
import concourse.bass as bass
import concourse.mybir as mybir

D = 4096
INC = 6400
EPS = 1e-6
TWO_PI = float(2 * np.pi)
C_QR, C_KR, C_VR, C_GR, C_US, C_QA, C_KA, C_VA, C_UP = 0, 512, 1024, 2048, 3072, 4096, 5120, 5248, 5376


class Ctx:
    pass


def host_consts():
    c = {}
    c["ident"] = np.eye(128, dtype=np.float32)
    inv64 = 10000.0 ** (-np.arange(64, dtype=np.float32) * 2.0 / 128)
    inv32 = 10000.0 ** (-np.arange(32, dtype=np.float32) * 2.0 / 64)
    c["rinv"] = np.concatenate([inv64, inv32]).astype(np.float32)[None, :]
    H, Lc, dk = 4, 128, 128
    log_g = np.log(1.0 - 2.0 ** (-5.0 - np.arange(H, dtype=np.float64)))
    idx = np.arange(Lc, dtype=np.float64)
    diff = idx[None, :] - idx[:, None]
    dec = np.where(diff[:, None, :] >= 0, np.exp(np.maximum(diff[:, None, :], 0) * log_g[None, :, None]), 0.0)
    c["ret_decT"] = (dec * dk ** -0.5).astype(np.float32).reshape(128, H * 128)
    wq = np.exp((idx + 1)[None, :] * log_g[:, None])
    c["ret_wq"] = np.broadcast_to(wq.reshape(1, H * 128), (128, H * 128)).astype(np.float32).copy()
    wk = np.exp((Lc - 1 - idx)[None, :] * log_g[:, None]) * dk ** -0.5
    c["ret_wk"] = wk.T.astype(np.float32).copy()
    c["ret_gL"] = [float(np.exp(Lc * lg)) for lg in log_g]
    kk = np.arange(128)[:, None]
    qq = np.arange(128)[None, :]
    c["swa_mcur"] = (kk <= qq).astype(np.float32)
    c["swa_mprev"] = (kk > qq).astype(np.float32)
    wins = (2, 4, 8, 16)
    mcur = np.zeros((128, 4, 128), np.float32)
    mcur0 = np.zeros((128, 4, 128), np.float32)
    mprev = np.zeros((128, 4, 128), np.float32)
    icnt = np.zeros((128, 4, 128), np.float32)
    icnt0 = np.zeros((128, 4, 128), np.float32)
    for gi, w in enumerate(wins):
        for t in range(128):
            for s in range(t - w + 1, t + 1):
                if s >= 0:
                    mcur[s, gi, t] += 1
                    mcur0[s, gi, t] += 1
                else:
                    mprev[s + 128, gi, t] += 1
            mcur[t, gi, t] -= w
            cnt0 = min(w, t + 1)
            mcur0[t, gi, t] -= cnt0
            icnt[:, gi, t] = 1.0 / w
            icnt0[:, gi, t] = 1.0 / cnt0
    c["pool_mcur"] = mcur.reshape(128, 512)
    c["pool_mcur0"] = mcur0.reshape(128, 512)
    c["pool_mprev"] = mprev.reshape(128, 512)
    c["pool_icnt"] = icnt.reshape(128, 512)
    c["pool_icnt0"] = icnt0.reshape(128, 512)
    c["iota128"] = np.broadcast_to(np.arange(128, dtype=np.float32)[None, :], (128, 128)).copy()
    c["iota512"] = np.broadcast_to(np.arange(1, 513, dtype=np.float32)[None, :], (128, 512)).copy()
    c["iota16"] = np.broadcast_to(np.arange(16, dtype=np.float32)[None, None, :], (128, 128, 16)).reshape(128, 2048).copy()
    return c


CONST_SHAPES = {"ident": [128, 128], "rinv": [1, 96], "ret_decT": [128, 512], "ret_wq": [128, 512], "ret_wk": [128, 4],
                "swa_mcur": [128, 128], "swa_mprev": [128, 128], "pool_mcur": [128, 512], "pool_mcur0": [128, 512],
                "pool_mprev": [128, 512], "pool_icnt": [128, 512], "pool_icnt0": [128, 512], "iota128": [128, 128],
                "iota512": [128, 512], "iota16": [128, 2048]}

WEIGHT_SHAPES = {
    "ada_w": [D, 6 * D], "ada_b": [6 * D], "norm1_g": [D], "norm2_g": [D], "w_in": [D, INC], "w_out": [D, D],
    "out_norm_g": [D], "ssm_a_re": [64, 64], "ssm_a_im": [64, 64], "ssm_log_dt": [64], "ssm_b_re": [64, 64, 16],
    "ssm_b_im": [64, 64, 16], "ssm_c_re": [64, 16, 64], "ssm_c_im": [64, 16, 64], "ssm_d": [1024],
    "ssm_w_glu": [1024, 1024], "ssm_b_glu": [1024], "attn_q_norm": [64], "attn_k_norm": [64], "attn_sinks": [16],
    "pool_w": [4, 256, 256], "pool_scale": [1024], "peer_w_query": [D, 2048], "peer_sub_keys": [2, 128, 128],
    "peer_u": [16384, D], "peer_v": [16384, D]}


def declare(nc, T, L, dbg=(), as_input=()):
    g = Ctx()
    g.T, g.L = T, L

    def inp(name, shape, dt=F32):
        return nc.dram_tensor(name, list(shape), dt, kind="ExternalInput").ap()

    def scr(name, shape, dt=F32):
        kind = "ExternalOutput" if name in dbg else "Internal"
        if name in as_input:
            kind = "ExternalInput"
        return nc.dram_tensor(name, list(shape), dt, kind=kind).ap()

    g.x = inp("x", [T, D])
    g.c = inp("c", [1, D])
    g.pos = inp("pos", [T, 1], I32)
    g.w = {}
    for k, shp in WEIGHT_SHAPES.items():
        if len(shp) == 1:
            shp = [1] + shp
        g.w[k] = inp(k, [L] + shp)
    g.k = {}
    for k, shp in CONST_SHAPES.items():
        g.k[k] = inp("k_" + k, shp)
    g.mod = scr("mod", [L, 6 * D])
    g.rot = scr("rot", [T, 192])
    g.P = scr("P", [T, INC])
    g.PuT = scr("PuT", [1024, T])
    g.y2T = scr("y2T", [1024, T])
    g.mix = scr("mix", [T, D], BF16)
    g.x1 = scr("x1", [T, D])
    g.x2 = scr("x2", [T, D])
    g.h2T = scr("h2T", [32, 128, T], BF16)
    g.Gd = scr("Gd", [128, 128, T], BF16)
    g.qT = scr("qT", [16, 128, T], BF16)
    g.Wd = scr("Wd", [128, 128, T], BF16)
    g.UTd = scr("UTd", [128, 128, 32, 128], BF16)
    g.out = nc.dram_tensor("out", [T, D], F32, kind="ExternalOutput").ap()
    return g


class Stage:
    CNT = [0]

    def __init__(self, p, nc):
        self.p, self.nc = p, nc
        self.st = ExitStack()
        Stage.CNT[0] += 1
        self.pre = "s%d_" % Stage.CNT[0]

    def sb(self, name, shape, dt=F32):
        return self.st.enter_context(self.nc.sbuf_tensor(self.pre + name, list(shape), dt))

    def ps(self, name, shape, dt=F32):
        return self.st.enter_context(self.nc.psum_tensor(self.pre + name, list(shape), dt))

    def done(self):
        self.p.emit()
        self.st.close()

    def tt(self, eng, out, in0, in1, op, r, w):
        self.p.add(eng, lambda e: e.tensor_tensor(out=out, in0=in0, in1=in1, op=op), r, w)

    def ts(self, eng, out, in0, s1, s2, op0, op1, r, w):
        if s2 is None:
            self.p.add(eng, lambda e: e.tensor_scalar(out=out, in0=in0, scalar1=s1, scalar2=None, op0=op0), r, w)
        else:
            self.p.add(eng, lambda e: e.tensor_scalar(out=out, in0=in0, scalar1=s1, scalar2=s2, op0=op0, op1=op1), r, w)

    def stt(self, eng, out, in0, scalar, in1, op0, op1, r, w):
        self.p.add(eng, lambda e: e.scalar_tensor_tensor(out=out, in0=in0, scalar=scalar, in1=in1, op0=op0, op1=op1), r, w)

    def act(self, out, in_, func, r, w, **kw):
        self.p.add("act", lambda e: e.activation(out=out, in_=in_, func=func, **kw), r, w)

    def cp(self, eng, out, in_, r, w):
        if eng == "act":
            self.p.add("act", lambda e: e.copy(out=out, in_=in_), r, w)
        else:
            self.p.add(eng, lambda e: e.tensor_copy(out=out, in_=in_), r, w)

    def rsqrt(self, t, key, scale, eps):
        self.ts("dve", t, t, scale, eps, ALU.mult, ALU.add, [key], [key])
        self.act(t, t, AF.Sqrt, [key], [key])
        self.p.add("dve", lambda e: e.reciprocal(out=t, in_=t), [key], [key])

    def range_reduce(self, dst, src, ki, kf, kd, ks, kki, kkf):
        self.ts("dve", ki, src, 1.0 / TWO_PI, None, ALU.mult, None, [ks], [kki])
        self.cp("dve", kf, ki, [kki], [kkf])
        self.stt("dve", dst, kf, -TWO_PI, src, ALU.mult, ALU.add, [kkf, ks], [kd])

    def mm(self, fn, r, w):
        self.p.add("pe", fn, r, w)


def bcast_rows(ap_row):
    return ap_row.partition_broadcast(128)[:, 0, :]


def stage_ada(p, nc, g):
    L = g.L
    s = Stage(p, nc)
    cT = s.sb("cT", [128, 32])
    cs = s.sb("cs", [128, 32], BF16)
    wt = [s.sb("adaw%d" % i, [128, 32, 512], BF16) for i in range(4)]
    bt = [s.sb("adab%d" % i, [1, 512]) for i in range(2)]
    ob = [s.sb("adao%d" % i, [1, 512]) for i in range(2)]
    pst = [s.ps("adaps%d" % i, [1, 512]) for i in range(2)]
    p.dma(cT[:, :], g.c.rearrange("o (kc q) -> q (o kc)", q=128), r=[], w=["cT"], allow_slow_non_contiguous=True)
    s.act(cs[:, :], cT[:, :], AF.Silu, ["cT"], ["cs"])
    it = 0
    for l in range(L):
        wv = g.w["ada_w"][l].rearrange("(kc q) n -> q kc n", q=128)
        for cc in range(48):
            b = it % 2
            b3 = it % 4
            it += 1
            cols = slice(cc * 512, (cc + 1) * 512)
            p.dma(wt[b3][:, :, :], wv[:, :, cols], r=[], w=["adaw%d" % b3], q="pool")
            p.dma(bt[b][:, :], g.w["ada_b"][l][:, cols], r=[], w=["adab%d" % b])

            def mm(e, b=b, b3=b3):
                ins = None
                for kc in range(32):
                    ins = e.matmul(pst[b][0:1, :], lhsT=cs[:, kc:kc + 1], rhs=wt[b3][:, kc, :], start=(kc == 0), stop=(kc == 31))
                return ins
            s.mm(mm, ["cs", "adaw%d" % b3], ["adaps%d" % b])
            s.tt("dve", ob[b][:, :], pst[b][0:1, :], bt[b][:, :], ALU.add, ["adaps%d" % b, "adab%d" % b], ["adao%d" % b])
            p.dma(g.mod[l:l + 1, cols], ob[b][:, :], r=["adao%d" % b], w=["mod"], q="act")
    s.done()


def stage_rot(p, nc, g):
    T = g.T
    s = Stage(p, nc)
    inv = s.sb("inv", [128, 96])
    posi = [s.sb("posi%d" % i, [128, 1], I32) for i in range(2)]
    posf = s.sb("posf", [128, 1])
    ang = s.sb("ang", [128, 96])
    ang2 = s.sb("ang2", [128, 96])
    ki = s.sb("ki", [128, 96], I32)
    kf = s.sb("kf", [128, 96])
    rr = s.sb("rr", [128, 96])
    res = [s.sb("res%d" % i, [128, 192]) for i in range(2)]
    p.dma(inv[:, :], bcast_rows(g.k["rinv"]), [], ["inv"])
    for tt in range(T // 128):
        b = tt % 2
        p.dma(posi[b][:, :], g.pos[tt * 128:(tt + 1) * 128, :], [], ["posi%d" % b])
        s.cp("dve", posf[:, :], posi[b][:, :], ["posi%d" % b], ["posf"])
        s.ts("dve", ang[:, :], inv[:, :], posf[:, 0:1], None, ALU.mult, None, ["inv", "posf"], ["ang"])
        s.range_reduce(rr[:, :], ang[:, :], ki[:, :], kf[:, :], "rr", "ang", "ki", "kf")
        s.act(res[b][:, 0:96], rr[:, :], AF.Sin, ["rr"], ["res%d" % b])
        s.ts("dve", ang2[:, :], ang[:, :], float(np.pi / 2), None, ALU.add, None, ["ang"], ["ang2"])
        s.range_reduce(rr[:, :], ang2[:, :], ki[:, :], kf[:, :], "rr", "ang2", "ki", "kf")
        s.act(res[b][:, 96:192], rr[:, :], AF.Sin, ["rr"], ["res%d" % b])
        p.dma(g.rot[tt * 128:(tt + 1) * 128, :], res[b][:, :], ["res%d" % b], ["rot"])
    s.done()


def load_ident(s, g):
    identf = s.sb("identf", [128, 128])
    ident = s.sb("identb", [128, 128], BF16)
    s.p.dma(identf[:, :], g.k["ident"], [], ["identf"])
    s.cp("dve", ident[:, :], identf[:, :], ["identf"], ["ident"])
    return ident, identf


def stage_proj(p, nc, g, l, xsrc):
    T = g.T
    TB = min(T, 1024)
    NB = T // TB
    NT = TB // 128
    CW = 512
    chunks = [(i * 512, 512) for i in range(12)] + [(6144, 256)]
    s = Stage(p, nc)
    ident, _ = load_ident(s, g)
    A1 = s.sb("A1", [128, D])
    B1 = s.sb("B1", [128, D])
    hT = s.sb("hT", [128, 32, TB], BF16)
    xt = s.sb("xt", [128, D])
    ssq = s.sb("ssq", [128, 1])
    hb = [s.sb("hb%d" % i, [128, D], BF16) for i in range(2)]
    wb = [s.sb("wb%d" % i, [128, 32, CW], BF16) for i in range(2)]
    ob = [s.sb("ob%d" % i, [128, CW]) for i in range(2)]
    pst = [s.ps("pst%d" % i, [128, 512], BF16) for i in range(2)]
    pso = [s.ps("pso%d" % i, [128, 512]) for i in range(2)]
    p.dma(A1[:, :], bcast_rows(g.mod[l:l + 1, D:2 * D]), ["mod"], ["A1"])
    p.dma(xt[:, :], bcast_rows(g.w["norm1_g"][l]), [], ["xt"])
    p.dma(B1[:, :], bcast_rows(g.mod[l:l + 1, 0:D]), ["mod"], ["B1"])
    s.stt("dve", A1[:, :], A1[:, :], 1.0, xt[:, :], ALU.add, ALU.mult, ["A1", "xt"], ["A1"])
    wv = g.w["w_in"][l].rearrange("(kc q) n -> q kc n", q=128)
    ti = wi = oi = 0
    for nb in range(NB):
        for tt in range(NT):
            b = ti % 2
            ti += 1
            r0 = nb * TB + tt * 128
            p.dma(xt[:, :], xsrc[r0:r0 + 128, :], ["xsrc"], ["xt"])
            s.act(hb[b][:, :], xt[:, :], AF.Square, ["xt"], ["hb%d" % b, "ssq"], accum_out=ssq[:, :])
            s.rsqrt(ssq[:, :], "ssq", 1.0 / D, EPS)
            s.stt("dve", xt[:, :], xt[:, :], ssq[:, 0:1], A1[:, :], ALU.mult, ALU.mult, ["xt", "ssq", "A1"], ["xt"])
            s.tt("pool", hb[b][:, :], xt[:, :], B1[:, :], ALU.add, ["xt", "B1"], ["hb%d" % b])
            for q4 in range(8):
                pb = q4 % 2

                def tr(e, b=b, q4=q4, pb=pb):
                    ins = None
                    for j in range(4):
                        dk = q4 * 4 + j
                        ins = e.transpose(out=pst[pb][:, j * 128:(j + 1) * 128], in_=hb[b][:, dk * 128:(dk + 1) * 128], identity=ident[:, :])
                    return ins
                s.mm(tr, ["hb%d" % b, "ident"], ["pst%d" % pb])
                outap = hT[:, q4 * 4:(q4 + 1) * 4, tt * 128:(tt + 1) * 128]
                inap = pst[pb][:, :].rearrange("q (j t) -> q j t", j=4)
                s.cp("act" if q4 % 2 == 0 else "dve", outap, inap, ["pst%d" % pb], ["hT"])
        for (c0, cw) in chunks:
            b = wi % 2
            wi += 1
            p.dma(wb[b][:, :, 0:cw], wv[:, :, c0:c0 + cw], [], ["wb%d" % b], q="pool")
            fm = (c0 >= C_US and c0 < C_QA)
            if not fm:
                for tt in range(NT):
                    o_i = oi % 2
                    oi += 1

                    def mm(e, b=b, tt=tt, o_i=o_i, cw=cw):
                        ins = None
                        for kc in range(32):
                            ins = e.matmul(pso[o_i][:, 0:cw], lhsT=hT[:, kc, tt * 128:(tt + 1) * 128], rhs=wb[b][:, kc, 0:cw],
                                           start=(kc == 0), stop=(kc == 31))
                        return ins
                    s.mm(mm, ["hT", "wb%d" % b], ["pso%d" % o_i])
                    s.cp("act" if o_i == 0 else "dve", ob[o_i][:, 0:cw], pso[o_i][:, 0:cw], ["pso%d" % o_i], ["ob%d" % o_i])
                    r0 = nb * TB + tt * 128
                    p.dma(g.P[r0:r0 + 128, c0:c0 + cw], ob[o_i][:, 0:cw], ["ob%d" % o_i], ["P"], q="act")
            else:
                for cs4 in range(cw // 128):
                    for th in range(max(1, TB // 512)):
                        tw = min(512, TB)
                        o_i = oi % 2
                        oi += 1

                        def mm(e, b=b, cs4=cs4, th=th, o_i=o_i, tw=tw):
                            ins = None
                            for kc in range(32):
                                ins = e.matmul(pso[o_i][:, 0:tw], lhsT=wb[b][:, kc, cs4 * 128:(cs4 + 1) * 128],
                                               rhs=hT[:, kc, th * 512:th * 512 + tw], start=(kc == 0), stop=(kc == 31))
                            return ins
                        s.mm(mm, ["hT", "wb%d" % b], ["pso%d" % o_i])
                        s.cp("act" if o_i == 0 else "dve", ob[o_i][:, 0:tw], pso[o_i][:, 0:tw], ["pso%d" % o_i], ["ob%d" % o_i])
                        rr0 = c0 - C_US + cs4 * 128
                        t0 = nb * TB + th * 512
                        p.dma(g.PuT[rr0:rr0 + 128, t0:t0 + tw], ob[o_i][:, 0:tw], ["ob%d" % o_i], ["PuT"], q="act")
    s.done()


def final_norm(s, g, l, y, ykey, col0, r0, gain, sq, ss, ob, obkey):
    s.act(sq, y, AF.Square, [ykey], ["fn_sq", "fn_ss"], accum_out=ss)
    s.rsqrt(ss, "fn_ss", 1.0 / 1024, EPS)
    s.stt("dve", ob, y, ss[:, 0:1], gain, ALU.mult, ALU.mult, [ykey, "fn_ss", "fn_gain"], [obkey])
    s.p.dma(g.mix[r0:r0 + 128, col0:col0 + 1024], ob, [obkey], ["mix"])


def stage_ret(p, nc, g, l):
    T = g.T
    K = g.k
    gL = host_consts_cache()["ret_gL"]
    s = Stage(p, nc)
    ident, _ = load_ident(s, g)
    decT = s.sb("decT", [128, 4, 128])
    wqr = s.sb("wqr", [128, 4, 128])
    wkc = s.sb("wkc", [128, 4])
    gain = s.sb("gain", [128, 1024])
    p.dma(decT[:, :, :].rearrange("p a b -> p (a b)"), K["ret_decT"], [], ["decT"])
    p.dma(wqr[:, :, :].rearrange("p a b -> p (a b)"), K["ret_wq"], [], ["wqr"])
    p.dma(wkc[:, :], K["ret_wk"], [], ["wkc"])
    p.dma(gain[:, :], bcast_rows(g.w["out_norm_g"][l][:, 0:1024]), [], ["fn_gain"])
    pt = [s.sb("pt%d" % i, [128, 3072]) for i in range(2)]
    rot = [s.sb("rot%d" % i, [128, 192]) for i in range(2)]
    ta = s.sb("ta", [128, 8, 64])
    tb_ = s.sb("tb", [128, 8, 64])
    tc_ = s.sb("tc", [128, 8, 64])
    td_ = s.sb("td", [128, 8, 64])
    qkr = s.sb("qkr", [128, 8, 2, 64], BF16)
    qkT = s.sb("qkT", [128, 8, 128], BF16)
    qTw = s.sb("qTw", [128, 4, 128], BF16)
    kw = s.sb("kw", [128, 4, 128], BF16)
    vb = s.sb("vb", [128, 1024], BF16)
    SD = s.sb("SD", [128, 4, 128], BF16)
    R = s.sb("R", [128, 1024])
    Rb = s.sb("Rb", [128, 1024], BF16)
    stats = s.sb("stats", [128, 4, 6])
    mv = s.sb("mv", [128, 4, 2])
    rstd = s.sb("rstd", [128, 4])
    on = s.sb("on", [128, 1024])
    sg = s.sb("sg", [128, 1024])
    sq = s.sb("sq", [128, 1024], BF16)
    ss = s.sb("ss", [128, 1])
    ob = [s.sb("ob%d" % i, [128, 1024], BF16) for i in range(2)]
    psT = s.ps("psT", [128, 8, 128], BF16)
    psS = s.ps("psS", [128, 4, 128])
    psO = s.ps("psO", [128, 1024])
    psK = s.ps("psK", [128, 1024])
    p.add("dve", lambda e: e.memset(R[:, :], 0.0), [], ["R"])
    p.add("dve", lambda e: e.memset(Rb[:, :], 0.0), [], ["Rb"])
    for n in range(T // 128):
        b = n % 2
        r0 = n * 128
        PT, RT = "pt%d" % b, "rot%d" % b
        p.dma(pt[b][:, :], g.P[r0:r0 + 128, 0:3072], ["P"], [PT])
        p.dma(rot[b][:, :], g.rot[r0:r0 + 128, :], ["rot"], [RT])
        qk4 = pt[b][:, 0:1024].rearrange("p (h two d) -> p h two d", h=8, two=2)
        x1, x2 = qk4[:, :, 0, :], qk4[:, :, 1, :]
        sinb = rot[b][:, 0:64].unsqueeze(1).to_broadcast([128, 8, 64])
        cosb = rot[b][:, 96:160].unsqueeze(1).to_broadcast([128, 8, 64])
        s.tt("dve", ta[:, :, :], x1, cosb, ALU.mult, [PT, RT], ["ta"])
        s.tt("dve", tb_[:, :, :], x2, sinb, ALU.mult, [PT, RT], ["tb"])
        s.tt("dve", qkr[:, :, 0, :], ta[:, :, :], tb_[:, :, :], ALU.subtract, ["ta", "tb"], ["qkr"])
        s.tt("pool", tc_[:, :, :], x2, cosb, ALU.mult, [PT, RT], ["tc"])
        s.tt("pool", td_[:, :, :], x1, sinb, ALU.mult, [PT, RT], ["td"])
        s.tt("pool", qkr[:, :, 1, :], tc_[:, :, :], td_[:, :, :], ALU.add, ["tc", "td"], ["qkr"])
        qkf = qkr[:, :, :, :].rearrange("p h two d -> p h (two d)")

        def tr(e):
            ins = None
            for j in range(8):
                ins = e.transpose(out=psT[:, j, :], in_=qkf[:, j, :], identity=ident[:, :])
            return ins
        s.mm(tr, ["qkr", "ident"], ["psT"])
        s.cp("act", qkT[:, :, :], psT[:, :, :], ["psT"], ["qkT"])
        s.tt("dve", qTw[:, :, :], qkT[:, 0:4, :], wqr[:, :, :], ALU.mult, ["qkT", "wqr"], ["qTw"])
        s.tt("pool", kw[:, :, :], qkf[:, 4:8, :], wkc[:, :].unsqueeze(2).to_broadcast([128, 4, 128]), ALU.mult, ["qkr", "wkc"], ["kw"])
        s.cp("act", vb[:, :], pt[b][:, 1024:2048], [PT], ["vb"])

        def mmS(e):
            ins = None
            for h in range(4):
                ins = e.matmul(psS[:, h, :], lhsT=qkT[:, 4 + h, :], rhs=qkT[:, h, :], start=True, stop=True)
            return ins
        s.mm(mmS, ["qkT"], ["psS"])
        s.tt("dve", SD[:, :, :], psS[:, :, :], decT[:, :, :], ALU.mult, ["psS", "decT"], ["SD"])

        def mmO(e):
            ins = None
            for h in range(4):
                e.matmul(psO[:, h * 256:(h + 1) * 256], lhsT=SD[:, h, :], rhs=vb[:, h * 256:(h + 1) * 256], start=True, stop=False)
                ins = e.matmul(psO[:, h * 256:(h + 1) * 256], lhsT=qTw[:, h, :], rhs=Rb[:, h * 256:(h + 1) * 256], start=False, stop=True)
            return ins
        s.mm(mmO, ["SD", "vb", "qTw", "Rb"], ["psO"])

        def mmK(e):
            ins = None
            for h in range(4):
                ins = e.matmul(psK[:, h * 256:(h + 1) * 256], lhsT=kw[:, h, :], rhs=vb[:, h * 256:(h + 1) * 256], start=True, stop=True)
            return ins
        s.mm(mmK, ["kw", "vb"], ["psK"])
        for h in range(4):
            sl = slice(h * 256, (h + 1) * 256)
            s.stt("dve", R[:, sl], R[:, sl], gL[h], psK[:, sl], ALU.mult, ALU.add, ["R", "psK"], ["R"])
        s.cp("pool", Rb[:, :], R[:, :], ["R"], ["Rb"])
        for h in range(4):
            p.add("dve", lambda e, h=h: e.bn_stats(out=stats[:, h, :], in_=psO[:, h * 256:(h + 1) * 256]), ["psO"], ["stats"])
        for h in range(4):
            p.add("dve", lambda e, h=h: e.bn_aggr(out=mv[:, h, :], in_=stats[:, h, :]), ["stats"], ["mv"])
        s.cp("dve", rstd[:, :], mv[:, :, 1], ["mv"], ["rstd"])
        s.rsqrt(rstd[:, :], "rstd", 1.0, EPS)
        for h in range(4):
            sl = slice(h * 256, (h + 1) * 256)
            s.ts("dve", on[:, sl], psO[:, sl], mv[:, h, 0:1], rstd[:, h:h + 1], ALU.subtract, ALU.mult, ["psO", "mv", "rstd"], ["on"])
        s.act(sg[:, :], pt[b][:, 2048:3072], AF.Silu, [PT], ["sg"])
        s.tt("pool", on[:, :], on[:, :], sg[:, :], ALU.mult, ["on", "sg"], ["on"])
        final_norm(s, g, l, on[:, :], "on", 0, r0, gain[:, :], sq[:, :], ss[:, :], ob[b][:, :], "ob%d" % b)
    s.done()


_HC = None


def host_consts_cache():
    global _HC
    if _HC is None:
        _HC = host_consts()
    return _HC


def stage_swa(p, nc, g, l):
    T = g.T
    K = g.k
    s = Stage(p, nc)
    ident, _ = load_ident(s, g)
    mtmp = s.sb("mtmp", [128, 2, 128])
    msk = s.sb("msk", [128, 2, 128], BF16)
    p.dma(mtmp[:, 0, :], K["swa_mprev"], [], ["mtmp"])
    p.dma(mtmp[:, 1, :], K["swa_mcur"], [], ["mtmp"])
    s.cp("dve", msk[:, :, :], mtmp[:, :, :], ["mtmp"], ["msk"])
    gq = s.sb("gq", [128, 64])
    gk = s.sb("gk", [128, 64])
    p.dma(gq[:, :], bcast_rows(g.w["attn_q_norm"][l]), [], ["gq"])
    p.dma(gk[:, :], bcast_rows(g.w["attn_k_norm"][l]), [], ["gk"])
    s.ts("dve", gq[:, :], gq[:, :], 0.125, None, ALU.mult, None, ["gq"], ["gq"])
    esink = s.sb("esink", [128, 16])
    p.dma(esink[:, :], bcast_rows(g.w["attn_sinks"][l]), [], ["esink"])
    s.act(esink[:, :], esink[:, :], AF.Exp, ["esink"], ["esink"])
    gain = s.sb("gain", [128, 1024])
    p.dma(gain[:, :], bcast_rows(g.w["out_norm_g"][l][:, 2048:3072]), [], ["fn_gain"])
    pt = [s.sb("pt%d" % i, [128, 1280]) for i in range(2)]
    rot = [s.sb("rot%d" % i, [128, 192]) for i in range(2)]
    sqv = s.sb("sqv", [128, 18, 64])
    ssn = s.sb("ssn", [128, 18])
    xn = s.sb("xn", [128, 18, 2, 32])
    ta = s.sb("ta", [128, 18, 32])
    tb_ = s.sb("tb", [128, 18, 32])
    tc_ = s.sb("tc", [128, 18, 32])
    td_ = s.sb("td", [128, 18, 32])
    qkr = s.sb("qkr", [128, 18, 2, 32], BF16)
    kd = s.sb("kd", [128, 2, 2, 64], BF16)
    qT = s.sb("qT", [128, 8, 128], BF16)
    kT = [s.sb("kT%d" % i, [128, 2, 128], BF16) for i in range(2)]
    vb1 = [s.sb("vb1%d" % i, [128, 2, 65], BF16) for i in range(2)]
    PTb = [s.sb("PTb%d" % i, [128, 2, 4, 128], BF16) for i in range(2)]
    den = s.sb("den", [128, 16])
    on = s.sb("on", [128, 16, 64])
    sq = s.sb("sq", [128, 1024], BF16)
    ss = s.sb("ss", [128, 1])
    ob = [s.sb("ob%d" % i, [128, 1024], BF16) for i in range(2)]
    psTq = s.ps("psTq", [128, 8, 128], BF16)
    psTk = s.ps("psTk", [128, 2, 128], BF16)
    psS = [s.ps("psS%d" % i, [128, 4, 128]) for i in range(2)]
    psO = s.ps("psO", [128, 16, 128])
    for i in range(2):
        p.add("dve", lambda e, i=i: e.memset(vb1[i][:, :, 64:65], 1.0), [], ["vb1%d" % i])
    for n in range(T // 128):
        b = n % 2
        pb = 1 - b
        r0 = n * 128
        PT, RT = "pt%d" % b, "rot%d" % b
        p.dma(pt[b][:, :], g.P[r0:r0 + 128, C_QA:C_QA + 1280], ["P"], [PT])
        p.dma(rot[b][:, :], g.rot[r0:r0 + 128, :], ["rot"], [RT])
        x3 = pt[b][:, 0:1152].rearrange("p (h d) -> p h d", h=18)
        s.act(sqv[:, :, :], x3, AF.Square, [PT], ["sqv"])
        p.add("dve", lambda e: e.tensor_reduce(out=ssn[:, :], in_=sqv[:, :, :], axis=AX.X, op=ALU.add), ["sqv"], ["ssn"])
        s.rsqrt(ssn[:, :], "ssn", 1.0 / 64, EPS)
        xn3 = xn[:, :, :, :].rearrange("p h two d -> p h (two d)")
        s.tt("dve", xn3, ssn[:, :].unsqueeze(2).to_broadcast([128, 18, 64]), x3, ALU.mult, ["ssn", PT], ["xn"])
        s.tt("dve", xn3[:, 0:16, :], xn3[:, 0:16, :], gq[:, :].unsqueeze(1).to_broadcast([128, 16, 64]), ALU.mult, ["xn", "gq"], ["xn"])
        s.tt("dve", xn3[:, 16:18, :], xn3[:, 16:18, :], gk[:, :].unsqueeze(1).to_broadcast([128, 2, 64]), ALU.mult, ["xn", "gk"], ["xn"])
        x1, x2 = xn[:, :, 0, :], xn[:, :, 1, :]
        sinb = rot[b][:, 64:96].unsqueeze(1).to_broadcast([128, 18, 32])
        cosb = rot[b][:, 160:192].unsqueeze(1).to_broadcast([128, 18, 32])
        s.tt("dve", ta[:, :, :], x1, cosb, ALU.mult, ["xn", RT], ["ta"])
        s.tt("dve", tb_[:, :, :], x2, sinb, ALU.mult, ["xn", RT], ["tb"])
        s.tt("dve", qkr[:, :, 0, :], ta[:, :, :], tb_[:, :, :], ALU.subtract, ["ta", "tb"], ["qkr"])
        s.tt("pool", tc_[:, :, :], x2, cosb, ALU.mult, ["xn", RT], ["tc"])
        s.tt("pool", td_[:, :, :], x1, sinb, ALU.mult, ["xn", RT], ["td"])
        s.tt("pool", qkr[:, :, 1, :], tc_[:, :, :], td_[:, :, :], ALU.add, ["tc", "td"], ["qkr"])
        qkf = qkr[:, :, :, :].rearrange("p h two d -> p h (two d)")
        for cpy in range(2):
            s.cp("pool", kd[:, :, cpy, :], qkf[:, 16:18, :], ["qkr"], ["kd"])
        qf = qkr[:, 0:16, :, :].rearrange("p (j e) two d -> p j (e two d)", e=2)
        kdf = kd[:, :, :, :].rearrange("p h c d -> p h (c d)")

        def tr(e, b=b):
            ins = None
            for j in range(8):
                ins = e.transpose(out=psTq[:, j, :], in_=qf[:, j, :], identity=ident[:, :])
            for hk in range(2):
                ins = e.transpose(out=psTk[:, hk, :], in_=kdf[:, hk, :], identity=ident[:, :])
            return ins
        s.mm(tr, ["qkr", "kd", "ident"], ["psTq", "psTk"])
        s.cp("act", qT[:, :, :], psTq[:, :, :], ["psTq"], ["qT"])
        s.cp("dve", kT[b][:, :, :], psTk[:, :, :], ["psTk"], ["kT%d" % b])
        s.cp("act", vb1[b][:, :, 0:64], pt[b][:, 1152:1280].rearrange("p (h d) -> p h d", h=2), [PT], ["vb1%d" % b])
        blocks = ([(pb, 0)] if n > 0 else []) + [(b, 1)]
        for hk in range(2):
            for bi, (kb, mi) in enumerate(blocks):
                PTk = "PTb%d" % mi
                for e_ in range(2):
                    def mmS(e, e_=e_, kb=kb, hk=hk):
                        return e.matmul(psS[e_][:, :, :], lhsT=kT[kb][e_ * 64:(e_ + 1) * 64, hk, :],
                                        rhs=qT[e_ * 64:(e_ + 1) * 64, hk * 4:(hk + 1) * 4, :], start=True, stop=True)
                    s.mm(mmS, ["kT%d" % kb, "qT"], ["psS%d" % e_])
                    s.act(PTb[mi][:, e_, :, :], psS[e_][:, :, :], AF.Exp, ["psS%d" % e_], [PTk])
                for e_ in range(2):
                    s.tt("dve" if e_ == 0 else "pool", PTb[mi][:, e_, :, :], PTb[mi][:, e_, :, :],
                         msk[:, mi, :].unsqueeze(1).to_broadcast([128, 4, 128]), ALU.mult, [PTk, "msk"], [PTk])

            def mmO(e, hk=hk, blocks=blocks):
                ins = None
                for pair in range(4):
                    for e_ in range(2):
                        hq = hk * 8 + pair * 2 + e_
                        for bi, (kb, mi) in enumerate(blocks):
                            ins = e.matmul(psO[:, hq, 0:65], lhsT=PTb[mi][:, e_, pair, :], rhs=vb1[kb][:, hk, :],
                                           start=(bi == 0), stop=(bi == len(blocks) - 1))
                return ins
            s.mm(mmO, ["PTb0", "PTb1", "vb10", "vb11"], ["psO"])
        s.tt("dve", den[:, :], psO[:, :, 64], esink[:, :], ALU.add, ["psO", "esink"], ["den"])
        p.add("dve", lambda e: e.reciprocal(out=den[:, :], in_=den[:, :]), ["den"], ["den"])
        s.tt("dve", on[:, :, :], den[:, :].unsqueeze(2).to_broadcast([128, 16, 64]), psO[:, :, 0:64], ALU.mult, ["den", "psO"], ["on"])
        onf = on[:, :, :].rearrange("p h d -> p (h d)")
        final_norm(s, g, l, onf, "on", 2048, r0, gain[:, :], sq[:, :], ss[:, :], ob[b][:, :], "ob%d" % b)
    s.done()


def stage_pool(p, nc, g, l):
    T = g.T
    K = g.k
    s = Stage(p, nc)
    mt = s.sb("mt", [128, 3, 512])
    M = s.sb("M", [128, 3, 4, 128], BF16)
    for i, nm in enumerate(("pool_mcur", "pool_mcur0", "pool_mprev")):
        p.dma(mt[:, i, :], K[nm], [], ["mt"])
    s.cp("dve", M[:, :, :, :].rearrange("p a g t -> p a (g t)"), mt[:, :, :], ["mt"], ["M"])
    icnt = s.sb("icnt", [128, 2, 4, 128])
    p.dma(icnt[:, 0, :, :].rearrange("p g t -> p (g t)"), K["pool_icnt"], [], ["icnt"])
    p.dma(icnt[:, 1, :, :].rearrange("p g t -> p (g t)"), K["pool_icnt0"], [], ["icnt"])
    wp = s.sb("wp", [128, 4, 2, 256], BF16)
    for gi in range(4):
        p.dma(wp[:, gi, :, :], g.w["pool_w"][l][gi].rearrange("(cc c) d -> c cc d", c=128), [], ["wp"], q="pool")
    psc = s.sb("psc", [128, 1024])
    p.dma(psc[:, :], bcast_rows(g.w["pool_scale"][l]), [], ["psc"])
    gain = s.sb("gain", [128, 1024])
    p.dma(gain[:, :], bcast_rows(g.w["out_norm_g"][l][:, 3072:4096]), [], ["fn_gain"])
    ut = [s.sb("ut%d" % i, [128, 1024]) for i in range(2)]
    ub = [s.sb("ub%d" % i, [128, 1024], BF16) for i in range(2)]
    pT = s.sb("pT", [128, 4, 2, 128], BF16)
    yv = s.sb("yv", [128, 1024])
    sq = s.sb("sq", [128, 1024], BF16)
    ss = s.sb("ss", [128, 1])
    ob = [s.sb("ob%d" % i, [128, 1024], BF16) for i in range(2)]
    psP = s.ps("psP", [128, 4, 2, 128])
    psY = s.ps("psY", [128, 1024])
    for n in range(T // 128):
        b = n % 2
        pb = 1 - b
        r0 = n * 128
        p.dma(ut[b][:, :], g.P[r0:r0 + 128, C_UP:C_UP + 1024], ["P"], ["ut%d" % b])
        s.cp("act", ub[b][:, :], ut[b][:, :], ["ut%d" % b], ["ub%d" % b])
        mi = 1 if n == 0 else 0

        def mmP(e, b=b, pb=pb, n=n, mi=mi):
            ins = None
            for gi in range(4):
                for cc in range(2):
                    ch = gi * 2 + cc
                    ins = e.matmul(psP[:, gi, cc, :], lhsT=ub[b][:, ch * 128:(ch + 1) * 128], rhs=M[:, mi, gi, :], start=True, stop=(n == 0))
                    if n > 0:
                        ins = e.matmul(psP[:, gi, cc, :], lhsT=ub[pb][:, ch * 128:(ch + 1) * 128], rhs=M[:, 2, gi, :], start=False, stop=True)
            return ins
        s.mm(mmP, ["ub0", "ub1", "M"], ["psP"])
        for cc in range(2):
            s.tt("dve" if cc == 0 else "pool" if False else "dve", pT[:, :, cc, :], psP[:, :, cc, :], icnt[:, mi, :, :], ALU.mult, ["psP", "icnt"], ["pT"])

        def mmY(e):
            ins = None
            for gi in range(4):
                for cc in range(2):
                    ins = e.matmul(psY[:, gi * 256:(gi + 1) * 256], lhsT=pT[:, gi, cc, :], rhs=wp[:, gi, cc, :], start=(cc == 0), stop=(cc == 1))
            return ins
        s.mm(mmY, ["pT", "wp"], ["psY"])
        s.tt("dve", yv[:, :], psY[:, :], psc[:, :], ALU.mult, ["psY", "psc"], ["yv"])
        final_norm(s, g, l, yv[:, :], "yv", 3072, r0, gain[:, :], sq[:, :], ss[:, :], ob[b][:, :], "ob%d" % b)
    s.done()


def stage_ssm_scan(p, nc, g, l):
    T = g.T
    K = g.k
    W = g.w
    s = Stage(p, nc)
    identb, identf = load_ident(s, g)
    PI2 = float(np.pi / 2)
    A3 = s.sb("A3", [32, 3, 128])
    ld = s.sb("ld", [32, 2])
    p.dma(A3[:, 0, :], W["ssm_a_re"][l].rearrange("(gp g2) q -> gp (g2 q)", g2=2), [], ["A3"])
    p.dma(A3[:, 1, :], W["ssm_a_im"][l].rearrange("(gp g2) q -> gp (g2 q)", g2=2), [], ["A3"])
    p.dma(ld[:, :], W["ssm_log_dt"][l].rearrange("o (gp g2) -> (o gp) g2", g2=2), [], ["ld"])
    s.cp("dve", A3[:, 2, :].rearrange("a (g2 q) -> a g2 q", g2=2), ld[:, :].unsqueeze(2).to_broadcast([32, 2, 64]), ["ld", "A3"], ["A3"])
    psPr = s.ps("psPr", [128, 512])
    psPb = s.ps("psPb", [128, 8, 128], BF16)

    def trA(e):
        ins = None
        for a in range(3):
            ins = e.transpose(out=psPr[:, a * 32:(a + 1) * 32], in_=A3[:, a, :], identity=identf[0:32, 0:32])
        return ins
    s.mm(trA, ["A3", "identf"], ["psPr"])
    prm = s.sb("prm", [128, 3, 32])
    s.cp("dve", prm[:, :, :].rearrange("k a g -> k (a g)"), psPr[:, 0:96], ["psPr"], ["prm"])
    ar, ai = prm[:, 0, :], prm[:, 1, :]
    sm = s.sb("sm", [128, 16, 32])
    smi = s.sb("smi", [128, 32], I32)
    dtk, lam, th, mag, rr_, cs_, sn_, abre, abim, den, nr, fre, fim, tmpa, tmpb, th2 = [sm[:, i, :] for i in range(16)]
    KS = ["sm%d" % i for i in range(16)]
    s.act(dtk, prm[:, 2, :], AF.Exp, ["prm"], [KS[0]])
    s.tt("dve", lam, ar, dtk, ALU.mult, ["prm", KS[0]], [KS[1]])
    s.tt("dve", th, ai, dtk, ALU.mult, ["prm", KS[0]], [KS[2]])
    s.act(mag, lam, AF.Exp, [KS[1]], [KS[3]])
    s.range_reduce(rr_, th, smi[:, :], tmpa, KS[4], KS[2], "smi", KS[13])
    s.act(sn_, rr_, AF.Sin, [KS[4]], [KS[6]])
    s.ts("dve", th2, th, PI2, None, ALU.add, None, [KS[2]], [KS[15]])
    s.range_reduce(rr_, th2, smi[:, :], tmpa, KS[4], KS[15], "smi", KS[13])
    s.act(cs_, rr_, AF.Sin, [KS[4]], [KS[5]])
    s.tt("dve", abre, mag, cs_, ALU.mult, [KS[3], KS[5]], [KS[7]])
    s.tt("dve", abim, mag, sn_, ALU.mult, [KS[3], KS[6]], [KS[8]])
    s.tt("dve", den, ar, ar, ALU.mult, ["prm"], [KS[9]])
    s.tt("dve", tmpa, ai, ai, ALU.mult, ["prm"], [KS[13]])
    s.tt("dve", den, den, tmpa, ALU.add, [KS[9], KS[13]], [KS[9]])
    p.add("dve", lambda e: e.reciprocal(out=den, in_=den), [KS[9]], [KS[9]])
    s.ts("dve", nr, abre, -1.0, None, ALU.add, None, [KS[7]], [KS[10]])
    s.tt("dve", tmpa, nr, ar, ALU.mult, [KS[10], "prm"], [KS[13]])
    s.tt("dve", tmpb, abim, ai, ALU.mult, [KS[8], "prm"], [KS[14]])
    s.tt("dve", fre, tmpa, tmpb, ALU.add, [KS[13], KS[14]], [KS[11]])
    s.tt("dve", fre, fre, den, ALU.mult, [KS[11], KS[9]], [KS[11]])
    s.tt("dve", tmpa, abim, ar, ALU.mult, [KS[8], "prm"], [KS[13]])
    s.tt("dve", tmpb, nr, ai, ALU.mult, [KS[10], "prm"], [KS[14]])
    s.tt("dve", fim, tmpa, tmpb, ALU.subtract, [KS[13], KS[14]], [KS[12]])
    s.tt("dve", fim, fim, den, ALU.mult, [KS[12], KS[9]], [KS[12]])
    bre = s.sb("bre", [128, 32, 16])
    bim = s.sb("bim", [128, 32, 16])
    bvr = W["ssm_b_re"][l].rearrange("(gp g2) q c -> gp (g2 q) c", g2=2).rearrange("gp k c -> k gp c")
    bvi = W["ssm_b_im"][l].rearrange("(gp g2) q c -> gp (g2 q) c", g2=2).rearrange("gp k c -> k gp c")
    for q in range(4):
        p.dma(bre[:, q * 8:(q + 1) * 8, :], bvr[:, q * 8:(q + 1) * 8, :], [], ["bre"])
        p.dma(bim[:, q * 8:(q + 1) * 8, :], bvi[:, q * 8:(q + 1) * 8, :], [], ["bim"])
    w1 = s.sb("w1", [128, 32, 16])
    w2 = s.sb("w2", [128, 32, 16])
    bbre = s.sb("bbre", [128, 32, 16])
    bbim = s.sb("bbim", [128, 32, 16])
    freb = fre.unsqueeze(2).to_broadcast([128, 32, 16])
    fimb = fim.unsqueeze(2).to_broadcast([128, 32, 16])
    s.tt("dve", w1[:, :, :], freb, bre[:, :, :], ALU.mult, [KS[11], "bre"], ["w1"])
    s.tt("dve", w2[:, :, :], fimb, bim[:, :, :], ALU.mult, [KS[12], "bim"], ["w2"])
    s.tt("dve", bbre[:, :, :], w1[:, :, :], w2[:, :, :], ALU.subtract, ["w1", "w2"], ["bbre"])
    s.tt("dve", w1[:, :, :], freb, bim[:, :, :], ALU.mult, [KS[11], "bim"], ["w1"])
    s.tt("dve", w2[:, :, :], fimb, bre[:, :, :], ALU.mult, [KS[12], "bre"], ["w2"])
    s.tt("dve", bbim[:, :, :], w1[:, :, :], w2[:, :, :], ALU.add, ["w1", "w2"], ["bbim"])
    BT = []
    for nm, bb in (("re", bbre), ("im", bbim)):
        Zp = s.sb("Zp" + nm, [128, 32, 128], BF16)
        p.add("pool", lambda e, Zp=Zp: e.memset(Zp[:, :, :], 0.0), [], ["Zp" + nm])
        Zp5 = Zp[:, :, :].rearrange("k (i j) c -> k i j c", j=4)
        bb4 = bb[:, :, :].rearrange("k (i j) c -> k i j c", j=4)
        for j in range(4):
            for g2 in range(2):
                lo = j * 32 + g2 * 16
                s.cp("dve", Zp5[g2 * 64:(g2 + 1) * 64, :, j, lo:lo + 16], bb4[g2 * 64:(g2 + 1) * 64, :, j, :], ["bb" + nm, "Zp" + nm], ["Zp" + nm])
        BTt = s.sb("BT" + nm, [128, 32, 128], BF16)
        for q in range(4):
            def trB(e, q=q, Zp=Zp):
                ins = None
                for a in range(8):
                    ins = e.transpose(out=psPb[:, a, :], in_=Zp[:, q * 8 + a, :], identity=identb[:, :])
                return ins
            s.mm(trB, ["Zp" + nm, "ident"], ["psPb"])
            s.cp("act", BTt[:, q * 8:(q + 1) * 8, :], psPb[:, :, :], ["psPb"], ["BT" + nm])
        BT.append(BTt)
    BTre, BTim = BT
    Cp = []
    for nm, key, sgn in (("re", "ssm_c_re", 1.0), ("im", "ssm_c_im", -1.0)):
        Xp = s.sb("Xp" + nm, [128, 8, 128])
        p.add("pool", lambda e, Xp=Xp: e.memset(Xp[:, :, :], 0.0), [], ["Xp" + nm])
        cv = W[key][l].rearrange("(i j g2) c q -> j g2 c i q", j=4, g2=2)
        for j in range(4):
            for g2 in range(2):
                lo = j * 32 + g2 * 16
                p.dma(Xp[lo:lo + 16, :, g2 * 64:(g2 + 1) * 64], cv[j, g2], ["Xp" + nm], ["Xp" + nm])
        XT = s.sb("XT" + nm, [128, 8, 128])
        for q in range(2):
            def trC(e, q=q, Xp=Xp):
                ins = None
                for a in range(4):
                    ins = e.transpose(out=psPr[:, a * 128:(a + 1) * 128], in_=Xp[:, q * 4 + a, :], identity=identf[:, :])
                return ins
            s.mm(trC, ["Xp" + nm, "identf"], ["psPr"])
            s.cp("dve", XT[:, q * 4:(q + 1) * 4, :], psPr[:, :].rearrange("k (a c) -> k a c", a=4), ["psPr"], ["XT" + nm])
        Cpt = s.sb("Cp" + nm, [128, 32, 128], BF16)
        p.add("pool", lambda e, Cpt=Cpt: e.memset(Cpt[:, :, :], 0.0), [], ["Cp" + nm])
        Cp5 = Cpt[:, :, :].rearrange("k (i j) c -> k i j c", j=4)
        for j in range(4):
            s.ts("dve", Cp5[:, :, j, j * 32:(j + 1) * 32], XT[:, :, j * 32:(j + 1) * 32], sgn, None, ALU.mult, None, ["XT" + nm, "Cp" + nm], ["Cp" + nm])
        Cp.append(Cpt)
    Cpre, Cpim = Cp
    dv = s.sb("dv", [8, 128])
    p.dma(dv[:, :], W["ssm_d"][l].rearrange("o (i k) -> (o i) k", k=128), [], ["dv"])
    s.mm(lambda e: e.transpose(out=psPr[:, 0:8], in_=dv[:, :], identity=identf[0:8, 0:8]), ["dv", "identf"], ["psPr"])
    dcol = s.sb("dcol", [128, 8])
    s.cp("dve", dcol[:, :], psPr[:, 0:8], ["psPr"], ["dcol"])
    iot = s.sb("iot", [128, 512])
    p.dma(iot[:, :], K["iota512"], [], ["iot"])
    tabC = [s.sb("tabC%d" % j, [128, 512]) for j in range(4)]
    tabS = [s.sb("tabS%d" % j, [128, 512]) for j in range(4)]
    rco = [s.sb("rco%d" % j, [128, 512]) for j in range(4)]
    ang = s.sb("ang", [128, 512])
    ang2 = s.sb("ang2", [128, 512])
    angr = s.sb("angr", [128, 512])
    aki = s.sb("aki", [128, 512], I32)
    akf = s.sb("akf", [128, 512])
    car_re = s.sb("car_re", [128, 32])
    car_im = s.sb("car_im", [128, 32])
    ctmp = s.sb("ctmp", [128, 2])
    p.add("dve", lambda e: e.memset(car_re[:, :], 0.0), [], ["car"])
    p.add("dve", lambda e: e.memset(car_im[:, :], 0.0), [], ["car"])
    uTf = [s.sb("uTf%d" % i, [128, 512]) for i in range(2)]
    uTb = s.sb("uTb", [128, 512], BF16)
    xrs = [s.sb("xr%d" % i, [128, 512]) for i in range(2)]
    xis = [s.sb("xi%d" % i, [128, 512]) for i in range(2)]
    t1, t2, t3, t4 = [s.sb("t%d" % i, [128, 512]) for i in range(1, 5)]
    gri = s.sb("gri", [128, 512])
    gii = s.sb("gii", [128, 512])
    Gr = s.sb("Gr", [128, 512])
    Gi = s.sb("Gi", [128, 512])
    Hrs = [s.sb("Hr%d" % i, [128, 512], BF16) for i in range(2)]
    His = [s.sb("Hi%d" % i, [128, 512], BF16) for i in range(2)]
    yf = s.sb("yf", [128, 512])
    y2 = [s.sb("y2%d" % i, [128, 512]) for i in range(2)]
    psXrs = [s.ps("psXr%d" % i, [128, 512]) for i in range(2)]
    psXis = [s.ps("psXi%d" % i, [128, 512]) for i in range(2)]
    pc = 0
    psY = s.ps("psY", [128, 512])
    NTB = T // 512
    it = 0
    for i in range(8):
        for j in range(4):
            gp = 4 * i + j
            s.ts("dve", ang[:, :], iot[:, :], th[:, gp:gp + 1], None, ALU.mult, None, ["iot", KS[2]], ["ang"])
            s.range_reduce(angr[:, :], ang[:, :], aki[:, :], akf[:, :], "angr", "ang", "aki", "akf")
            s.act(tabS[j][:, :], angr[:, :], AF.Sin, ["angr"], ["tabS%d" % j])
            s.ts("dve", ang2[:, :], ang[:, :], PI2, None, ALU.add, None, ["ang"], ["ang2"])
            s.range_reduce(angr[:, :], ang2[:, :], aki[:, :], akf[:, :], "angr", "ang2", "aki", "akf")
            s.act(tabC[j][:, :], angr[:, :], AF.Sin, ["angr"], ["tabC%d" % j])
            s.cp("pool", rco[j][:, :], mag[:, gp:gp + 1].to_broadcast([128, 512]), [KS[3]], ["rco%d" % j])
        for tb in range(NTB):
            t0 = tb * 512
            ub = it % 2
            it += 1
            p.dma(uTf[ub][:, :], g.PuT[i * 128:(i + 1) * 128, t0:t0 + 512], ["PuT"], ["uTf%d" % ub])
            s.cp("act", uTb[:, :], uTf[ub][:, :], ["uTf%d" % ub], ["uTb"])
            for j in range(4):
                gp = 4 * i + j
                TC, TS = "tabC%d" % j, "tabS%d" % j
                kb = pc % 2
                pc += 1
                xr, xi, Hr, Hi, psXr, psXi = xrs[kb], xis[kb], Hrs[kb], His[kb], psXrs[kb], psXis[kb]
                XR, XI, HR, HI, PXR, PXI = "xr%d" % kb, "xi%d" % kb, "Hr%d" % kb, "Hi%d" % kb, "psXr%d" % kb, "psXi%d" % kb
                s.mm(lambda e, gp=gp, psXr=psXr: e.matmul(psXr[:, :], lhsT=BTre[:, gp, :], rhs=uTb[:, :], start=True, stop=True), ["BTre", "uTb"], [PXR])
                s.mm(lambda e, gp=gp, psXi=psXi: e.matmul(psXi[:, :], lhsT=BTim[:, gp, :], rhs=uTb[:, :], start=True, stop=True), ["BTim", "uTb"], [PXI])
                s.cp("act", xr[:, :], psXr[:, :], [PXR], [XR])
                s.cp("act", xi[:, :], psXi[:, :], [PXI], [XI])
                s.tt("dve", t1[:, :], xr[:, :], tabC[j][:, :], ALU.mult, [XR, TC], ["t1"])
                s.tt("dve", t2[:, :], xi[:, :], tabS[j][:, :], ALU.mult, [XI, TS], ["t2"])
                s.tt("dve", gri[:, :], t1[:, :], t2[:, :], ALU.add, ["t1", "t2"], ["gri"])
                s.tt("pool", t3[:, :], xi[:, :], tabC[j][:, :], ALU.mult, [XI, TC], ["t3"])
                s.tt("pool", t4[:, :], xr[:, :], tabS[j][:, :], ALU.mult, [XR, TS], ["t4"])
                s.tt("pool", gii[:, :], t3[:, :], t4[:, :], ALU.subtract, ["t3", "t4"], ["gii"])
                p.add("dve", lambda e, j=j, gp=gp: e.tensor_tensor_scan(out=Gr[:, :], data0=rco[j][:, :], data1=gri[:, :],
                                                                       initial=car_re[:, gp:gp + 1], op0=ALU.mult, op1=ALU.add),
                      ["rco%d" % j, "gri", "car"], ["Gr"])
                p.add("dve", lambda e, j=j, gp=gp: e.tensor_tensor_scan(out=Gi[:, :], data0=rco[j][:, :], data1=gii[:, :],
                                                                       initial=car_im[:, gp:gp + 1], op0=ALU.mult, op1=ALU.add),
                      ["rco%d" % j, "gii", "car"], ["Gi"])
                s.tt("dve", ctmp[:, 0:1], Gi[:, 511:512], tabS[j][:, 511:512], ALU.mult, ["Gi", TS], ["ctmp"])
                s.tt("dve", ctmp[:, 1:2], Gr[:, 511:512], tabS[j][:, 511:512], ALU.mult, ["Gr", TS], ["ctmp"])
                s.stt("dve", car_re[:, gp:gp + 1], Gr[:, 511:512], tabC[j][:, 511:512], ctmp[:, 0:1], ALU.mult, ALU.subtract, ["Gr", TC, "ctmp", "car"], ["car"])
                s.stt("dve", car_im[:, gp:gp + 1], Gi[:, 511:512], tabC[j][:, 511:512], ctmp[:, 1:2], ALU.mult, ALU.add, ["Gi", TC, "ctmp", "car"], ["car"])
                s.tt("dve", t1[:, :], Gr[:, :], tabC[j][:, :], ALU.mult, ["Gr", TC], ["t1"])
                s.tt("dve", t2[:, :], Gi[:, :], tabS[j][:, :], ALU.mult, ["Gi", TS], ["t2"])
                s.tt("dve", Hr[:, :], t1[:, :], t2[:, :], ALU.subtract, ["t1", "t2"], [HR])
                s.tt("pool", t3[:, :], Gi[:, :], tabC[j][:, :], ALU.mult, ["Gi", TC], ["t3"])
                s.tt("pool", t4[:, :], Gr[:, :], tabS[j][:, :], ALU.mult, ["Gr", TS], ["t4"])
                s.tt("pool", Hi[:, :], t3[:, :], t4[:, :], ALU.add, ["t3", "t4"], [HI])
                s.mm(lambda e, gp=gp, j=j, Hr=Hr: e.matmul(psY[:, :], lhsT=Cpre[:, gp, :], rhs=Hr[:, :], start=(j == 0), stop=False), ["Cpre", HR], ["psY"])
                s.mm(lambda e, gp=gp, j=j, Hi=Hi: e.matmul(psY[:, :], lhsT=Cpim[:, gp, :], rhs=Hi[:, :], start=False, stop=(j == 3)), ["Cpim", HI], ["psY"])
            s.stt("dve", yf[:, :], uTf[ub][:, :], dcol[:, i:i + 1], psY[:, :], ALU.mult, ALU.add, ["uTf%d" % ub, "dcol", "psY"], ["yf"])
            s.act(y2[ub][:, :], yf[:, :], AF.Gelu, ["yf"], ["y2%d" % ub])
            p.dma(g.y2T[i * 128:(i + 1) * 128, t0:t0 + 512], y2[ub][:, :], ["y2%d" % ub], ["y2T"])
    s.done()


def stage_ssm_glu(p, nc, g, l):
    T = g.T
    W = g.w
    s = Stage(p, nc)
    identb, identf = load_ident(s, g)
    wglu = s.sb("wglu", [128, 8, 1024], BF16)
    wv = W["ssm_w_glu"][l].rearrange("(ct c) n -> c ct n", c=128)
    for q in range(4):
        p.dma(wglu[:, q * 2:(q + 1) * 2, :], wv[:, q * 2:(q + 1) * 2, :], [], ["wglu"], q="pool")
    bv = s.sb("bv", [8, 128])
    p.dma(bv[:, :], W["ssm_b_glu"][l].rearrange("o (i k) -> (o i) k", k=128), [], ["bv"])
    psR = s.ps("psR", [128, 4, 128])
    psZ = [s.ps("psZ%d" % i, [128, 512]) for i in range(2)]
    s.mm(lambda e: e.transpose(out=psR[:, 0, 0:8], in_=bv[:, :], identity=identf[0:8, 0:8]), ["bv", "identf"], ["psR"])
    bcol = s.sb("bcol", [128, 8])
    s.cp("dve", bcol[:, :], psR[:, 0, 0:8], ["psR"], ["bcol"])
    gain = s.sb("gain", [128, 1024])
    p.dma(gain[:, :], bcast_rows(W["out_norm_g"][l][:, 1024:2048]), [], ["fn_gain"])
    y2f = [s.sb("y2f%d" % i, [128, 8, 512]) for i in range(2)]
    y2b = s.sb("y2b", [128, 8, 512], BF16)
    sig = [s.sb("sig%d" % i, [128, 512]) for i in range(2)]
    res = [s.sb("res%d" % i, [128, 512]) for i in range(2)]
    ytok = s.sb("ytok", [128, 4, 1024])
    sq = s.sb("sq", [128, 1024], BF16)
    ss = s.sb("ss", [128, 1])
    ob = [s.sb("ob%d" % i, [128, 1024], BF16) for i in range(2)]
    yv = g.y2T.rearrange("(ct c) t -> c ct t", c=128)
    oi = 0
    for tb in range(T // 512):
        t0 = tb * 512
        b = tb % 2
        YK = "y2f%d" % b
        p.dma(y2f[b][:, :, :], yv[:, :, t0:t0 + 512], ["y2T"], [YK])
        s.cp("act", y2b[:, 0:4, :], y2f[b][:, 0:4, :], [YK], ["y2b"])
        s.cp("pool", y2b[:, 4:8, :], y2f[b][:, 4:8, :], [YK], ["y2b"])
        for nt in range(8):
            zb = nt % 2

            def mmZ(e, nt=nt, zb=zb):
                ins = None
                for ct in range(8):
                    ins = e.matmul(psZ[zb][:, :], lhsT=wglu[:, ct, nt * 128:(nt + 1) * 128], rhs=y2b[:, ct, :], start=(ct == 0), stop=(ct == 7))
                return ins
            s.mm(mmZ, ["wglu", "y2b"], ["psZ%d" % zb])
            s.act(sig[zb][:, :], psZ[zb][:, :], AF.Sigmoid, ["psZ%d" % zb, "bcol"], ["sig%d" % zb], bias=bcol[:, nt:nt + 1])
            s.tt("dve" if zb == 0 else "pool", res[zb][:, :], y2f[b][:, nt, :], sig[zb][:, :], ALU.mult, [YK, "sig%d" % zb], ["res%d" % zb])

            def trR(e, zb=zb):
                ins = None
                for tt in range(4):
                    ins = e.transpose(out=psR[:, tt, :], in_=res[zb][:, tt * 128:(tt + 1) * 128], identity=identf[:, :])
                return ins
            s.mm(trR, ["res%d" % zb, "identf"], ["psR"])
            s.cp("dve" if nt % 2 == 0 else "act", ytok[:, :, nt * 128:(nt + 1) * 128], psR[:, :, :], ["psR"], ["ytok"])
        for tt in range(4):
            o_i = oi % 2
            oi += 1
            final_norm(s, g, l, ytok[:, tt, :], "ytok", 1024, t0 + tt * 128, gain[:, :], sq[:, :], ss[:, :], ob[o_i][:, :], "ob%d" % o_i)
    s.done()


def norm_mod_transpose(s, g, l, xsrc, xkey, norm_key, sh_col, sc_col, TB, consume):
    p = s.p
    T = g.T
    NB = T // TB
    NT = TB // 128
    ident, _ = load_ident(s, g)
    A1 = s.sb("A1", [128, D])
    B1 = s.sb("B1", [128, D])
    hT = s.sb("hT", [128, 32, TB], BF16)
    xt = s.sb("xt", [128, D])
    ssq = s.sb("ssq", [128, 1])
    hb = [s.sb("hb%d" % i, [128, D], BF16) for i in range(2)]
    pst = [s.ps("pst%d" % i, [128, 512], BF16) for i in range(2)]
    p.dma(A1[:, :], bcast_rows(g.mod[l:l + 1, sc_col * D:(sc_col + 1) * D]), ["mod"], ["A1"])
    p.dma(xt[:, :], bcast_rows(g.w[norm_key][l]), [], ["xt"])
    p.dma(B1[:, :], bcast_rows(g.mod[l:l + 1, sh_col * D:(sh_col + 1) * D]), ["mod"], ["B1"])
    s.stt("dve", A1[:, :], A1[:, :], 1.0, xt[:, :], ALU.add, ALU.mult, ["A1", "xt"], ["A1"])
    ti = 0
    for nb in range(NB):
        for tt in range(NT):
            b = ti % 2
            ti += 1
            r0 = nb * TB + tt * 128
            p.dma(xt[:, :], xsrc[r0:r0 + 128, :], [xkey], ["xt"])
            s.act(hb[b][:, :], xt[:, :], AF.Square, ["xt"], ["hb%d" % b, "ssq"], accum_out=ssq[:, :])
            s.rsqrt(ssq[:, :], "ssq", 1.0 / D, EPS)
            s.stt("dve", xt[:, :], xt[:, :], ssq[:, 0:1], A1[:, :], ALU.mult, ALU.mult, ["xt", "ssq", "A1"], ["xt"])
            s.tt("pool", hb[b][:, :], xt[:, :], B1[:, :], ALU.add, ["xt", "B1"], ["hb%d" % b])
            for q4 in range(8):
                pb = q4 % 2

                def tr(e, b=b, q4=q4, pb=pb):
                    ins = None
                    for j in range(4):
                        dk = q4 * 4 + j
                        ins = e.transpose(out=pst[pb][:, j * 128:(j + 1) * 128], in_=hb[b][:, dk * 128:(dk + 1) * 128], identity=ident[:, :])
                    return ins
                s.mm(tr, ["hb%d" % b, "ident"], ["pst%d" % pb])
                outap = hT[:, q4 * 4:(q4 + 1) * 4, tt * 128:(tt + 1) * 128]
                inap = pst[pb][:, :].rearrange("q (j t) -> q j t", j=4)
                s.cp("act" if q4 % 2 == 0 else "dve", outap, inap, ["pst%d" % pb], ["hT"])
        consume(nb, hT)


def stage_wout(p, nc, g, l, xsrc, xdst):
    T = g.T
    TB = min(T, 1024)
    NB = T // TB
    NT = TB // 128
    s = Stage(p, nc)
    ident, _ = load_ident(s, g)
    g1 = s.sb("g1", [128, D])
    p.dma(g1[:, :], bcast_rows(g.mod[l:l + 1, 2 * D:3 * D]), ["mod"], ["g1"])
    mT = s.sb("mT", [128, 32, TB], BF16)
    mt = [s.sb("mt%d" % i, [128, D], BF16) for i in range(2)]
    wb = [s.sb("wb%d" % i, [128, 32, 512], BF16) for i in range(2)]
    xs = [s.sb("xs%d" % i, [128, 512]) for i in range(2)]
    tm = [s.sb("tm%d" % i, [128, 512]) for i in range(2)]
    ob = [s.sb("ob%d" % i, [128, 512]) for i in range(2)]
    pst = [s.ps("pst%d" % i, [128, 512], BF16) for i in range(2)]
    pso = [s.ps("pso%d" % i, [128, 512]) for i in range(2)]
    wv = g.w["w_out"][l].rearrange("(kc q) n -> q kc n", q=128)
    ti = wi = oi = 0
    for nb in range(NB):
        for tt in range(NT):
            b = ti % 2
            ti += 1
            r0 = nb * TB + tt * 128
            p.dma(mt[b][:, :], g.mix[r0:r0 + 128, :], ["mix"], ["mt%d" % b])
            for q4 in range(8):
                pb = q4 % 2

                def tr(e, b=b, q4=q4, pb=pb):
                    ins = None
                    for j in range(4):
                        dk = q4 * 4 + j
                        ins = e.transpose(out=pst[pb][:, j * 128:(j + 1) * 128], in_=mt[b][:, dk * 128:(dk + 1) * 128], identity=ident[:, :])
                    return ins
                s.mm(tr, ["mt%d" % b, "ident"], ["pst%d" % pb])
                outap = mT[:, q4 * 4:(q4 + 1) * 4, tt * 128:(tt + 1) * 128]
                inap = pst[pb][:, :].rearrange("q (j t) -> q j t", j=4)
                s.cp("act" if q4 % 2 == 0 else "dve", outap, inap, ["pst%d" % pb], ["mT"])
        for cc in range(8):
            b = wi % 2
            wi += 1
            cols = slice(cc * 512, (cc + 1) * 512)
            p.dma(wb[b][:, :, :], wv[:, :, cols], [], ["wb%d" % b], q="pool")
            for tt in range(NT):
                o_i = oi % 2
                oi += 1
                r0 = nb * TB + tt * 128
                p.dma(xs[o_i][:, :], xsrc[r0:r0 + 128, cols], ["xsrc"], ["xs%d" % o_i])

                def mm(e, b=b, tt=tt, o_i=o_i):
                    ins = None
                    for kc in range(32):
                        ins = e.matmul(pso[o_i][:, :], lhsT=mT[:, kc, tt * 128:(tt + 1) * 128], rhs=wb[b][:, kc, :], start=(kc == 0), stop=(kc == 31))
                    return ins
                s.mm(mm, ["mT", "wb%d" % b], ["pso%d" % o_i])
                s.tt("dve", tm[o_i][:, :], pso[o_i][:, :], g1[:, cols], ALU.mult, ["pso%d" % o_i, "g1"], ["tm%d" % o_i])
                s.tt("pool", ob[o_i][:, :], tm[o_i][:, :], xs[o_i][:, :], ALU.add, ["tm%d" % o_i, "xs%d" % o_i], ["ob%d" % o_i])
                p.dma(xdst[r0:r0 + 128, cols], ob[o_i][:, :], ["ob%d" % o_i], ["xdst"], q="act")
    s.done()


def stage_peer_h(p, nc, g, l, xsrc):
    TB = min(g.T, 512)
    s = Stage(p, nc)
    hv = g.h2T.rearrange("kc d t -> d kc t")

    def consume(nb, hT):
        p.dma(hv[:, :, nb * TB:(nb + 1) * TB], hT[:, :, :], ["hT"], ["h2T"])
    norm_mod_transpose(s, g, l, xsrc, "xsrc", "norm2_g", 3, 4, TB, consume)
    s.done()


def stage_peer_q(p, nc, g, l):
    T = g.T
    TB = min(T, 512)
    s = Stage(p, nc)
    wq = s.sb("wq", [128, 32, 2048], BF16)
    wv = g.w["peer_w_query"][l].rearrange("(kc q) n -> q kc n", q=128)
    for q in range(8):
        p.dma(wq[:, q * 4:(q + 1) * 4, :], wv[:, q * 4:(q + 1) * 4, :], [], ["wq"], q="pool")
    hT = [s.sb("hT%d" % i, [128, 32, TB], BF16) for i in range(2)]
    qs = [s.sb("qs%d" % i, [128, TB], BF16) for i in range(2)]
    psq = [s.ps("psq%d" % i, [128, 512]) for i in range(2)]
    hv = g.h2T.rearrange("kc d t -> d kc t")
    oi = 0
    for nb in range(T // TB):
        b = nb % 2
        p.dma(hT[b][:, :, :], hv[:, :, nb * TB:(nb + 1) * TB], ["h2T"], ["hT%d" % b])
        for cq in range(16):
            o_i = oi % 2
            oi += 1

            def mm(e, b=b, cq=cq, o_i=o_i):
                ins = None
                for kc in range(32):
                    ins = e.matmul(psq[o_i][:, 0:TB], lhsT=wq[:, kc, cq * 128:(cq + 1) * 128], rhs=hT[b][:, kc, :], start=(kc == 0), stop=(kc == 31))
                return ins
            s.mm(mm, ["wq", "hT%d" % b], ["psq%d" % o_i])
            s.cp("act" if o_i == 0 else "dve", qs[o_i][:, :], psq[o_i][:, 0:TB], ["psq%d" % o_i], ["qs%d" % o_i])
            p.dma(g.qT[cq, :, nb * TB:(nb + 1) * TB], qs[o_i][:, :], ["qs%d" % o_i], ["qT"])
    s.done()


def stage_peer_g(p, nc, g, l):
    T = g.T
    K = g.k
    s = Stage(p, nc)
    identb, identf = load_ident(s, g)
    kf = s.sb("kf", [128, 2, 128])
    for pp in range(2):
        p.dma(kf[:, pp, :], g.w["peer_sub_keys"][l][pp], [], ["kf"])
    psK = s.ps("psK", [128, 512])

    def trK(e):
        ins = None
        for pp in range(2):
            ins = e.transpose(out=psK[:, pp * 128:(pp + 1) * 128], in_=kf[:, pp, :], identity=identf[:, :])
        return ins
    s.mm(trK, ["kf", "identf"], ["psK"])
    kT = s.sb("kT", [128, 2, 128], BF16)
    s.cp("dve", kT[:, :, :].rearrange("d a n -> d (a n)"), psK[:, 0:256], ["psK"], ["kT"])
    TN = 32
    iof = s.sb("iof", [128, 128])
    p.dma(iof[:, :], K["iota128"], [], ["iof"])
    iota3 = s.sb("iota3", [128, TN, 128], BF16)
    s.cp("dve", iota3[:, :, :], iof[:, :].unsqueeze(1).to_broadcast([128, TN, 128]), ["iof"], ["iota3"])
    iota16 = s.sb("iota16", [128, 128, 16])
    p.dma(iota16[:, :, :].rearrange("p a b -> p (a b)"), K["iota16"], [], ["iota16"])
    qt = [s.sb("qt%d" % i, [128, 16, 128], BF16) for i in range(2)]
    scs = [s.sb("sc%d" % i, [128, 16, 128]) for i in range(2)]
    tops = [s.sb("top%d" % i, [128, 16, 16]) for i in range(2)]
    work = s.sb("work", [128, 16, 128])
    idxs = [s.sb("idx%d" % i, [128, 16, 16], U32) for i in range(2)]
    idxf = s.sb("idxf", [128, 16, 16])
    cand = s.sb("cand", [128, 8, 16, 16])
    work2 = s.sb("work2", [128, 8, 256])
    best = s.sb("best", [128, 8, 16])
    pos = s.sb("pos", [128, 8, 16], U32)
    pa = s.sb("pa", [128, 128], U32)
    pb_ = s.sb("pb", [128, 128], U32)
    paf = s.sb("paf", [128, 128])
    pbf = s.sb("pbf", [128, 128])
    negm = s.sb("negm", [128, 8])
    eg = s.sb("eg", [128, 8, 16])
    sme = s.sb("sme", [128, 8])
    ohA = s.sb("ohA", [128, 128, 16])
    ohB = s.sb("ohB", [128, 128, 16])
    sel3 = s.sb("sel3", [128, 3, 128])
    selTs = [s.sb("selT%d" % i, [128, 3, 128], BF16) for i in range(2)]
    Ablk = [s.sb("Ablk%d" % i, [128, TN, 128], BF16) for i in range(2)]
    Bblk = [s.sb("Bblk%d" % i, [128, TN, 128], BF16) for i in range(2)]
    Gs = s.sb("Gs", [128, 128, 128], BF16)
    psSc = [s.ps("psSc%d" % i, [128, 4, 128]) for i in range(2)]
    psT3 = s.ps("psT3", [128, 4, 128])
    psG = [s.ps("psG%d" % i, [128, 4, 128]) for i in range(3)]
    qv = g.qT.rearrange("cq d t -> d cq t")
    gv = g.Gd.rearrange("i j t -> j i t")
    bi = 0
    NTL = T // 128

    def phase1(n):
        b = n % 2
        r0 = n * 128
        QT = "qt%d" % b
        sc, top, idx = scs[b], tops[b], idxs[b]
        S = "b%d_" % b
        p.dma(qt[b][:, :, :], qv[:, :, r0:r0 + 128], ["qT"], [QT])
        for grp in range(4):
            sb_ = grp % 2

            def mmS(e, b=b, grp=grp, sb_=sb_):
                ins = None
                for a in range(4):
                    cq = grp * 4 + a
                    ins = e.matmul(psSc[sb_][:, a, :], lhsT=qt[b][:, cq, :], rhs=kT[:, cq % 2, :], start=True, stop=True)
                return ins
            s.mm(mmS, [QT, "kT"], ["psSc%d" % sb_])
            s.cp("act", sc[:, grp * 4:(grp + 1) * 4, :], psSc[sb_][:, :, :], ["psSc%d" % sb_], [S + "sc%d" % grp])
        for cq in range(16):
            p.add("dve", lambda e, cq=cq: e.max(out=top[:, cq, 0:8], in_=sc[:, cq, :]), [S + "sc%d" % (cq // 4)], [S + "topa%d" % cq])
        for cq in range(16):
            p.add("dve", lambda e, cq=cq: e.match_replace(out=work[:, cq, :], in_to_replace=top[:, cq, 0:8], in_values=sc[:, cq, :], imm_value=-1e30),
                  [S + "sc%d" % (cq // 4), S + "topa%d" % cq], ["work%d" % cq])
        for cq in range(16):
            p.add("dve", lambda e, cq=cq: e.max(out=top[:, cq, 8:16], in_=work[:, cq, :]), ["work%d" % cq], [S + "topb%d" % cq])
        for cq in range(16):
            p.add("dve", lambda e, cq=cq: e.max_index(out=idx[:, cq, 0:8], in_max=top[:, cq, 0:8], in_values=sc[:, cq, :]),
                  [S + "sc%d" % (cq // 4), S + "topa%d" % cq], [S + "idxa%d" % cq])
        for cq in range(16):
            p.add("dve", lambda e, cq=cq: e.max_index(out=idx[:, cq, 8:16], in_max=top[:, cq, 8:16], in_values=sc[:, cq, :]),
                  [S + "sc%d" % (cq // 4), S + "topb%d" % cq], [S + "idxb%d" % cq])

    def phase2(n):
        nonlocal bi
        b = n % 2
        r0 = n * 128
        sc, top, idx = scs[b], tops[b], idxs[b]
        S = "b%d_" % b
        selT = selTs[b]
        STK = "selT%d" % b
        s.cp("dve", idxf[:, :, :], idx[:, :, :], [S + "idxa%d" % i for i in range(16)] + [S + "idxb%d" % i for i in range(16)], ["idxf"])
        for h in range(8):
            s.cp("dve", cand[:, h, :, :], top[:, 2 * h, :].unsqueeze(2).to_broadcast([128, 16, 16]),
                 [S + "topa%d" % (2 * h), S + "topb%d" % (2 * h)], ["cand%d" % h])
        for h in range(8):
            s.tt("dve", cand[:, h, :, :], cand[:, h, :, :], top[:, 2 * h + 1, :].unsqueeze(1).to_broadcast([128, 16, 16]), ALU.add,
                 ["cand%d" % h, S + "topa%d" % (2 * h + 1), S + "topb%d" % (2 * h + 1)], ["cand%d" % h])
        cfs = [cand[:, h, :, :].rearrange("p a b -> p (a b)") for h in range(8)]
        for h in range(8):
            p.add("dve", lambda e, h=h: e.max(out=best[:, h, 0:8], in_=cfs[h]), ["cand%d" % h], ["besta%d" % h])
        for h in range(8):
            p.add("dve", lambda e, h=h: e.match_replace(out=work2[:, h, :], in_to_replace=best[:, h, 0:8], in_values=cfs[h], imm_value=-1e30),
                  ["cand%d" % h, "besta%d" % h], ["work2%d" % h])
        for h in range(8):
            p.add("dve", lambda e, h=h: e.max(out=best[:, h, 8:16], in_=work2[:, h, :]), ["work2%d" % h], ["bestb%d" % h])
        for h in range(8):
            p.add("dve", lambda e, h=h: e.max_index(out=pos[:, h, 0:8], in_max=best[:, h, 0:8], in_values=cfs[h]), ["cand%d" % h, "besta%d" % h], ["posa%d" % h])
        for h in range(8):
            p.add("dve", lambda e, h=h: e.max_index(out=pos[:, h, 8:16], in_max=best[:, h, 8:16], in_values=cfs[h]), ["cand%d" % h, "bestb%d" % h], ["posb%d" % h])
        BKS = ["besta%d" % h for h in range(8)] + ["bestb%d" % h for h in range(8)]
        PKS = ["posa%d" % h for h in range(8)] + ["posb%d" % h for h in range(8)]
        s.ts("dve", negm[:, :], best[:, :, 0], -1.0, None, ALU.mult, None, BKS, ["negm"])
        for h in range(8):
            s.act(eg[:, h, :], best[:, h, :], AF.Exp, ["besta%d" % h, "bestb%d" % h, "negm"], ["eg%d" % h, "sme%d" % h], bias=negm[:, h:h + 1], accum_out=sme[:, h:h + 1])
        posf = pos[:, :, :].rearrange("p h k -> p (h k)")
        s.ts("dve", pa[:, :], posf, 4, None, ALU.arith_shift_right, None, PKS, ["pa"])
        s.ts("dve", pb_[:, :], posf, 15, None, ALU.bitwise_and, None, PKS, ["pb"])
        s.cp("dve", paf[:, :], pa[:, :], ["pa"], ["paf"])
        s.cp("dve", pbf[:, :], pb_[:, :], ["pb"], ["pbf"])
        s.tt("dve", ohA[:, :, :], paf[:, :].unsqueeze(2).to_broadcast([128, 128, 16]), iota16[:, :, :], ALU.is_equal, ["paf", "iota16"], ["ohA"])
        s.tt("dve", ohB[:, :, :], pbf[:, :].unsqueeze(2).to_broadcast([128, 128, 16]), iota16[:, :, :], ALU.is_equal, ["pbf", "iota16"], ["ohB"])
        for h in range(8):
            s.tt("dve", ohA[:, h * 16:(h + 1) * 16, :], ohA[:, h * 16:(h + 1) * 16, :],
                 idxf[:, 2 * h, :].unsqueeze(1).to_broadcast([128, 16, 16]), ALU.mult, ["ohA", "idxf"], ["ohA%d" % h])
            s.tt("pool", ohB[:, h * 16:(h + 1) * 16, :], ohB[:, h * 16:(h + 1) * 16, :],
                 idxf[:, 2 * h + 1, :].unsqueeze(1).to_broadcast([128, 16, 16]), ALU.mult, ["ohB", "idxf"], ["ohB%d" % h])
        p.add("dve", lambda e: e.tensor_reduce(out=sel3[:, 0, :], in_=ohA[:, :, :], axis=AX.X, op=ALU.add), ["ohA%d" % h for h in range(8)], ["sel3i"])
        p.add("dve", lambda e: e.tensor_reduce(out=sel3[:, 1, :], in_=ohB[:, :, :], axis=AX.X, op=ALU.add), ["ohB%d" % h for h in range(8)], ["sel3j"])
        p.add("dve", lambda e: e.reciprocal(out=sme[:, :], in_=sme[:, :]), ["sme%d" % h for h in range(8)], ["rs"])
        s.tt("dve", sel3[:, 2, :].rearrange("p (h k) -> p h k", h=8), sme[:, :].unsqueeze(2).to_broadcast([128, 8, 16]), eg[:, :, :], ALU.mult,
             ["rs"] + ["eg%d" % h for h in range(8)], ["sel3g"])

        def trS(e):
            ins = None
            for a in range(3):
                ins = e.transpose(out=psT3[:, a, :], in_=sel3[:, a, :], identity=identf[:, :])
            return ins
        s.mm(trS, ["sel3i", "sel3j", "sel3g", "identf"], ["psT3"])
        s.cp("act", selT[:, :, :], psT3[:, 0:3, :], ["psT3"], [STK])

    def phase3(n):
        nonlocal bi
        b = n % 2
        r0 = n * 128
        selT = selTs[b]
        STK = "selT%d" % b
        for tq in range(128 // TN):
            k = bi % 2
            bi += 1
            tsl = slice(tq * TN, (tq + 1) * TN)
            s.tt("dve", Bblk[k][:, :, :], selT[:, 1, tsl].unsqueeze(2).to_broadcast([128, TN, 128]), iota3[:, :, :], ALU.is_equal,
                 ["iota3", STK], ["B%d" % k])
            s.tt("dve", Ablk[k][:, :, :], selT[:, 0, tsl].unsqueeze(2).to_broadcast([128, TN, 128]), iota3[:, :, :], ALU.is_equal,
                 ["iota3", STK], ["A%d" % k])
            s.tt("pool", Ablk[k][:, :, :], Ablk[k][:, :, :], selT[:, 2, tsl].unsqueeze(2).to_broadcast([128, TN, 128]), ALU.mult,
                 ["A%d" % k, STK], ["A%d" % k])
            for q in range(TN // 4):
                gq_ = tq * (TN // 4) + q
                gb = gq_ % 3

                def mmG(e, k=k, q=q, gb=gb):
                    ins = None
                    for a in range(4):
                        tl = q * 4 + a
                        ins = e.matmul(psG[gb][:, a, :], lhsT=Bblk[k][:, tl, :], rhs=Ablk[k][:, tl, :], start=True, stop=True)
                    return ins
                s.mm(mmG, ["A%d" % k, "B%d" % k], ["psG%d" % gb])
                s.cp("act", Gs[:, :, gq_ * 4:(gq_ + 1) * 4].rearrange("j i t -> j t i"), psG[gb][:, :, :], ["psG%d" % gb], ["Gs"])
        p.dma(gv[:, :, r0:r0 + 128], Gs[:, :, :], ["Gs"], ["Gd"])


    phase1(0)
    phase2(0)
    for n in range(NTL):
        if n + 1 < NTL:
            phase1(n + 1)
            phase2(n + 1)
        phase3(n)
    s.done()


def stage_peer_ut(p, nc, g, l):
    s = Stage(p, nc)
    identb, identf = load_ident(s, g)
    Uc = [s.sb("Uc%d" % i, [128, D], BF16) for i in range(3)]
    UcT = [s.sb("UcT%d" % i, [128, 32, 128], BF16) for i in range(2)]
    psU = [s.ps("psU%d" % i, [128, 8, 128], BF16) for i in range(4)]
    ui = 0
    for c in range(128):
        b3 = c % 3
        b = c % 2
        p.dma(Uc[b3][:, :], g.w["peer_u"][l][c * 128:(c + 1) * 128, :], [], ["Uc%d" % b3], q="pool")
        for q4 in range(4):
            ub = ui % 4
            ui += 1

            def trU(e, b3=b3, q4=q4, ub=ub):
                ins = None
                for a in range(8):
                    kc = q4 * 8 + a
                    ins = e.transpose(out=psU[ub][:, a, :], in_=Uc[b3][:, kc * 128:(kc + 1) * 128], identity=identb[:, :])
                return ins
            s.mm(trU, ["Uc%d" % b3, "ident"], ["psU%d" % ub])
            s.cp("act" if q4 % 2 == 0 else "dve", UcT[b][:, q4 * 8:(q4 + 1) * 8, :], psU[ub][:, :, :], ["psU%d" % ub], ["UcT%d" % b])
        p.dma(g.UTd[c], UcT[b][:, :, :], ["UcT%d" % b], ["UTd"])
    s.done()


def stage_peer_a(p, nc, g, l):
    T = g.T
    TB = min(T, 1024)
    NH = max(1, TB // 512)
    TW = min(512, TB)
    s = Stage(p, nc)
    NBUF = 4
    PF = 3
    hT = s.sb("hT", [128, 32, TB], BF16)
    UcT = [s.sb("UcT%d" % i, [128, 32, 128], BF16) for i in range(NBUF)]
    Gc = [s.sb("Gc%d" % i, [128, TB], BF16) for i in range(NBUF)]
    ga = [s.sb("ga%d" % i, [128, 512]) for i in range(2)]
    Wc = [s.sb("Wc%d" % i, [128, TB], BF16) for i in range(3)]
    psA = [s.ps("psA%d" % i, [128, 512]) for i in range(6)]
    hv = g.h2T.rearrange("kc d t -> d kc t")
    its = [(nb, c) for nb in range(T // TB) for c in range(128)]

    def load(i):
        nb, c = its[i]
        k = i % NBUF
        tsl = slice(nb * TB, (nb + 1) * TB)
        p.dma(UcT[k][:, :, :], g.UTd[c], ["UTd"], ["UcT%d" % k])
        p.dma(Gc[k][:, :], g.Gd[c, :, tsl], ["Gd"], ["Gc%d" % k])
    ai = 0
    for i in range(min(PF, len(its))):
        load(i)
    for i, (nb, c) in enumerate(its):
        tsl = slice(nb * TB, (nb + 1) * TB)
        if c == 0:
            p.dma(hT[:, :, :], hv[:, :, tsl], ["h2T"], ["hT"], q="act")
        if i + PF < len(its):
            load(i + PF)
        k = i % NBUF
        wb_ = i % 3
        for th in range(NH):
            pk = ai % 6
            gk = ai % 2
            ai += 1

            def mmA(e, k=k, th=th, pk=pk):
                ins = None
                for kc in range(32):
                    ins = e.matmul(psA[pk][:, 0:TW], lhsT=UcT[k][:, kc, :], rhs=hT[:, kc, th * 512:th * 512 + TW], start=(kc == 0), stop=(kc == 31))
                return ins
            s.mm(mmA, ["UcT%d" % k, "hT"], ["psA%d" % pk])
            s.act(ga[gk][:, 0:TW], psA[pk][:, 0:TW], AF.Gelu, ["psA%d" % pk], ["ga%d" % gk])
            s.tt("dve" if gk == 0 else "pool", Wc[wb_][:, th * 512:th * 512 + TW], ga[gk][:, 0:TW], Gc[k][:, th * 512:th * 512 + TW], ALU.mult,
                 ["ga%d" % gk, "Gc%d" % k], ["Wc%d" % wb_])
        p.dma(g.Wd[c, :, tsl], Wc[wb_][:, :], ["Wc%d" % wb_], ["Wd"], q="act")
    s.done()


def stage_peer_b(p, nc, g, l, xsrc, xdst):
    T = g.T
    TB = min(T, 1024)
    NT = TB // 128
    s = Stage(p, nc)
    g2 = s.sb("g2", [128, D])
    p.dma(g2[:, :], bcast_rows(g.mod[l:l + 1, 5 * D:6 * D]), ["mod"], ["g2"])
    NR = 4
    PF = 3
    Vc = [s.sb("Vc%d" % i, [128, 4, 512], BF16) for i in range(NR)]
    Wc = [s.sb("Wc%d" % i, [128, 4, TB], BF16) for i in range(NR)]
    xs = [s.sb("xs%d" % i, [128, 512]) for i in range(2)]
    tm = [s.sb("tm%d" % i, [128, 512]) for i in range(2)]
    ob = [s.sb("ob%d" % i, [128, 512]) for i in range(2)]
    psB = [s.ps("psB%d" % i, [128, 512]) for i in range(NT)]
    vv = g.w["peer_v"][l].rearrange("(c4 a e) d -> c4 e a d", a=4, e=128)
    wv = g.Wd.rearrange("(c4 a) e t -> c4 e a t", a=4)
    its = [(nb, r, c4) for nb in range(T // TB) for r in range(8) for c4 in range(32)]

    def load(i):
        nb, r, c4 = its[i]
        k = i % NR
        p.dma(Vc[k][:, :, :], vv[c4][:, :, r * 512:(r + 1) * 512], [], ["Vc%d" % k], q="pool")
        p.dma(Wc[k][:, :, :], wv[c4][:, :, nb * TB:(nb + 1) * TB], ["Wd"], ["Wc%d" % k])
    for i in range(min(PF, len(its))):
        load(i)
    oi = 0
    for i, (nb, r, c4) in enumerate(its):
        if i + PF < len(its):
            load(i + PF)
        k = i % NR
        cols = slice(r * 512, (r + 1) * 512)

        def mmB(e, k=k, c4=c4):
            ins = None
            for a in range(4):
                for tt in range(NT):
                    ins = e.matmul(psB[tt][:, :], lhsT=Wc[k][:, a, tt * 128:(tt + 1) * 128], rhs=Vc[k][:, a, :],
                                   start=(c4 == 0 and a == 0), stop=(c4 == 31 and a == 3))
            return ins
        s.mm(mmB, ["Vc%d" % k, "Wc%d" % k], ["psB"])
        if c4 == 31:
            for tt in range(NT):
                o_i = oi % 2
                oi += 1
                r0 = nb * TB + tt * 128
                p.dma(xs[o_i][:, :], xsrc[r0:r0 + 128, cols], ["xsrc"], ["xs%d" % o_i], q="act")
                s.tt("dve", tm[o_i][:, :], psB[tt][:, :], g2[:, cols], ALU.mult, ["psB", "g2"], ["tm%d" % o_i])
                s.tt("pool", ob[o_i][:, :], tm[o_i][:, :], xs[o_i][:, :], ALU.add, ["tm%d" % o_i, "xs%d" % o_i], ["ob%d" % o_i])
                p.dma(xdst[r0:r0 + 128, cols], ob[o_i][:, :], ["ob%d" % o_i], ["xdst"], q="act")
    s.done()


def build_all(p, nc, g):
    stage_ada(p, nc, g)
    stage_rot(p, nc, g)
    xs = g.x
    for l in range(g.L):
        xo = g.out if l == g.L - 1 else g.x2
        stage_proj(p, nc, g, l, xs)
        stage_ret(p, nc, g, l)
        stage_ssm_scan(p, nc, g, l)
        stage_ssm_glu(p, nc, g, l)
        stage_swa(p, nc, g, l)
        stage_pool(p, nc, g, l)
        stage_wout(p, nc, g, l, xs, g.x1)
        stage_peer_h(p, nc, g, l, g.x1)
        stage_peer_q(p, nc, g, l)
        stage_peer_g(p, nc, g, l)
        stage_peer_ut(p, nc, g, l)
        stage_peer_a(p, nc, g, l)
        stage_peer_b(p, nc, g, l, g.x1, xo)
        xs = xo


from concourse.bass_utils import run_bass_kernel_spmd

SEQ = 4096
NCORES = 4
_NC_CACHE = {}


def _build(T, L):
    key = (T, L)
    if key not in _NC_CACHE:
        nc = bass.Bass("TRN2", target_bir_lowering=False)
        g = declare(nc, T, L)
        with ExitStack() as st:
            p = Prog(nc, st)
            build_all(p, nc, g)
        _NC_CACHE[key] = nc
    return _NC_CACHE[key]


def _in_map(b, inputs, consts, L):
    m = {}
    m["x"] = np.ascontiguousarray(inputs["x"][b])
    m["c"] = np.ascontiguousarray(inputs["c"][b:b + 1])
    m["pos"] = np.ascontiguousarray(inputs["positions"][b].astype(np.int32).reshape(-1, 1))
    for k, shp in WEIGHT_SHAPES.items():
        a = np.asarray(inputs[k])
        if len(shp) == 1:
            a = a.reshape(L, 1, shp[0])
        m[k] = a
    for k, shp in CONST_SHAPES.items():
        m["k_" + k] = np.ascontiguousarray(consts[k], dtype=np.float32).reshape(shp)
    return m


def kernel(**inputs):
    inputs = {k: np.asarray(v) for k, v in inputs.items()}
    B, S, _ = inputs["x"].shape
    L = inputs["ada_w"].shape[0]
    nc = _build(S, L)
    consts = host_consts_cache()
    maps = [_in_map(i % B, inputs, consts, L) for i in range(NCORES)]
    res = run_bass_kernel_spmd(nc, maps, core_ids=list(range(NCORES)))
    out = np.stack([np.asarray(res.results[b]["out"]) for b in range(B)], axis=0)
    return out.astype(np.float32)
```

```python
import numpy as np
import concourse.bass as bass
import concourse.mybir as mybir

F32 = mybir.dt.float32
BF16 = mybir.dt.bfloat16
I32 = mybir.dt.int32
U32 = mybir.dt.uint32
ALU = mybir.AluOpType
AF = mybir.ActivationFunctionType
AX = mybir.AxisListType

ENGS = ("pe", "act", "dve", "pool", "sp")


class Prog:
    NDS = 12

    def __init__(self, nc, stack):
        self.nc = nc
        self.ops = []
        self.csem = {}
        self.ccnt = {}
        self.dsem = {}
        self.dcnt = {}
        for e in ENGS:
            if e != "sp":
                self.csem[e] = stack.enter_context(nc.semaphore("c_" + e))
                self.ccnt[e] = 0
        for e in ("sp", "act", "pool"):
            self.dsem[e] = [stack.enter_context(nc.semaphore("d_%s%d" % (e, i))) for i in range(self.NDS)]
            self.dcnt[e] = 0
        self.seen = {e: {} for e in ENGS}
        self.nstage = 0

    def add(self, eng, fn, r=(), w=(), dma=False):
        self.ops.append((eng, fn, tuple(r), tuple(w), dma))

    def dma(self, out, in_, r, w, q="sp", **kw):
        self.add(q, lambda e: e.dma_start(out=out, in_=in_, **kw), r, w, dma=True)

    def emit(self):
        nc = self.nc
        ops = self.ops
        self.ops = []
        n = len(ops)
        deps = [None] * n
        last_w = {}
        readers = {}
        for i, (eng, fn, r, w, dma) in enumerate(ops):
            d = set()
            for k in r:
                j = last_w.get(k)
                if j is not None:
                    d.add(j)
            for k in w:
                j = last_w.get(k)
                if j is not None:
                    d.add(j)
                rr = readers.get(k)
                if rr:
                    d.update(rr)
            for k in r:
                readers.setdefault(k, []).append(i)
            for k in w:
                last_w[k] = i
                readers[k] = []
            d.discard(i)
            if eng == "pe":
                d = {j for j in d if not (ops[j][0] == "pe" and not ops[j][4])}
            deps[i] = d
        hasdep = [False] * n
        for d in deps:
            for j in d:
                hasdep[j] = True
        lastc = {}
        for i, (eng, fn, r, w, dma) in enumerate(ops):
            if not dma and fn is not None:
                lastc[eng] = i
        for e, i in lastc.items():
            hasdep[i] = True
        sig = [None] * n
        prew = [None] * n
        for i, (eng, fn, r, w, dma) in enumerate(ops):
            if dma:
                k = self.dcnt[eng]
                self.dcnt[eng] = k + 1
                s = self.dsem[eng][k % self.NDS]
                v = 16 * (k // self.NDS + 1)
                sig[i] = (s, v, 16)
                if v > 16:
                    prew[i] = (s, v - 16)
            elif fn is not None and hasdep[i]:
                self.ccnt[eng] += 1
                sig[i] = (self.csem[eng], self.ccnt[eng], 1)
        byeng = {e: [] for e in ENGS}
        for i, op in enumerate(ops):
            byeng[op[0]].append(i)
        seen = self.seen

        def run(e, eobj):
            sn = seen[e]
            for i in byeng[e]:
                eng, fn, r, w, dma = ops[i]
                for j in sorted(deps[i]):
                    s, v, _ = sig[j]
                    if sn.get(id(s), 0) < v:
                        eobj.wait_ge(s, v)
                        sn[id(s)] = v
                if prew[i] is not None:
                    s, v = prew[i]
                    key = id(s)
                    if sn.get(key, 0) < v:
                        eobj.wait_ge(s, v)
                        sn[key] = v
                if fn is not None:
                    ins = fn(eobj)
                    if sig[i] is not None:
                        s, v, inc = sig[i]
                        ins.then_inc(s, inc)
            for ee in ENGS:
                if ee != "sp":
                    s = self.csem[ee]
                    v = self.ccnt[ee]
                    key = id(s)
                    if v > 0 and sn.get(key, 0) < v:
                        eobj.wait_ge(s, v)
                        sn[key] = v
            for q in self.dsem:
                k = self.dcnt[q]
                for si, s in enumerate(self.dsem[q]):
                    cnt = (k - si + self.NDS - 1) // self.NDS if k > si else 0
                    v = 16 * cnt
                    key = id(s)
                    if v > 0 and sn.get(key, 0) < v:
                        eobj.wait_ge(s, v)
                        sn[key] = v

        with nc.Block() as block:
            @block.sync
            def _(e):
                run("sp", e)

            @block.scalar
            def _(e):
                run("act", e)

            @block.vector
            def _(e):
                run("dve", e)

            @block.gpsimd
            def _(e):
                run("pool", e)

            @block.tensor
            def _(e):
                run("pe", e)
        self.nstage += 1
        return n


from contextlib import ExitStack
import math
import numpy as np
import concourse.bass as bass
import concourse.mybir as mybir

D = 4096
INC = 6400
EPS = 1e-6
TWO_PI = float(2 * np.pi)
C_QR, C_KR, C_VR, C_GR, C_US, C_QA, C_KA, C_VA, C_UP = 0, 512, 1024, 2048, 3072, 4096, 5120, 5248, 5376


class Ctx:
    pass


def host_consts():
    c = {}
    c["ident"] = np.eye(128, dtype=np.float32)
    inv64 = 10000.0 ** (-np.arange(64, dtype=np.float32) * 2.0 / 128)
    inv32 = 10000.0 ** (-np.arange(32, dtype=np.float32) * 2.0 / 64)
    c["rinv"] = np.concatenate([inv64, inv32]).astype(np.float32)[None, :]
    H, Lc, dk = 4, 128, 128
    log_g = np.log(1.0 - 2.0 ** (-5.0 - np.arange(H, dtype=np.float64)))
    idx = np.arange(Lc, dtype=np.float64)
    diff = idx[None, :] - idx[:, None]
    dec = np.where(diff[:, None, :] >= 0, np.exp(np.maximum(diff[:, None, :], 0) * log_g[None, :, None]), 0.0)
    c["ret_decT"] = (dec * dk ** -0.5).astype(np.float32).reshape(128, H * 128)
    wq = np.exp((idx + 1)[None, :] * log_g[:, None])
    c["ret_wq"] = np.broadcast_to(wq.reshape(1, H * 128), (128, H * 128)).astype(np.float32).copy()
    wk = np.exp((Lc - 1 - idx)[None, :] * log_g[:, None]) * dk ** -0.5
    c["ret_wk"] = wk.T.astype(np.float32).copy()
    c["ret_gL"] = [float(np.exp(Lc * lg)) for lg in log_g]
    kk = np.arange(128)[:, None]
    qq = np.arange(128)[None, :]
    c["swa_mcur"] = (kk <= qq).astype(np.float32)
    c["swa_mprev"] = (kk > qq).astype(np.float32)
    wins = (2, 4, 8, 16)
    mcur = np.zeros((128, 4, 128), np.float32)
    mcur0 = np.zeros((128, 4, 128), np.float32)
    mprev = np.zeros((128, 4, 128), np.float32)
    icnt = np.zeros((128, 4, 128), np.float32)
    icnt0 = np.zeros((128, 4, 128), np.float32)
    for gi, w in enumerate(wins):
        for t in range(128):
            for s in range(t - w + 1, t + 1):
                if s >= 0:
                    mcur[s, gi, t] += 1
                    mcur0[s, gi, t] += 1
                else:
                    mprev[s + 128, gi, t] += 1
            mcur[t, gi, t] -= w
            cnt0 = min(w, t + 1)
            mcur0[t, gi, t] -= cnt0
            icnt[:, gi, t] = 1.0 / w
            icnt0[:, gi, t] = 1.0 / cnt0
    c["pool_mcur"] = mcur.reshape(128, 512)
    c["pool_mcur0"] = mcur0.reshape(128, 512)
    c["pool_mprev"] = mprev.reshape(128, 512)
    c["pool_icnt"] = icnt.reshape(128, 512)
    c["pool_icnt0"] = icnt0.reshape(128, 512)
    c["iota128"] = np.broadcast_to(np.arange(128, dtype=np.float32)[None, :], (128, 128)).copy()
    c["iota512"] = np.broadcast_to(np.arange(1, 513, dtype=np.float32)[None, :], (128, 512)).copy()
    c["iota16"] = np.broadcast_to(np.arange(16, dtype=np.float32)[None, None, :], (128, 128, 16)).reshape(128, 2048).copy()
    return c


CONST_SHAPES = {"ident": [128, 128], "rinv": [1, 96], "ret_decT": [128, 512], "ret_wq": [128, 512], "ret_wk": [128, 4],
                "swa_mcur": [128, 128], "swa_mprev": [128, 128], "pool_mcur": [128, 512], "pool_mcur0": [128, 512],
                "pool_mprev": [128, 512], "pool_icnt": [128, 512], "pool_icnt0": [128, 512], "iota128": [128, 128],
                "iota512": [128, 512], "iota16": [128, 2048]}

WEIGHT_SHAPES = {
    "ada_w": [D, 6 * D], "ada_b": [6 * D], "norm1_g": [D], "norm2_g": [D], "w_in": [D, INC], "w_out": [D, D],
    "out_norm_g": [D], "ssm_a_re": [64, 64], "ssm_a_im": [64, 64], "ssm_log_dt": [64], "ssm_b_re": [64, 64, 16],
    "ssm_b_im": [64, 64, 16], "ssm_c_re": [64, 16, 64], "ssm_c_im": [64, 16, 64], "ssm_d": [1024],
    "ssm_w_glu": [1024, 1024], "ssm_b_glu": [1024], "attn_q_norm": [64], "attn_k_norm": [64], "attn_sinks": [16],
    "pool_w": [4, 256, 256], "pool_scale": [1024], "peer_w_query": [D, 2048], "peer_sub_keys": [2, 128, 128],
    "peer_u": [16384, D], "peer_v": [16384, D]}


def declare(nc, T, L, dbg=(), as_input=()):
    g = Ctx()
    g.T, g.L = T, L

    def inp(name, shape, dt=F32):
        return nc.dram_tensor(name, list(shape), dt, kind="ExternalInput").ap()

    def scr(name, shape, dt=F32):
        kind = "ExternalOutput" if name in dbg else "Internal"
        if name in as_input:
            kind = "ExternalInput"
        return nc.dram_tensor(name, list(shape), dt, kind=kind).ap()

    g.x = inp("x", [T, D])
    g.c = inp("c", [1, D])
    g.pos = inp("pos", [T, 1], I32)
    g.w = {}
    for k, shp in WEIGHT_SHAPES.items():
        if len(shp) == 1:
            shp = [1] + shp
        g.w[k] = inp(k, [L] + shp)
    g.k = {}
    for k, shp in CONST_SHAPES.items():
        g.k[k] = inp("k_" + k, shp)
    g.mod = scr("mod", [L, 6 * D])
    g.rot = scr("rot", [T, 192])
    g.P = scr("P", [T, INC])
    g.PuT = scr("PuT", [1024, T])
    g.y2T = scr("y2T", [1024, T])
    g.mix = scr("mix", [T, D], BF16)
    g.x1 = scr("x1", [T, D])
    g.x2 = scr("x2", [T, D])
    g.h2T = scr("h2T", [32, 128, T], BF16)
    g.Gd = scr("Gd", [128, 128, T], BF16)
    g.qT = scr("qT", [16, 128, T], BF16)
    g.Wd = scr("Wd", [128, 128, T], BF16)
    g.UTd = scr("UTd", [128, 128, 32, 128], BF16)
    g.out = nc.dram_tensor("out", [T, D], F32, kind="ExternalOutput").ap()
    return g


class Stage:
    CNT = [0]

    def __init__(self, p, nc):
        self.p, self.nc = p, nc
        self.st = ExitStack()
        Stage.CNT[0] += 1
        self.pre = "s%d_" % Stage.CNT[0]

    def sb(self, name, shape, dt=F32):
        return self.st.enter_context(self.nc.sbuf_tensor(self.pre + name, list(shape), dt))

    def ps(self, name, shape, dt=F32):
        return self.st.enter_context(self.nc.psum_tensor(self.pre + name, list(shape), dt))

    def done(self):
        self.p.emit()
        self.st.close()

    def tt(self, eng, out, in0, in1, op, r, w):
        self.p.add(eng, lambda e: e.tensor_tensor(out=out, in0=in0, in1=in1, op=op), r, w)

    def ts(self, eng, out, in0, s1, s2, op0, op1, r, w):
        if s2 is None:
            self.p.add(eng, lambda e: e.tensor_scalar(out=out, in0=in0, scalar1=s1, scalar2=None, op0=op0), r, w)
        else:
            self.p.add(eng, lambda e: e.tensor_scalar(out=out, in0=in0, scalar1=s1, scalar2=s2, op0=op0, op1=op1), r, w)

    def stt(self, eng, out, in0, scalar, in1, op0, op1, r, w):
        self.p.add(eng, lambda e: e.scalar_tensor_tensor(out=out, in0=in0, scalar=scalar, in1=in1, op0=op0, op1=op1), r, w)

    def act(self, out, in_, func, r, w, **kw):
        self.p.add("act", lambda e: e.activation(out=out, in_=in_, func=func, **kw), r, w)

    def cp(self, eng, out, in_, r, w):
        if eng == "act":
            self.p.add("act", lambda e: e.copy(out=out, in_=in_), r, w)
        else:
            self.p.add(eng, lambda e: e.tensor_copy(out=out, in_=in_), r, w)

    def rsqrt(self, t, key, scale, eps):
        self.ts("dve", t, t, scale, eps, ALU.mult, ALU.add, [key], [key])
        self.act(t, t, AF.Sqrt, [key], [key])
        self.p.add("dve", lambda e: e.reciprocal(out=t, in_=t), [key], [key])

    def range_reduce(self, dst, src, ki, kf, kd, ks, kki, kkf):
        self.ts("dve", ki, src, 1.0 / TWO_PI, None, ALU.mult, None, [ks], [kki])
        self.cp("dve", kf, ki, [kki], [kkf])
        self.stt("dve", dst, kf, -TWO_PI, src, ALU.mult, ALU.add, [kkf, ks], [kd])

    def mm(self, fn, r, w):
        self.p.add("pe", fn, r, w)


def bcast_rows(ap_row):
    return ap_row.partition_broadcast(128)[:, 0, :]


def stage_ada(p, nc, g):
    L = g.L
    s = Stage(p, nc)
    cT = s.sb("cT", [128, 32])
    cs = s.sb("cs", [128, 32], BF16)
    wt = [s.sb("adaw%d" % i, [128, 32, 512], BF16) for i in range(4)]
    bt = [s.sb("adab%d" % i, [1, 512]) for i in range(2)]
    ob = [s.sb("adao%d" % i, [1, 512]) for i in range(2)]
    pst = [s.ps("adaps%d" % i, [1, 512]) for i in range(2)]
    p.dma(cT[:, :], g.c.rearrange("o (kc q) -> q (o kc)", q=128), r=[], w=["cT"], allow_slow_non_contiguous=True)
    s.act(cs[:, :], cT[:, :], AF.Silu, ["cT"], ["cs"])
    it = 0
    for l in range(L):
        wv = g.w["ada_w"][l].rearrange("(kc q) n -> q kc n", q=128)
        for cc in range(48):
            b = it % 2
            b3 = it % 4
            it += 1
            cols = slice(cc * 512, (cc + 1) * 512)
            p.dma(wt[b3][:, :, :], wv[:, :, cols], r=[], w=["adaw%d" % b3], q="pool")
            p.dma(bt[b][:, :], g.w["ada_b"][l][:, cols], r=[], w=["adab%d" % b])

            def mm(e, b=b, b3=b3):
                ins = None
                for kc in range(32):
                    ins = e.matmul(pst[b][0:1, :], lhsT=cs[:, kc:kc + 1], rhs=wt[b3][:, kc, :], start=(kc == 0), stop=(kc == 31))
                return ins
            s.mm(mm, ["cs", "adaw%d" % b3], ["adaps%d" % b])
            s.tt("dve", ob[b][:, :], pst[b][0:1, :], bt[b][:, :], ALU.add, ["adaps%d" % b, "adab%d" % b], ["adao%d" % b])
            p.dma(g.mod[l:l + 1, cols], ob[b][:, :], r=["adao%d" % b], w=["mod"], q="act")
    s.done()


def stage_rot(p, nc, g):
    T = g.T
    s = Stage(p, nc)
    inv = s.sb("inv", [128, 96])
    posi = [s.sb("posi%d" % i, [128, 1], I32) for i in range(2)]
    posf = s.sb("posf", [128, 1])
    ang = s.sb("ang", [128, 96])
    ang2 = s.sb("ang2", [128, 96])
    ki = s.sb("ki", [128, 96], I32)
    kf = s.sb("kf", [128, 96])
    rr = s.sb("rr", [128, 96])
    res = [s.sb("res%d" % i, [128, 192]) for i in range(2)]
    p.dma(inv[:, :], bcast_rows(g.k["rinv"]), [], ["inv"])
    for tt in range(T // 128):
        b = tt % 2
        p.dma(posi[b][:, :], g.pos[tt * 128:(tt + 1) * 128, :], [], ["posi%d" % b])
        s.cp("dve", posf[:, :], posi[b][:, :], ["posi%d" % b], ["posf"])
        s.ts("dve", ang[:, :], inv[:, :], posf[:, 0:1], None, ALU.mult, None, ["inv", "posf"], ["ang"])
        s.range_reduce(rr[:, :], ang[:, :], ki[:, :], kf[:, :], "rr", "ang", "ki", "kf")
        s.act(res[b][:, 0:96], rr[:, :], AF.Sin, ["rr"], ["res%d" % b])
        s.ts("dve", ang2[:, :], ang[:, :], float(np.pi / 2), None, ALU.add, None, ["ang"], ["ang2"])
        s.range_reduce(rr[:, :], ang2[:, :], ki[:, :], kf[:, :], "rr", "ang2", "ki", "kf")
        s.act(res[b][:, 96:192], rr[:, :], AF.Sin, ["rr"], ["res%d" % b])
        p.dma(g.rot[tt * 128:(tt + 1) * 128, :], res[b][:, :], ["res%d" % b], ["rot"])
    s.done()


def load_ident(s, g):
    identf = s.sb("identf", [128, 128])
    ident = s.sb("identb", [128, 128], BF16)
    s.p.dma(identf[:, :], g.k["ident"], [], ["identf"])
    s.cp("dve", ident[:, :], identf[:, :], ["identf"], ["ident"])
    return ident, identf


def stage_proj(p, nc, g, l, xsrc):
    T = g.T
    TB = min(T, 1024)
    NB = T // TB
    NT = TB // 128
    CW = 512
    chunks = [(i * 512, 512) for i in range(12)] + [(6144, 256)]
    s = Stage(p, nc)
    ident, _ = load_ident(s, g)
    A1 = s.sb("A1", [128, D])
    B1 = s.sb("B1", [128, D])
    hT = s.sb("hT", [128, 32, TB], BF16)
    xt = s.sb("xt", [128, D])
    ssq = s.sb("ssq", [128, 1])
    hb = [s.sb("hb%d" % i, [128, D], BF16) for i in range(2)]
    wb = [s.sb("wb%d" % i, [128, 32, CW], BF16) for i in range(2)]
    ob = [s.sb("ob%d" % i, [128, CW]) for i in range(2)]
    pst = [s.ps("pst%d" % i, [128, 512], BF16) for i in range(2)]
    pso = [s.ps("pso%d" % i, [128, 512]) for i in range(2)]
    p.dma(A1[:, :], bcast_rows(g.mod[l:l + 1, D:2 * D]), ["mod"], ["A1"])
    p.dma(xt[:, :], bcast_rows(g.w["norm1_g"][l]), [], ["xt"])
    p.dma(B1[:, :], bcast_rows(g.mod[l:l + 1, 0:D]), ["mod"], ["B1"])
    s.stt("dve", A1[:, :], A1[:, :], 1.0, xt[:, :], ALU.add, ALU.mult, ["A1", "xt"], ["A1"])
    wv = g.w["w_in"][l].rearrange("(kc q) n -> q kc n", q=128)
    ti = wi = oi = 0
    for nb in range(NB):
        for tt in range(NT):
            b = ti % 2
            ti += 1
            r0 = nb * TB + tt * 128
            p.dma(xt[:, :], xsrc[r0:r0 + 128, :], ["xsrc"], ["xt"])
            s.act(hb[b][:, :], xt[:, :], AF.Square, ["xt"], ["hb%d" % b, "ssq"], accum_out=ssq[:, :])
            s.rsqrt(ssq[:, :], "ssq", 1.0 / D, EPS)
            s.stt("dve", xt[:, :], xt[:, :], ssq[:, 0:1], A1[:, :], ALU.mult, ALU.mult, ["xt", "ssq", "A1"], ["xt"])
            s.tt("pool", hb[b][:, :], xt[:, :], B1[:, :], ALU.add, ["xt", "B1"], ["hb%d" % b])
            for q4 in range(8):
                pb = q4 % 2

                def tr(e, b=b, q4=q4, pb=pb):
                    ins = None
                    for j in range(4):
                        dk = q4 * 4 + j
                        ins = e.transpose(out=pst[pb][:, j * 128:(j + 1) * 128], in_=hb[b][:, dk * 128:(dk + 1) * 128], identity=ident[:, :])
                    return ins
                s.mm(tr, ["hb%d" % b, "ident"], ["pst%d" % pb])
                outap = hT[:, q4 * 4:(q4 + 1) * 4, tt * 128:(tt + 1) * 128]
                inap = pst[pb][:, :].rearrange("q (j t) -> q j t", j=4)
                s.cp("act" if q4 % 2 == 0 else "dve", outap, inap, ["pst%d" % pb], ["hT"])
        for (c0, cw) in chunks:
            b = wi % 2
            wi += 1
            p.dma(wb[b][:, :, 0:cw], wv[:, :, c0:c0 + cw], [], ["wb%d" % b], q="pool")
            fm = (c0 >= C_US and c0 < C_QA)
            if not fm:
                for tt in range(NT):
                    o_i = oi % 2
                    oi += 1

                    def mm(e, b=b, tt=tt, o_i=o_i, cw=cw):
                        ins = None
                        for kc in range(32):
                            ins = e.matmul(pso[o_i][:, 0:cw], lhsT=hT[:, kc, tt * 128:(tt + 1) * 128], rhs=wb[b][:, kc, 0:cw],
                                           start=(kc == 0), stop=(kc == 31))
                        return ins
                    s.mm(mm, ["hT", "wb%d" % b], ["pso%d" % o_i])
                    s.cp("act" if o_i == 0 else "dve", ob[o_i][:, 0:cw], pso[o_i][:, 0:cw], ["pso%d" % o_i], ["ob%d" % o_i])
                    r0 = nb * TB + tt * 128
                    p.dma(g.P[r0:r0 + 128, c0:c0 + cw], ob[o_i][:, 0:cw], ["ob%d" % o_i], ["P"], q="act")
            else:
                for cs4 in range(cw // 128):
                    for th in range(max(1, TB // 512)):
                        tw = min(512, TB)
                        o_i = oi % 2
                        oi += 1

                        def mm(e, b=b, cs4=cs4, th=th, o_i=o_i, tw=tw):
                            ins = None
                            for kc in range(32):
                                ins = e.matmul(pso[o_i][:, 0:tw], lhsT=wb[b][:, kc, cs4 * 128:(cs4 + 1) * 128],
                                               rhs=hT[:, kc, th * 512:th * 512 + tw], start=(kc == 0), stop=(kc == 31))
                            return ins
                        s.mm(mm, ["hT", "wb%d" % b], ["pso%d" % o_i])
                        s.cp("act" if o_i == 0 else "dve", ob[o_i][:, 0:tw], pso[o_i][:, 0:tw], ["pso%d" % o_i], ["ob%d" % o_i])
                        rr0 = c0 - C_US + cs4 * 128
                        t0 = nb * TB + th * 512
                        p.dma(g.PuT[rr0:rr0 + 128, t0:t0 + tw], ob[o_i][:, 0:tw], ["ob%d" % o_i], ["PuT"], q="act")
    s.done()


def final_norm(s, g, l, y, ykey, col0, r0, gain, sq, ss, ob, obkey):
    s.act(sq, y, AF.Square, [ykey], ["fn_sq", "fn_ss"], accum_out=ss)
    s.rsqrt(ss, "fn_ss", 1.0 / 1024, EPS)
    s.stt("dve", ob, y, ss[:, 0:1], gain, ALU.mult, ALU.mult, [ykey, "fn_ss", "fn_gain"], [obkey])
    s.p.dma(g.mix[r0:r0 + 128, col0:col0 + 1024], ob, [obkey], ["mix"])


def stage_ret(p, nc, g, l):
    T = g.T
    K = g.k
    gL = host_consts_cache()["ret_gL"]
    s = Stage(p, nc)
    ident, _ = load_ident(s, g)
    decT = s.sb("decT", [128, 4, 128])
    wqr = s.sb("wqr", [128, 4, 128])
    wkc = s.sb("wkc", [128, 4])
    gain = s.sb("gain", [128, 1024])
    p.dma(decT[:, :, :].rearrange("p a b -> p (a b)"), K["ret_decT"], [], ["decT"])
    p.dma(wqr[:, :, :].rearrange("p a b -> p (a b)"), K["ret_wq"], [], ["wqr"])
    p.dma(wkc[:, :], K["ret_wk"], [], ["wkc"])
    p.dma(gain[:, :], bcast_rows(g.w["out_norm_g"][l][:, 0:1024]), [], ["fn_gain"])
    pt = [s.sb("pt%d" % i, [128, 3072]) for i in range(2)]
    rot = [s.sb("rot%d" % i, [128, 192]) for i in range(2)]
    ta = s.sb("ta", [128, 8, 64])
    tb_ = s.sb("tb", [128, 8, 64])
    tc_ = s.sb("tc", [128, 8, 64])
    td_ = s.sb("td", [128, 8, 64])
    qkr = s.sb("qkr", [128, 8, 2, 64], BF16)
    qkT = s.sb("qkT", [128, 8, 128], BF16)
    qTw = s.sb("qTw", [128, 4, 128], BF16)
    kw = s.sb("kw", [128, 4, 128], BF16)
    vb = s.sb("vb", [128, 1024], BF16)
    SD = s.sb("SD", [128, 4, 128], BF16)
    R = s.sb("R", [128, 1024])
    Rb = s.sb("Rb", [128, 1024], BF16)
    stats = s.sb("stats", [128, 4, 6])
    mv = s.sb("mv", [128, 4, 2])
    rstd = s.sb("rstd", [128, 4])
    on = s.sb("on", [128, 1024])
    sg = s.sb("sg", [128, 1024])
    sq = s.sb("sq", [128, 1024], BF16)
    ss = s.sb("ss", [128, 1])
    ob = [s.sb("ob%d" % i, [128, 1024], BF16) for i in range(2)]
    psT = s.ps("psT", [128, 8, 128], BF16)
    psS = s.ps("psS", [128, 4, 128])
    psO = s.ps("psO", [128, 1024])
    psK = s.ps("psK", [128, 1024])
    p.add("dve", lambda e: e.memset(R[:, :], 0.0), [], ["R"])
    p.add("dve", lambda e: e.memset(Rb[:, :], 0.0), [], ["Rb"])
    for n in range(T // 128):
        b = n % 2
        r0 = n * 128
        PT, RT = "pt%d" % b, "rot%d" % b
        p.dma(pt[b][:, :], g.P[r0:r0 + 128, 0:3072], ["P"], [PT])
        p.dma(rot[b][:, :], g.rot[r0:r0 + 128, :], ["rot"], [RT])
        qk4 = pt[b][:, 0:1024].rearrange("p (h two d) -> p h two d", h=8, two=2)
        x1, x2 = qk4[:, :, 0, :], qk4[:, :, 1, :]
        sinb = rot[b][:, 0:64].unsqueeze(1).to_broadcast([128, 8, 64])
        cosb = rot[b][:, 96:160].unsqueeze(1).to_broadcast([128, 8, 64])
        s.tt("dve", ta[:, :, :], x1, cosb, ALU.mult, [PT, RT], ["ta"])
        s.tt("dve", tb_[:, :, :], x2, sinb, ALU.mult, [PT, RT], ["tb"])
        s.tt("dve", qkr[:, :, 0, :], ta[:, :, :], tb_[:, :, :], ALU.subtract, ["ta", "tb"], ["qkr"])
        s.tt("pool", tc_[:, :, :], x2, cosb, ALU.mult, [PT, RT], ["tc"])
        s.tt("pool", td_[:, :, :], x1, sinb, ALU.mult, [PT, RT], ["td"])
        s.tt("pool", qkr[:, :, 1, :], tc_[:, :, :], td_[:, :, :], ALU.add, ["tc", "td"], ["qkr"])
        qkf = qkr[:, :, :, :].rearrange("p h two d -> p h (two d)")

        def tr(e):
            ins = None
            for j in range(8):
                ins = e.transpose(out=psT[:, j, :], in_=qkf[:, j, :], identity=ident[:, :])
            return ins
        s.mm(tr, ["qkr", "ident"], ["psT"])
        s.cp("act", qkT[:, :, :], psT[:, :, :], ["psT"], ["qkT"])
        s.tt("dve", qTw[:, :, :], qkT[:, 0:4, :], wqr[:, :, :], ALU.mult, ["qkT", "wqr"], ["qTw"])
        s.tt("pool", kw[:, :, :], qkf[:, 4:8, :], wkc[:, :].unsqueeze(2).to_broadcast([128, 4, 128]), ALU.mult, ["qkr", "wkc"], ["kw"])
        s.cp("act", vb[:, :], pt[b][:, 1024:2048], [PT], ["vb"])

        def mmS(e):
            ins = None
            for h in range(4):
                ins = e.matmul(psS[:, h, :], lhsT=qkT[:, 4 + h, :], rhs=qkT[:, h, :], start=True, stop=True)
            return ins
        s.mm(mmS, ["qkT"], ["psS"])
        s.tt("dve", SD[:, :, :], psS[:, :, :], decT[:, :, :], ALU.mult, ["psS", "decT"], ["SD"])

        def mmO(e):
            ins = None
            for h in range(4):
                e.matmul(psO[:, h * 256:(h + 1) * 256], lhsT=SD[:, h, :], rhs=vb[:, h * 256:(h + 1) * 256], start=True, stop=False)
                ins = e.matmul(psO[:, h * 256:(h + 1) * 256], lhsT=qTw[:, h, :], rhs=Rb[:, h * 256:(h + 1) * 256], start=False, stop=True)
            return ins
        s.mm(mmO, ["SD", "vb", "qTw", "Rb"], ["psO"])

        def mmK(e):
            ins = None
            for h in range(4):
                ins = e.matmul(psK[:, h * 256:(h + 1) * 256], lhsT=kw[:, h, :], rhs=vb[:, h * 256:(h + 1) * 256], start=True, stop=True)
            return ins
        s.mm(mmK, ["kw", "vb"], ["psK"])
        for h in range(4):
            sl = slice(h * 256, (h + 1) * 256)
            s.stt("dve", R[:, sl], R[:, sl], gL[h], psK[:, sl], ALU.mult, ALU.add, ["R", "psK"], ["R"])
        s.cp("pool", Rb[:, :], R[:, :], ["R"], ["Rb"])
        for h in range(4):
            p.add("dve", lambda e, h=h: e.bn_stats(out=stats[:, h, :], in_=psO[:, h * 256:(h + 1) * 256]), ["psO"], ["stats"])
        for h in range(4):
            p.add("dve", lambda e, h=h: e.bn_aggr(out=mv[:, h, :], in_=stats[:, h, :]), ["stats"], ["mv"])
        s.cp("dve", rstd[:, :], mv[:, :, 1], ["mv"], ["rstd"])
        s.rsqrt(rstd[:, :], "rstd", 1.0, EPS)
        for h in range(4):
            sl = slice(h * 256, (h + 1) * 256)
            s.ts("dve", on[:, sl], psO[:, sl], mv[:, h, 0:1], rstd[:, h:h + 1], ALU.subtract, ALU.mult, ["psO", "mv", "rstd"], ["on"])
        s.act(sg[:, :], pt[b][:, 2048:3072], AF.Silu, [PT], ["sg"])
        s.tt("pool", on[:, :], on[:, :], sg[:, :], ALU.mult, ["on", "sg"], ["on"])
        final_norm(s, g, l, on[:, :], "on", 0, r0, gain[:, :], sq[:, :], ss[:, :], ob[b][:, :], "ob%d" % b)
    s.done()


_HC = None


def host_consts_cache():
    global _HC
    if _HC is None:
        _HC = host_consts()
    return _HC


def stage_swa(p, nc, g, l):
    T = g.T
    K = g.k
    s = Stage(p, nc)
    ident, _ = load_ident(s, g)
    mtmp = s.sb("mtmp", [128, 2, 128])
    msk = s.sb("msk", [128, 2, 128], BF16)
    p.dma(mtmp[:, 0, :], K["swa_mprev"], [], ["mtmp"])
    p.dma(mtmp[:, 1, :], K["swa_mcur"], [], ["mtmp"])
    s.cp("dve", msk[:, :, :], mtmp[:, :, :], ["mtmp"], ["msk"])
    gq = s.sb("gq", [128, 64])
    gk = s.sb("gk", [128, 64])
    p.dma(gq[:, :], bcast_rows(g.w["attn_q_norm"][l]), [], ["gq"])
    p.dma(gk[:, :], bcast_rows(g.w["attn_k_norm"][l]), [], ["gk"])
    s.ts("dve", gq[:, :], gq[:, :], 0.125, None, ALU.mult, None, ["gq"], ["gq"])
    esink = s.sb("esink", [128, 16])
    p.dma(esink[:, :], bcast_rows(g.w["attn_sinks"][l]), [], ["esink"])
    s.act(esink[:, :], esink[:, :], AF.Exp, ["esink"], ["esink"])
    gain = s.sb("gain", [128, 1024])
    p.dma(gain[:, :], bcast_rows(g.w["out_norm_g"][l][:, 2048:3072]), [], ["fn_gain"])
    pt = [s.sb("pt%d" % i, [128, 1280]) for i in range(2)]
    rot = [s.sb("rot%d" % i, [128, 192]) for i in range(2)]
    sqv = s.sb("sqv", [128, 18, 64])
    ssn = s.sb("ssn", [128, 18])
    xn = s.sb("xn", [128, 18, 2, 32])
    ta = s.sb("ta", [128, 18, 32])
    tb_ = s.sb("tb", [128, 18, 32])
    tc_ = s.sb("tc", [128, 18, 32])
    td_ = s.sb("td", [128, 18, 32])
    qkr = s.sb("qkr", [128, 18, 2, 32], BF16)
    kd = s.sb("kd", [128, 2, 2, 64], BF16)
    qT = s.sb("qT", [128, 8, 128], BF16)
    kT = [s.sb("kT%d" % i, [128, 2, 128], BF16) for i in range(2)]
    vb1 = [s.sb("vb1%d" % i, [128, 2, 65], BF16) for i in range(2)]
    PTb = [s.sb("PTb%d" % i, [128, 2, 4, 128], BF16) for i in range(2)]
    den = s.sb("den", [128, 16])
    on = s.sb("on", [128, 16, 64])
    sq = s.sb("sq", [128, 1024], BF16)
    ss = s.sb("ss", [128, 1])
    ob = [s.sb("ob%d" % i, [128, 1024], BF16) for i in range(2)]
    psTq = s.ps("psTq", [128, 8, 128], BF16)
    psTk = s.ps("psTk", [128, 2, 128], BF16)
    psS = [s.ps("psS%d" % i, [128, 4, 128]) for i in range(2)]
    psO = s.ps("psO", [128, 16, 128])
    for i in range(2):
        p.add("dve", lambda e, i=i: e.memset(vb1[i][:, :, 64:65], 1.0), [], ["vb1%d" % i])
    for n in range(T // 128):
        b = n % 2
        pb = 1 - b
        r0 = n * 128
        PT, RT = "pt%d" % b, "rot%d" % b
        p.dma(pt[b][:, :], g.P[r0:r0 + 128, C_QA:C_QA + 1280], ["P"], [PT])
        p.dma(rot[b][:, :], g.rot[r0:r0 + 128, :], ["rot"], [RT])
        x3 = pt[b][:, 0:1152].rearrange("p (h d) -> p h d", h=18)
        s.act(sqv[:, :, :], x3, AF.Square, [PT], ["sqv"])
        p.add("dve", lambda e: e.tensor_reduce(out=ssn[:, :], in_=sqv[:, :, :], axis=AX.X, op=ALU.add), ["sqv"], ["ssn"])
        s.rsqrt(ssn[:, :], "ssn", 1.0 / 64, EPS)
        xn3 = xn[:, :, :, :].rearrange("p h two d -> p h (two d)")
        s.tt("dve", xn3, ssn[:, :].unsqueeze(2).to_broadcast([128, 18, 64]), x3, ALU.mult, ["ssn", PT], ["xn"])
        s.tt("dve", xn3[:, 0:16, :], xn3[:, 0:16, :], gq[:, :].unsqueeze(1).to_broadcast([128, 16, 64]), ALU.mult, ["xn", "gq"], ["xn"])
        s.tt("dve", xn3[:, 16:18, :], xn3[:, 16:18, :], gk[:, :].unsqueeze(1).to_broadcast([128, 2, 64]), ALU.mult, ["xn", "gk"], ["xn"])
        x1, x2 = xn[:, :, 0, :], xn[:, :, 1, :]
        sinb = rot[b][:, 64:96].unsqueeze(1).to_broadcast([128, 18, 32])
        cosb = rot[b][:, 160:192].unsqueeze(1).to_broadcast([128, 18, 32])
        s.tt("dve", ta[:, :, :], x1, cosb, ALU.mult, ["xn", RT], ["ta"])
        s.tt("dve", tb_[:, :, :], x2, sinb, ALU.mult, ["xn", RT], ["tb"])
        s.tt("dve", qkr[:, :, 0, :], ta[:, :, :], tb_[:, :, :], ALU.subtract, ["ta", "tb"], ["qkr"])
        s.tt("pool", tc_[:, :, :], x2, cosb, ALU.mult, ["xn", RT], ["tc"])
        s.tt("pool", td_[:, :, :], x1, sinb, ALU.mult, ["xn", RT], ["td"])
        s.tt("pool", qkr[:, :, 1, :], tc_[:, :, :], td_[:, :, :], ALU.add, ["tc", "td"], ["qkr"])
        qkf = qkr[:, :, :, :].rearrange("p h two d -> p h (two d)")
        for cpy in range(2):
            s.cp("pool", kd[:, :, cpy, :], qkf[:, 16:18, :], ["qkr"], ["kd"])
        qf = qkr[:, 0:16, :, :].rearrange("p (j e) two d -> p j (e two d)", e=2)
        kdf = kd[:, :, :, :].rearrange("p h c d -> p h (c d)")

        def tr(e, b=b):
            ins = None
            for j in range(8):
                ins = e.transpose(out=psTq[:, j, :], in_=qf[:, j, :], identity=ident[:, :])
            for hk in range(2):
                ins = e.transpose(out=psTk[:, hk, :], in_=kdf[:, hk, :], identity=ident[:, :])
            return ins
        s.mm(tr, ["qkr", "kd", "ident"], ["psTq", "psTk"])
        s.cp("act", qT[:, :, :], psTq[:, :, :], ["psTq"], ["qT"])
        s.cp("dve", kT[b][:, :, :], psTk[:, :, :], ["psTk"], ["kT%d" % b])
        s.cp("act", vb1[b][:, :, 0:64], pt[b][:, 1152:1280].rearrange("p (h d) -> p h d", h=2), [PT], ["vb1%d" % b])
        blocks = ([(pb, 0)] if n > 0 else []) + [(b, 1)]
        for hk in range(2):
            for bi, (kb, mi) in enumerate(blocks):
                PTk = "PTb%d" % mi
                for e_ in range(2):
                    def mmS(e, e_=e_, kb=kb, hk=hk):
                        return e.matmul(psS[e_][:, :, :], lhsT=kT[kb][e_ * 64:(e_ + 1) * 64, hk, :],
                                        rhs=qT[e_ * 64:(e_ + 1) * 64, hk * 4:(hk + 1) * 4, :], start=True, stop=True)
                    s.mm(mmS, ["kT%d" % kb, "qT"], ["psS%d" % e_])
                    s.act(PTb[mi][:, e_, :, :], psS[e_][:, :, :], AF.Exp, ["psS%d" % e_], [PTk])
                for e_ in range(2):
                    s.tt("dve" if e_ == 0 else "pool", PTb[mi][:, e_, :, :], PTb[mi][:, e_, :, :],
                         msk[:, mi, :].unsqueeze(1).to_broadcast([128, 4, 128]), ALU.mult, [PTk, "msk"], [PTk])

            def mmO(e, hk=hk, blocks=blocks):
                ins = None
                for pair in range(4):
                    for e_ in range(2):
                        hq = hk * 8 + pair * 2 + e_
                        for bi, (kb, mi) in enumerate(blocks):
                            ins = e.matmul(psO[:, hq, 0:65], lhsT=PTb[mi][:, e_, pair, :], rhs=vb1[kb][:, hk, :],
                                           start=(bi == 0), stop=(bi == len(blocks) - 1))
                return ins
            s.mm(mmO, ["PTb0", "PTb1", "vb10", "vb11"], ["psO"])
        s.tt("dve", den[:, :], psO[:, :, 64], esink[:, :], ALU.add, ["psO", "esink"], ["den"])
        p.add("dve", lambda e: e.reciprocal(out=den[:, :], in_=den[:, :]), ["den"], ["den"])
        s.tt("dve", on[:, :, :], den[:, :].unsqueeze(2).to_broadcast([128, 16, 64]), psO[:, :, 0:64], ALU.mult, ["den", "psO"], ["on"])
        onf = on[:, :, :].rearrange("p h d -> p (h d)")
        final_norm(s, g, l, onf, "on", 2048, r0, gain[:, :], sq[:, :], ss[:, :], ob[b][:, :], "ob%d" % b)
    s.done()


def stage_pool(p, nc, g, l):
    T = g.T
    K = g.k
    s = Stage(p, nc)
    mt = s.sb("mt", [128, 3, 512])
    M = s.sb("M", [128, 3, 4, 128], BF16)
    for i, nm in enumerate(("pool_mcur", "pool_mcur0", "pool_mprev")):
        p.dma(mt[:, i, :], K[nm], [], ["mt"])
    s.cp("dve", M[:, :, :, :].rearrange("p a g t -> p a (g t)"), mt[:, :, :], ["mt"], ["M"])
    icnt = s.sb("icnt", [128, 2, 4, 128])
    p.dma(icnt[:, 0, :, :].rearrange("p g t -> p (g t)"), K["pool_icnt"], [], ["icnt"])
    p.dma(icnt[:, 1, :, :].rearrange("p g t -> p (g t)"), K["pool_icnt0"], [], ["icnt"])
    wp = s.sb("wp", [128, 4, 2, 256], BF16)
    for gi in range(4):
        p.dma(wp[:, gi, :, :], g.w["pool_w"][l][gi].rearrange("(cc c) d -> c cc d", c=128), [], ["wp"], q="pool")
    psc = s.sb("psc", [128, 1024])
    p.dma(psc[:, :], bcast_rows(g.w["pool_scale"][l]), [], ["psc"])
    gain = s.sb("gain", [128, 1024])
    p.dma(gain[:, :], bcast_rows(g.w["out_norm_g"][l][:, 3072:4096]), [], ["fn_gain"])
    ut = [s.sb("ut%d" % i, [128, 1024]) for i in range(2)]
    ub = [s.sb("ub%d" % i, [128, 1024], BF16) for i in range(2)]
    pT = s.sb("pT", [128, 4, 2, 128], BF16)
    yv = s.sb("yv", [128, 1024])
    sq = s.sb("sq", [128, 1024], BF16)
    ss = s.sb("ss", [128, 1])
    ob = [s.sb("ob%d" % i, [128, 1024], BF16) for i in range(2)]
    psP = s.ps("psP", [128, 4, 2, 128])
    psY = s.ps("psY", [128, 1024])
    for n in range(T // 128):
        b = n % 2
        pb = 1 - b
        r0 = n * 128
        p.dma(ut[b][:, :], g.P[r0:r0 + 128, C_UP:C_UP + 1024], ["P"], ["ut%d" % b])
        s.cp("act", ub[b][:, :], ut[b][:, :], ["ut%d" % b], ["ub%d" % b])
        mi = 1 if n == 0 else 0

        def mmP(e, b=b, pb=pb, n=n, mi=mi):
            ins = None
            for gi in range(4):
                for cc in range(2):
                    ch = gi * 2 + cc
                    ins = e.matmul(psP[:, gi, cc, :], lhsT=ub[b][:, ch * 128:(ch + 1) * 128], rhs=M[:, mi, gi, :], start=True, stop=(n == 0))
                    if n > 0:
                        ins = e.matmul(psP[:, gi, cc, :], lhsT=ub[pb][:, ch * 128:(ch + 1) * 128], rhs=M[:, 2, gi, :], start=False, stop=True)
            return ins
        s.mm(mmP, ["ub0", "ub1", "M"], ["psP"])
        for cc in range(2):
            s.tt("dve" if cc == 0 else "pool" if False else "dve", pT[:, :, cc, :], psP[:, :, cc, :], icnt[:, mi, :, :], ALU.mult, ["psP", "icnt"], ["pT"])

        def mmY(e):
            ins = None
            for gi in range(4):
                for cc in range(2):
                    ins = e.matmul(psY[:, gi * 256:(gi + 1) * 256], lhsT=pT[:, gi, cc, :], rhs=wp[:, gi, cc, :], start=(cc == 0), stop=(cc == 1))
            return ins
        s.mm(mmY, ["pT", "wp"], ["psY"])
        s.tt("dve", yv[:, :], psY[:, :], psc[:, :], ALU.mult, ["psY", "psc"], ["yv"])
        final_norm(s, g, l, yv[:, :], "yv", 3072, r0, gain[:, :], sq[:, :], ss[:, :], ob[b][:, :], "ob%d" % b)
    s.done()


def stage_ssm_scan(p, nc, g, l):
    T = g.T
    K = g.k
    W = g.w
    s = Stage(p, nc)
    identb, identf = load_ident(s, g)
    PI2 = float(np.pi / 2)
    A3 = s.sb("A3", [32, 3, 128])
    ld = s.sb("ld", [32, 2])
    p.dma(A3[:, 0, :], W["ssm_a_re"][l].rearrange("(gp g2) q -> gp (g2 q)", g2=2), [], ["A3"])
    p.dma(A3[:, 1, :], W["ssm_a_im"][l].rearrange("(gp g2) q -> gp (g2 q)", g2=2), [], ["A3"])
    p.dma(ld[:, :], W["ssm_log_dt"][l].rearrange("o (gp g2) -> (o gp) g2", g2=2), [], ["ld"])
    s.cp("dve", A3[:, 2, :].rearrange("a (g2 q) -> a g2 q", g2=2), ld[:, :].unsqueeze(2).to_broadcast([32, 2, 64]), ["ld", "A3"], ["A3"])
    psPr = s.ps("psPr", [128, 512])
    psPb = s.ps("psPb", [128, 8, 128], BF16)

    def trA(e):
        ins = None
        for a in range(3):
            ins = e.transpose(out=psPr[:, a * 32:(a + 1) * 32], in_=A3[:, a, :], identity=identf[0:32, 0:32])
        return ins
    s.mm(trA, ["A3", "identf"], ["psPr"])
    prm = s.sb("prm", [128, 3, 32])
    s.cp("dve", prm[:, :, :].rearrange("k a g -> k (a g)"), psPr[:, 0:96], ["psPr"], ["prm"])
    ar, ai = prm[:, 0, :], prm[:, 1, :]
    sm = s.sb("sm", [128, 16, 32])
    smi = s.sb("smi", [128, 32], I32)
    dtk, lam, th, mag, rr_, cs_, sn_, abre, abim, den, nr, fre, fim, tmpa, tmpb, th2 = [sm[:, i, :] for i in range(16)]
    KS = ["sm%d" % i for i in range(16)]
    s.act(dtk, prm[:, 2, :], AF.Exp, ["prm"], [KS[0]])
    s.tt("dve", lam, ar, dtk, ALU.mult, ["prm", KS[0]], [KS[1]])
    s.tt("dve", th, ai, dtk, ALU.mult, ["prm", KS[0]], [KS[2]])
    s.act(mag, lam, AF.Exp, [KS[1]], [KS[3]])
    s.range_reduce(rr_, th, smi[:, :], tmpa, KS[4], KS[2], "smi", KS[13])
    s.act(sn_, rr_, AF.Sin, [KS[4]], [KS[6]])
    s.ts("dve", th2, th, PI2, None, ALU.add, None, [KS[2]], [KS[15]])
    s.range_reduce(rr_, th2, smi[:, :], tmpa, KS[4], KS[15], "smi", KS[13])
    s.act(cs_, rr_, AF.Sin, [KS[4]], [KS[5]])
    s.tt("dve", abre, mag, cs_, ALU.mult, [KS[3], KS[5]], [KS[7]])
    s.tt("dve", abim, mag, sn_, ALU.mult, [KS[3], KS[6]], [KS[8]])
    s.tt("dve", den, ar, ar, ALU.mult, ["prm"], [KS[9]])
    s.tt("dve", tmpa, ai, ai, ALU.mult, ["prm"], [KS[13]])
    s.tt("dve", den, den, tmpa, ALU.add, [KS[9], KS[13]], [KS[9]])
    p.add("dve", lambda e: e.reciprocal(out=den, in_=den), [KS[9]], [KS[9]])
    s.ts("dve", nr, abre, -1.0, None, ALU.add, None, [KS[7]], [KS[10]])
    s.tt("dve", tmpa, nr, ar, ALU.mult, [KS[10], "prm"], [KS[13]])
    s.tt("dve", tmpb, abim, ai, ALU.mult, [KS[8], "prm"], [KS[14]])
    s.tt("dve", fre, tmpa, tmpb, ALU.add, [KS[13], KS[14]], [KS[11]])
    s.tt("dve", fre, fre, den, ALU.mult, [KS[11], KS[9]], [KS[11]])
    s.tt("dve", tmpa, abim, ar, ALU.mult, [KS[8], "prm"], [KS[13]])
    s.tt("dve", tmpb, nr, ai, ALU.mult, [KS[10], "prm"], [KS[14]])
    s.tt("dve", fim, tmpa, tmpb, ALU.subtract, [KS[13], KS[14]], [KS[12]])
    s.tt("dve", fim, fim, den, ALU.mult, [KS[12], KS[9]], [KS[12]])
    bre = s.sb("bre", [128, 32, 16])
    bim = s.sb("bim", [128, 32, 16])
    bvr = W["ssm_b_re"][l].rearrange("(gp g2) q c -> gp (g2 q) c", g2=2).rearrange("gp k c -> k gp c")
    bvi = W["ssm_b_im"][l].rearrange("(gp g2) q c -> gp (g2 q) c", g2=2).rearrange("gp k c -> k gp c")
    for q in range(4):
        p.dma(bre[:, q * 8:(q + 1) * 8, :], bvr[:, q * 8:(q + 1) * 8, :], [], ["bre"])
        p.dma(bim[:, q * 8:(q + 1) * 8, :], bvi[:, q * 8:(q + 1) * 8, :], [], ["bim"])
    w1 = s.sb("w1", [128, 32, 16])
    w2 = s.sb("w2", [128, 32, 16])
    bbre = s.sb("bbre", [128, 32, 16])
    bbim = s.sb("bbim", [128, 32, 16])
    freb = fre.unsqueeze(2).to_broadcast([128, 32, 16])
    fimb = fim.unsqueeze(2).to_broadcast([128, 32, 16])
    s.tt("dve", w1[:, :, :], freb, bre[:, :, :], ALU.mult, [KS[11], "bre"], ["w1"])
    s.tt("dve", w2[:, :, :], fimb, bim[:, :, :], ALU.mult, [KS[12], "bim"], ["w2"])
    s.tt("dve", bbre[:, :, :], w1[:, :, :], w2[:, :, :], ALU.subtract, ["w1", "w2"], ["bbre"])
    s.tt("dve", w1[:, :, :], freb, bim[:, :, :], ALU.mult, [KS[11], "bim"], ["w1"])
    s.tt("dve", w2[:, :, :], fimb, bre[:, :, :], ALU.mult, [KS[12], "bre"], ["w2"])
    s.tt("dve", bbim[:, :, :], w1[:, :, :], w2[:, :, :], ALU.add, ["w1", "w2"], ["bbim"])
    BT = []
    for nm, bb in (("re", bbre), ("im", bbim)):
        Zp = s.sb("Zp" + nm, [128, 32, 128], BF16)
        p.add("pool", lambda e, Zp=Zp: e.memset(Zp[:, :, :], 0.0), [], ["Zp" + nm])
        Zp5 = Zp[:, :, :].rearrange("k (i j) c -> k i j c", j=4)
        bb4 = bb[:, :, :].rearrange("k (i j) c -> k i j c", j=4)
        for j in range(4):
            for g2 in range(2):
                lo = j * 32 + g2 * 16
                s.cp("dve", Zp5[g2 * 64:(g2 + 1) * 64, :, j, lo:lo + 16], bb4[g2 * 64:(g2 + 1) * 64, :, j, :], ["bb" + nm, "Zp" + nm], ["Zp" + nm])
        BTt = s.sb("BT" + nm, [128, 32, 128], BF16)
        for q in range(4):
            def trB(e, q=q, Zp=Zp):
                ins = None
                for a in range(8):
                    ins = e.transpose(out=psPb[:, a, :], in_=Zp[:, q * 8 + a, :], identity=identb[:, :])
                return ins
            s.mm(trB, ["Zp" + nm, "ident"], ["psPb"])
            s.cp("act", BTt[:, q * 8:(q + 1) * 8, :], psPb[:, :, :], ["psPb"], ["BT" + nm])
        BT.append(BTt)
    BTre, BTim = BT
    Cp = []
    for nm, key, sgn in (("re", "ssm_c_re", 1.0), ("im", "ssm_c_im", -1.0)):
        Xp = s.sb("Xp" + nm, [128, 8, 128])
        p.add("pool", lambda e, Xp=Xp: e.memset(Xp[:, :, :], 0.0), [], ["Xp" + nm])
        cv = W[key][l].rearrange("(i j g2) c q -> j g2 c i q", j=4, g2=2)
        for j in range(4):
            for g2 in range(2):
                lo = j * 32 + g2 * 16
                p.dma(Xp[lo:lo + 16, :, g2 * 64:(g2 + 1) * 64], cv[j, g2], ["Xp" + nm], ["Xp" + nm])
        XT = s.sb("XT" + nm, [128, 8, 128])
        for q in range(2):
            def trC(e, q=q, Xp=Xp):
                ins = None
                for a in range(4):
                    ins = e.transpose(out=psPr[:, a * 128:(a + 1) * 128], in_=Xp[:, q * 4 + a, :], identity=identf[:, :])
                return ins
            s.mm(trC, ["Xp" + nm, "identf"], ["psPr"])
            s.cp("dve", XT[:, q * 4:(q + 1) * 4, :], psPr[:, :].rearrange("k (a c) -> k a c", a=4), ["psPr"], ["XT" + nm])
        Cpt = s.sb("Cp" + nm, [128, 32, 128], BF16)
        p.add("pool", lambda e, Cpt=Cpt: e.memset(Cpt[:, :, :], 0.0), [], ["Cp" + nm])
        Cp5 = Cpt[:, :, :].rearrange("k (i j) c -> k i j c", j=4)
        for j in range(4):
            s.ts("dve", Cp5[:, :, j, j * 32:(j + 1) * 32], XT[:, :, j * 32:(j + 1) * 32], sgn, None, ALU.mult, None, ["XT" + nm, "Cp" + nm], ["Cp" + nm])
        Cp.append(Cpt)
    Cpre, Cpim = Cp
    dv = s.sb("dv", [8, 128])
    p.dma(dv[:, :], W["ssm_d"][l].rearrange("o (i k) -> (o i) k", k=128), [], ["dv"])
    s.mm(lambda e: e.transpose(out=psPr[:, 0:8], in_=dv[:, :], identity=identf[0:8, 0:8]), ["dv", "identf"], ["psPr"])
    dcol = s.sb("dcol", [128, 8])
    s.cp("dve", dcol[:, :], psPr[:, 0:8], ["psPr"], ["dcol"])
    iot = s.sb("iot", [128, 512])
    p.dma(iot[:, :], K["iota512"], [], ["iot"])
    tabC = [s.sb("tabC%d" % j, [128, 512]) for j in range(4)]
    tabS = [s.sb("tabS%d" % j, [128, 512]) for j in range(4)]
    rco = [s.sb("rco%d" % j, [128, 512]) for j in range(4)]
    ang = s.sb("ang", [128, 512])
    ang2 = s.sb("ang2", [128, 512])
    angr = s.sb("angr", [128, 512])
    aki = s.sb("aki", [128, 512], I32)
    akf = s.sb("akf", [128, 512])
    car_re = s.sb("car_re", [128, 32])
    car_im = s.sb("car_im", [128, 32])
    ctmp = s.sb("ctmp", [128, 2])
    p.add("dve", lambda e: e.memset(car_re[:, :], 0.0), [], ["car"])
    p.add("dve", lambda e: e.memset(car_im[:, :], 0.0), [], ["car"])
    uTf = [s.sb("uTf%d" % i, [128, 512]) for i in range(2)]
    uTb = s.sb("uTb", [128, 512], BF16)
    xrs = [s.sb("xr%d" % i, [128, 512]) for i in range(2)]
    xis = [s.sb("xi%d" % i, [128, 512]) for i in range(2)]
    t1, t2, t3, t4 = [s.sb("t%d" % i, [128, 512]) for i in range(1, 5)]
    gri = s.sb("gri", [128, 512])
    gii = s.sb("gii", [128, 512])
    Gr = s.sb("Gr", [128, 512])
    Gi = s.sb("Gi", [128, 512])
    Hrs = [s.sb("Hr%d" % i, [128, 512], BF16) for i in range(2)]
    His = [s.sb("Hi%d" % i, [128, 512], BF16) for i in range(2)]
    yf = s.sb("yf", [128, 512])
    y2 = [s.sb("y2%d" % i, [128, 512]) for i in range(2)]
    psXrs = [s.ps("psXr%d" % i, [128, 512]) for i in range(2)]
    psXis = [s.ps("psXi%d" % i, [128, 512]) for i in range(2)]
    pc = 0
    psY = s.ps("psY", [128, 512])
    NTB = T // 512
    it = 0
    for i in range(8):
        for j in range(4):
            gp = 4 * i + j
            s.ts("dve", ang[:, :], iot[:, :], th[:, gp:gp + 1], None, ALU.mult, None, ["iot", KS[2]], ["ang"])
            s.range_reduce(angr[:, :], ang[:, :], aki[:, :], akf[:, :], "angr", "ang", "aki", "akf")
            s.act(tabS[j][:, :], angr[:, :], AF.Sin, ["angr"], ["tabS%d" % j])
            s.ts("dve", ang2[:, :], ang[:, :], PI2, None, ALU.add, None, ["ang"], ["ang2"])
            s.range_reduce(angr[:, :], ang2[:, :], aki[:, :], akf[:, :], "angr", "ang2", "aki", "akf")
            s.act(tabC[j][:, :], angr[:, :], AF.Sin, ["angr"], ["tabC%d" % j])
            s.cp("pool", rco[j][:, :], mag[:, gp:gp + 1].to_broadcast([128, 512]), [KS[3]], ["rco%d" % j])
        for tb in range(NTB):
            t0 = tb * 512
            ub = it % 2
            it += 1
            p.dma(uTf[ub][:, :], g.PuT[i * 128:(i + 1) * 128, t0:t0 + 512], ["PuT"], ["uTf%d" % ub])
            s.cp("act", uTb[:, :], uTf[ub][:, :], ["uTf%d" % ub], ["uTb"])
            for j in range(4):
                gp = 4 * i + j
                TC, TS = "tabC%d" % j, "tabS%d" % j
                kb = pc % 2
                pc += 1
                xr, xi, Hr, Hi, psXr, psXi = xrs[kb], xis[kb], Hrs[kb], His[kb], psXrs[kb], psXis[kb]
                XR, XI, HR, HI, PXR, PXI = "xr%d" % kb, "xi%d" % kb, "Hr%d" % kb, "Hi%d" % kb, "psXr%d" % kb, "psXi%d" % kb
                s.mm(lambda e, gp=gp, psXr=psXr: e.matmul(psXr[:, :], lhsT=BTre[:, gp, :], rhs=uTb[:, :], start=True, stop=True), ["BTre", "uTb"], [PXR])
                s.mm(lambda e, gp=gp, psXi=psXi: e.matmul(psXi[:, :], lhsT=BTim[:, gp, :], rhs=uTb[:, :], start=True, stop=True), ["BTim", "uTb"], [PXI])
                s.cp("act", xr[:, :], psXr[:, :], [PXR], [XR])
                s.cp("act", xi[:, :], psXi[:, :], [PXI], [XI])
                s.tt("dve", t1[:, :], xr[:, :], tabC[j][:, :], ALU.mult, [XR, TC], ["t1"])
                s.tt("dve", t2[:, :], xi[:, :], tabS[j][:, :], ALU.mult, [XI, TS], ["t2"])
                s.tt("dve", gri[:, :], t1[:, :], t2[:, :], ALU.add, ["t1", "t2"], ["gri"])
                s.tt("pool", t3[:, :], xi[:, :], tabC[j][:, :], ALU.mult, [XI, TC], ["t3"])
                s.tt("pool", t4[:, :], xr[:, :], tabS[j][:, :], ALU.mult, [XR, TS], ["t4"])
                s.tt("pool", gii[:, :], t3[:, :], t4[:, :], ALU.subtract, ["t3", "t4"], ["gii"])
                p.add("dve", lambda e, j=j, gp=gp: e.tensor_tensor_scan(out=Gr[:, :], data0=rco[j][:, :], data1=gri[:, :],
                                                                       initial=car_re[:, gp:gp + 1], op0=ALU.mult, op1=ALU.add),
                      ["rco%d" % j, "gri", "car"], ["Gr"])
                p.add("dve", lambda e, j=j, gp=gp: e.tensor_tensor_scan(out=Gi[:, :], data0=rco[j][:, :], data1=gii[:, :],
                                                                       initial=car_im[:, gp:gp + 1], op0=ALU.mult, op1=ALU.add),
                      ["rco%d" % j, "gii", "car"], ["Gi"])
                s.tt("dve", ctmp[:, 0:1], Gi[:, 511:512], tabS[j][:, 511:512], ALU.mult, ["Gi", TS], ["ctmp"])
                s.tt("dve", ctmp[:, 1:2], Gr[:, 511:512], tabS[j][:, 511:512], ALU.mult, ["Gr", TS], ["ctmp"])
                s.stt("dve", car_re[:, gp:gp + 1], Gr[:, 511:512], tabC[j][:, 511:512], ctmp[:, 0:1], ALU.mult, ALU.subtract, ["Gr", TC, "ctmp", "car"], ["car"])
                s.stt("dve", car_im[:, gp:gp + 1], Gi[:, 511:512], tabC[j][:, 511:512], ctmp[:, 1:2], ALU.mult, ALU.add, ["Gi", TC, "ctmp", "car"], ["car"])
                s.tt("dve", t1[:, :], Gr[:, :], tabC[j][:, :], ALU.mult, ["Gr", TC], ["t1"])
                s.tt("dve", t2[:, :], Gi[:, :], tabS[j][:, :], ALU.mult, ["Gi", TS], ["t2"])
                s.tt("dve", Hr[:, :], t1[:, :], t2[:, :], ALU.subtract, ["t1", "t2"], [HR])
                s.tt("pool", t3[:, :], Gi[:, :], tabC[j][:, :], ALU.mult, ["Gi", TC], ["t3"])
                s.tt("pool", t4[:, :], Gr[:, :], tabS[j][:, :], ALU.mult, ["Gr", TS], ["t4"])
                s.tt("pool", Hi[:, :], t3[:, :], t4[:, :], ALU.add, ["t3", "t4"], [HI])
                s.mm(lambda e, gp=gp, j=j, Hr=Hr: e.matmul(psY[:, :], lhsT=Cpre[:, gp, :], rhs=Hr[:, :], start=(j == 0), stop=False), ["Cpre", HR], ["psY"])
                s.mm(lambda e, gp=gp, j=j, Hi=Hi: e.matmul(psY[:, :], lhsT=Cpim[:, gp, :], rhs=Hi[:, :], start=False, stop=(j == 3)), ["Cpim", HI], ["psY"])
            s.stt("dve", yf[:, :], uTf[ub][:, :], dcol[:, i:i + 1], psY[:, :], ALU.mult, ALU.add, ["uTf%d" % ub, "dcol", "psY"], ["yf"])
            s.act(y2[ub][:, :], yf[:, :], AF.Gelu, ["yf"], ["y2%d" % ub])
            p.dma(g.y2T[i * 128:(i + 1) * 128, t0:t0 + 512], y2[ub][:, :], ["y2%d" % ub], ["y2T"])
    s.done()


def stage_ssm_glu(p, nc, g, l):
    T = g.T
    W = g.w
    s = Stage(p, nc)
    identb, identf = load_ident(s, g)
    wglu = s.sb("wglu", [128, 8, 1024], BF16)
    wv = W["ssm_w_glu"][l].rearrange("(ct c) n -> c ct n", c=128)
    for q in range(4):
        p.dma(wglu[:, q * 2:(q + 1) * 2, :], wv[:, q * 2:(q + 1) * 2, :], [], ["wglu"], q="pool")
    bv = s.sb("bv", [8, 128])
    p.dma(bv[:, :], W["ssm_b_glu"][l].rearrange("o (i k) -> (o i) k", k=128), [], ["bv"])
    psR = s.ps("psR", [128, 4, 128])
    psZ = [s.ps("psZ%d" % i, [128, 512]) for i in range(2)]
    s.mm(lambda e: e.transpose(out=psR[:, 0, 0:8], in_=bv[:, :], identity=identf[0:8, 0:8]), ["bv", "identf"], ["psR"])
    bcol = s.sb("bcol", [128, 8])
    s.cp("dve", bcol[:, :], psR[:, 0, 0:8], ["psR"], ["bcol"])
    gain = s.sb("gain", [128, 1024])
    p.dma(gain[:, :], bcast_rows(W["out_norm_g"][l][:, 1024:2048]), [], ["fn_gain"])
    y2f = [s.sb("y2f%d" % i, [128, 8, 512]) for i in range(2)]
    y2b = s.sb("y2b", [128, 8, 512], BF16)
    sig = [s.sb("sig%d" % i, [128, 512]) for i in range(2)]
    res = [s.sb("res%d" % i, [128, 512]) for i in range(2)]
    ytok = s.sb("ytok", [128, 4, 1024])
    sq = s.sb("sq", [128, 1024], BF16)
    ss = s.sb("ss", [128, 1])
    ob = [s.sb("ob%d" % i, [128, 1024], BF16) for i in range(2)]
    yv = g.y2T.rearrange("(ct c) t -> c ct t", c=128)
    oi = 0
    for tb in range(T // 512):
        t0 = tb * 512
        b = tb % 2
        YK = "y2f%d" % b
        p.dma(y2f[b][:, :, :], yv[:, :, t0:t0 + 512], ["y2T"], [YK])
        s.cp("act", y2b[:, 0:4, :], y2f[b][:, 0:4, :], [YK], ["y2b"])
        s.cp("pool", y2b[:, 4:8, :], y2f[b][:, 4:8, :], [YK], ["y2b"])
        for nt in range(8):
            zb = nt % 2

            def mmZ(e, nt=nt, zb=zb):
                ins = None
                for ct in range(8):
                    ins = e.matmul(psZ[zb][:, :], lhsT=wglu[:, ct, nt * 128:(nt + 1) * 128], rhs=y2b[:, ct, :], start=(ct == 0), stop=(ct == 7))
                return ins
            s.mm(mmZ, ["wglu", "y2b"], ["psZ%d" % zb])
            s.act(sig[zb][:, :], psZ[zb][:, :], AF.Sigmoid, ["psZ%d" % zb, "bcol"], ["sig%d" % zb], bias=bcol[:, nt:nt + 1])
            s.tt("dve" if zb == 0 else "pool", res[zb][:, :], y2f[b][:, nt, :], sig[zb][:, :], ALU.mult, [YK, "sig%d" % zb], ["res%d" % zb])

            def trR(e, zb=zb):
                ins = None
                for tt in range(4):
                    ins = e.transpose(out=psR[:, tt, :], in_=res[zb][:, tt * 128:(tt + 1) * 128], identity=identf[:, :])
                return ins
            s.mm(trR, ["res%d" % zb, "identf"], ["psR"])
            s.cp("dve" if nt % 2 == 0 else "act", ytok[:, :, nt * 128:(nt + 1) * 128], psR[:, :, :], ["psR"], ["ytok"])
        for tt in range(4):
            o_i = oi % 2
            oi += 1
            final_norm(s, g, l, ytok[:, tt, :], "ytok", 1024, t0 + tt * 128, gain[:, :], sq[:, :], ss[:, :], ob[o_i][:, :], "ob%d" % o_i)
    s.done()


def norm_mod_transpose(s, g, l, xsrc, xkey, norm_key, sh_col, sc_col, TB, consume):
    p = s.p
    T = g.T
    NB = T // TB
    NT = TB // 128
    ident, _ = load_ident(s, g)
    A1 = s.sb("A1", [128, D])
    B1 = s.sb("B1", [128, D])
    hT = s.sb("hT", [128, 32, TB], BF16)
    xt = s.sb("xt", [128, D])
    ssq = s.sb("ssq", [128, 1])
    hb = [s.sb("hb%d" % i, [128, D], BF16) for i in range(2)]
    pst = [s.ps("pst%d" % i, [128, 512], BF16) for i in range(2)]
    p.dma(A1[:, :], bcast_rows(g.mod[l:l + 1, sc_col * D:(sc_col + 1) * D]), ["mod"], ["A1"])
    p.dma(xt[:, :], bcast_rows(g.w[norm_key][l]), [], ["xt"])
    p.dma(B1[:, :], bcast_rows(g.mod[l:l + 1, sh_col * D:(sh_col + 1) * D]), ["mod"], ["B1"])
    s.stt("dve", A1[:, :], A1[:, :], 1.0, xt[:, :], ALU.add, ALU.mult, ["A1", "xt"], ["A1"])
    ti = 0
    for nb in range(NB):
        for tt in range(NT):
            b = ti % 2
            ti += 1
            r0 = nb * TB + tt * 128
            p.dma(xt[:, :], xsrc[r0:r0 + 128, :], [xkey], ["xt"])
            s.act(hb[b][:, :], xt[:, :], AF.Square, ["xt"], ["hb%d" % b, "ssq"], accum_out=ssq[:, :])
            s.rsqrt(ssq[:, :], "ssq", 1.0 / D, EPS)
            s.stt("dve", xt[:, :], xt[:, :], ssq[:, 0:1], A1[:, :], ALU.mult, ALU.mult, ["xt", "ssq", "A1"], ["xt"])
            s.tt("pool", hb[b][:, :], xt[:, :], B1[:, :], ALU.add, ["xt", "B1"], ["hb%d" % b])
            for q4 in range(8):
                pb = q4 % 2

                def tr(e, b=b, q4=q4, pb=pb):
                    ins = None
                    for j in range(4):
                        dk = q4 * 4 + j
                        ins = e.transpose(out=pst[pb][:, j * 128:(j + 1) * 128], in_=hb[b][:, dk * 128:(dk + 1) * 128], identity=ident[:, :])
                    return ins
                s.mm(tr, ["hb%d" % b, "ident"], ["pst%d" % pb])
                outap = hT[:, q4 * 4:(q4 + 1) * 4, tt * 128:(tt + 1) * 128]
                inap = pst[pb][:, :].rearrange("q (j t) -> q j t", j=4)
                s.cp("act" if q4 % 2 == 0 else "dve", outap, inap, ["pst%d" % pb], ["hT"])
        consume(nb, hT)


def stage_wout(p, nc, g, l, xsrc, xdst):
    T = g.T
    TB = min(T, 1024)
    NB = T // TB
    NT = TB // 128
    s = Stage(p, nc)
    ident, _ = load_ident(s, g)
    g1 = s.sb("g1", [128, D])
    p.dma(g1[:, :], bcast_rows(g.mod[l:l + 1, 2 * D:3 * D]), ["mod"], ["g1"])
    mT = s.sb("mT", [128, 32, TB], BF16)
    mt = [s.sb("mt%d" % i, [128, D], BF16) for i in range(2)]
    wb = [s.sb("wb%d" % i, [128, 32, 512], BF16) for i in range(2)]
    xs = [s.sb("xs%d" % i, [128, 512]) for i in range(2)]
    tm = [s.sb("tm%d" % i, [128, 512]) for i in range(2)]
    ob = [s.sb("ob%d" % i, [128, 512]) for i in range(2)]
    pst = [s.ps("pst%d" % i, [128, 512], BF16) for i in range(2)]
    pso = [s.ps("pso%d" % i, [128, 512]) for i in range(2)]
    wv = g.w["w_out"][l].rearrange("(kc q) n -> q kc n", q=128)
    ti = wi = oi = 0
    for nb in range(NB):
        for tt in range(NT):
            b = ti % 2
            ti += 1
            r0 = nb * TB + tt * 128
            p.dma(mt[b][:, :], g.mix[r0:r0 + 128, :], ["mix"], ["mt%d" % b])
            for q4 in range(8):
                pb = q4 % 2

                def tr(e, b=b, q4=q4, pb=pb):
                    ins = None
                    for j in range(4):
                        dk = q4 * 4 + j
                        ins = e.transpose(out=pst[pb][:, j * 128:(j + 1) * 128], in_=mt[b][:, dk * 128:(dk + 1) * 128], identity=ident[:, :])
                    return ins
                s.mm(tr, ["mt%d" % b, "ident"], ["pst%d" % pb])
                outap = mT[:, q4 * 4:(q4 + 1) * 4, tt * 128:(tt + 1) * 128]
                inap = pst[pb][:, :].rearrange("q (j t) -> q j t", j=4)
                s.cp("act" if q4 % 2 == 0 else "dve", outap, inap, ["pst%d" % pb], ["mT"])
        for cc in range(8):
            b = wi % 2
            wi += 1
            cols = slice(cc * 512, (cc + 1) * 512)
            p.dma(wb[b][:, :, :], wv[:, :, cols], [], ["wb%d" % b], q="pool")
            for tt in range(NT):
                o_i = oi % 2
                oi += 1
                r0 = nb * TB + tt * 128
                p.dma(xs[o_i][:, :], xsrc[r0:r0 + 128, cols], ["xsrc"], ["xs%d" % o_i])

                def mm(e, b=b, tt=tt, o_i=o_i):
                    ins = None
                    for kc in range(32):
                        ins = e.matmul(pso[o_i][:, :], lhsT=mT[:, kc, tt * 128:(tt + 1) * 128], rhs=wb[b][:, kc, :], start=(kc == 0), stop=(kc == 31))
                    return ins
                s.mm(mm, ["mT", "wb%d" % b], ["pso%d" % o_i])
                s.tt("dve", tm[o_i][:, :], pso[o_i][:, :], g1[:, cols], ALU.mult, ["pso%d" % o_i, "g1"], ["tm%d" % o_i])
                s.tt("pool", ob[o_i][:, :], tm[o_i][:, :], xs[o_i][:, :], ALU.add, ["tm%d" % o_i, "xs%d" % o_i], ["ob%d" % o_i])
                p.dma(xdst[r0:r0 + 128, cols], ob[o_i][:, :], ["ob%d" % o_i], ["xdst"], q="act")
    s.done()


def stage_peer_h(p, nc, g, l, xsrc):
    TB = min(g.T, 512)
    s = Stage(p, nc)
    hv = g.h2T.rearrange("kc d t -> d kc t")

    def consume(nb, hT):
        p.dma(hv[:, :, nb * TB:(nb + 1) * TB], hT[:, :, :], ["hT"], ["h2T"])
    norm_mod_transpose(s, g, l, xsrc, "xsrc", "norm2_g", 3, 4, TB, consume)
    s.done()


def stage_peer_q(p, nc, g, l):
    T = g.T
    TB = min(T, 512)
    s = Stage(p, nc)
    wq = s.sb("wq", [128, 32, 2048], BF16)
    wv = g.w["peer_w_query"][l].rearrange("(kc q) n -> q kc n", q=128)
    for q in range(8):
        p.dma(wq[:, q * 4:(q + 1) * 4, :], wv[:, q * 4:(q + 1) * 4, :], [], ["wq"], q="pool")
    hT = [s.sb("hT%d" % i, [128, 32, TB], BF16) for i in range(2)]
    qs = [s.sb("qs%d" % i, [128, TB], BF16) for i in range(2)]
    psq = [s.ps("psq%d" % i, [128, 512]) for i in range(2)]
    hv = g.h2T.rearrange("kc d t -> d kc t")
    oi = 0
    for nb in range(T // TB):
        b = nb % 2
        p.dma(hT[b][:, :, :], hv[:, :, nb * TB:(nb + 1) * TB], ["h2T"], ["hT%d" % b])
        for cq in range(16):
            o_i = oi % 2
            oi += 1

            def mm(e, b=b, cq=cq, o_i=o_i):
                ins = None
                for kc in range(32):
                    ins = e.matmul(psq[o_i][:, 0:TB], lhsT=wq[:, kc, cq * 128:(cq + 1) * 128], rhs=hT[b][:, kc, :], start=(kc == 0), stop=(kc == 31))
                return ins
            s.mm(mm, ["wq", "hT%d" % b], ["psq%d" % o_i])
            s.cp("act" if o_i == 0 else "dve", qs[o_i][:, :], psq[o_i][:, 0:TB], ["psq%d" % o_i], ["qs%d" % o_i])
            p.dma(g.qT[cq, :, nb * TB:(nb + 1) * TB], qs[o_i][:, :], ["qs%d" % o_i], ["qT"])
    s.done()


def stage_peer_g(p, nc, g, l):
    T = g.T
    K = g.k
    s = Stage(p, nc)
    identb, identf = load_ident(s, g)
    kf = s.sb("kf", [128, 2, 128])
    for pp in range(2):
        p.dma(kf[:, pp, :], g.w["peer_sub_keys"][l][pp], [], ["kf"])
    psK = s.ps("psK", [128, 512])

    def trK(e):
        ins = None
        for pp in range(2):
            ins = e.transpose(out=psK[:, pp * 128:(pp + 1) * 128], in_=kf[:, pp, :], identity=identf[:, :])
        return ins
    s.mm(trK, ["kf", "identf"], ["psK"])
    kT = s.sb("kT", [128, 2, 128], BF16)
    s.cp("dve", kT[:, :, :].rearrange("d a n -> d (a n)"), psK[:, 0:256], ["psK"], ["kT"])
    TN = 32
    iof = s.sb("iof", [128, 128])
    p.dma(iof[:, :], K["iota128"], [], ["iof"])
    iota3 = s.sb("iota3", [128, TN, 128], BF16)
    s.cp("dve", iota3[:, :, :], iof[:, :].unsqueeze(1).to_broadcast([128, TN, 128]), ["iof"], ["iota3"])
    iota16 = s.sb("iota16", [128, 128, 16])
    p.dma(iota16[:, :, :].rearrange("p a b -> p (a b)"), K["iota16"], [], ["iota16"])
    qt = [s.sb("qt%d" % i, [128, 16, 128], BF16) for i in range(2)]
    scs = [s.sb("sc%d" % i, [128, 16, 128]) for i in range(2)]
    tops = [s.sb("top%d" % i, [128, 16, 16]) for i in range(2)]
    work = s.sb("work", [128, 16, 128])
    idxs = [s.sb("idx%d" % i, [128, 16, 16], U32) for i in range(2)]
    idxf = s.sb("idxf", [128, 16, 16])
    cand = s.sb("cand", [128, 8, 16, 16])
    work2 = s.sb("work2", [128, 8, 256])
    best = s.sb("best", [128, 8, 16])
    pos = s.sb("pos", [128, 8, 16], U32)
    pa = s.sb("pa", [128, 128], U32)
    pb_ = s.sb("pb", [128, 128], U32)
    paf = s.sb("paf", [128, 128])
    pbf = s.sb("pbf", [128, 128])
    negm = s.sb("negm", [128, 8])
    eg = s.sb("eg", [128, 8, 16])
    sme = s.sb("sme", [128, 8])
    ohA = s.sb("ohA", [128, 128, 16])
    ohB = s.sb("ohB", [128, 128, 16])
    sel3 = s.sb("sel3", [128, 3, 128])
    selTs = [s.sb("selT%d" % i, [128, 3, 128], BF16) for i in range(2)]
    Ablk = [s.sb("Ablk%d" % i, [128, TN, 128], BF16) for i in range(2)]
    Bblk = [s.sb("Bblk%d" % i, [128, TN, 128], BF16) for i in range(2)]
    Gs = s.sb("Gs", [128, 128, 128], BF16)
    psSc = [s.ps("psSc%d" % i, [128, 4, 128]) for i in range(2)]
    psT3 = s.ps("psT3", [128, 4, 128])
    psG = [s.ps("psG%d" % i, [128, 128, 4]) for i in range(3)]
    qv = g.qT.rearrange("cq d t -> d cq t")
    gv = g.Gd.rearrange("i j t -> j i t")
    bi = 0
    NTL = T // 128

    def phase1(n):
        b = n % 2
        r0 = n * 128
        QT = "qt%d" % b
        sc, top, idx = scs[b], tops[b], idxs[b]
        S = "b%d_" % b
        p.dma(qt[b][:, :, :], qv[:, :, r0:r0 + 128], ["qT"], [QT])
        for grp in range(4):
            sb_ = grp % 2

            def mmS(e, b=b, grp=grp, sb_=sb_):
                ins = None
                for a in range(4):
                    cq = grp * 4 + a
                    ins = e.matmul(psSc[sb_][:, a, :], lhsT=qt[b][:, cq, :], rhs=kT[:, cq % 2, :], start=True, stop=True)
                return ins
            s.mm(mmS, [QT, "kT"], ["psSc%d" % sb_])
            s.cp("act", sc[:, grp * 4:(grp + 1) * 4, :], psSc[sb_][:, :, :], ["psSc%d" % sb_], [S + "sc%d" % grp])
        for cq in range(16):
            p.add("dve", lambda e, cq=cq: e.max(out=top[:, cq, 0:8], in_=sc[:, cq, :]), [S + "sc%d" % (cq // 4)], [S + "topa%d" % cq])
        for cq in range(16):
            p.add("dve", lambda e, cq=cq: e.match_replace(out=work[:, cq, :], in_to_replace=top[:, cq, 0:8], in_values=sc[:, cq, :], imm_value=-1e30),
                  [S + "sc%d" % (cq // 4), S + "topa%d" % cq], ["work%d" % cq])
        for cq in range(16):
            p.add("dve", lambda e, cq=cq: e.max(out=top[:, cq, 8:16], in_=work[:, cq, :]), ["work%d" % cq], [S + "topb%d" % cq])
        for cq in range(16):
            p.add("dve", lambda e, cq=cq: e.max_index(out=idx[:, cq, 0:8], in_max=top[:, cq, 0:8], in_values=sc[:, cq, :]),
                  [S + "sc%d" % (cq // 4), S + "topa%d" % cq], [S + "idxa%d" % cq])
        for cq in range(16):
            p.add("dve", lambda e, cq=cq: e.max_index(out=idx[:, cq, 8:16], in_max=top[:, cq, 8:16], in_values=sc[:, cq, :]),
                  [S + "sc%d" % (cq // 4), S + "topb%d" % cq], [S + "idxb%d" % cq])

    def phase2(n):
        nonlocal bi
        b = n % 2
        r0 = n * 128
        sc, top, idx = scs[b], tops[b], idxs[b]
        S = "b%d_" % b
        selT = selTs[b]
        STK = "selT%d" % b
        s.cp("dve", idxf[:, :, :], idx[:, :, :], [S + "idxa%d" % i for i in range(16)] + [S + "idxb%d" % i for i in range(16)], ["idxf"])
        for h in range(8):
            s.cp("dve", cand[:, h, :, :], top[:, 2 * h, :].unsqueeze(2).to_broadcast([128, 16, 16]),
                 [S + "topa%d" % (2 * h), S + "topb%d" % (2 * h)], ["cand%d" % h])
        for h in range(8):
            s.tt("dve", cand[:, h, :, :], cand[:, h, :, :], top[:, 2 * h + 1, :].unsqueeze(1).to_broadcast([128, 16, 16]), ALU.add,
                 ["cand%d" % h, S + "topa%d" % (2 * h + 1), S + "topb%d" % (2 * h + 1)], ["cand%d" % h])
        cfs = [cand[:, h, :, :].rearrange("p a b -> p (a b)") for h in range(8)]
        for h in range(8):
            p.add("dve", lambda e, h=h: e.max(out=best[:, h, 0:8], in_=cfs[h]), ["cand%d" % h], ["besta%d" % h])
        for h in range(8):
            p.add("dve", lambda e, h=h: e.match_replace(out=work2[:, h, :], in_to_replace=best[:, h, 0:8], in_values=cfs[h], imm_value=-1e30),
                  ["cand%d" % h, "besta%d" % h], ["work2%d" % h])
        for h in range(8):
            p.add("dve", lambda e, h=h: e.max(out=best[:, h, 8:16], in_=work2[:, h, :]), ["work2%d" % h], ["bestb%d" % h])
        for h in range(8):
            p.add("dve", lambda e, h=h: e.max_index(out=pos[:, h, 0:8], in_max=best[:, h, 0:8], in_values=cfs[h]), ["cand%d" % h, "besta%d" % h], ["posa%d" % h])
        for h in range(8):
            p.add("dve", lambda e, h=h: e.max_index(out=pos[:, h, 8:16], in_max=best[:, h, 8:16], in_values=cfs[h]), ["cand%d" % h, "bestb%d" % h], ["posb%d" % h])
        BKS = ["besta%d" % h for h in range(8)] + ["bestb%d" % h for h in range(8)]
        PKS = ["posa%d" % h for h in range(8)] + ["posb%d" % h for h in range(8)]
        s.ts("dve", negm[:, :], best[:, :, 0], -1.0, None, ALU.mult, None, BKS, ["negm"])
        for h in range(8):
            s.act(eg[:, h, :], best[:, h, :], AF.Exp, ["besta%d" % h, "bestb%d" % h, "negm"], ["eg%d" % h, "sme%d" % h], bias=negm[:, h:h + 1], accum_out=sme[:, h:h + 1])
        posf = pos[:, :, :].rearrange("p h k -> p (h k)")
        s.ts("dve", pa[:, :], posf, 4, None, ALU.arith_shift_right, None, PKS, ["pa"])
        s.ts("dve", pb_[:, :], posf, 15, None, ALU.bitwise_and, None, PKS, ["pb"])
        s.cp("dve", paf[:, :], pa[:, :], ["pa"], ["paf"])
        s.cp("dve", pbf[:, :], pb_[:, :], ["pb"], ["pbf"])
        s.tt("dve", ohA[:, :, :], paf[:, :].unsqueeze(2).to_broadcast([128, 128, 16]), iota16[:, :, :], ALU.is_equal, ["paf", "iota16"], ["ohA"])
        s.tt("dve", ohB[:, :, :], pbf[:, :].unsqueeze(2).to_broadcast([128, 128, 16]), iota16[:, :, :], ALU.is_equal, ["pbf", "iota16"], ["ohB"])
        for h in range(8):
            s.tt("dve", ohA[:, h * 16:(h + 1) * 16, :], ohA[:, h * 16:(h + 1) * 16, :],
                 idxf[:, 2 * h, :].unsqueeze(1).to_broadcast([128, 16, 16]), ALU.mult, ["ohA", "idxf"], ["ohA%d" % h])
            s.tt("pool", ohB[:, h * 16:(h + 1) * 16, :], ohB[:, h * 16:(h + 1) * 16, :],
                 idxf[:, 2 * h + 1, :].unsqueeze(1).to_broadcast([128, 16, 16]), ALU.mult, ["ohB", "idxf"], ["ohB%d" % h])
        p.add("dve", lambda e: e.tensor_reduce(out=sel3[:, 0, :], in_=ohA[:, :, :], axis=AX.X, op=ALU.add), ["ohA%d" % h for h in range(8)], ["sel3i"])
        p.add("dve", lambda e: e.tensor_reduce(out=sel3[:, 1, :], in_=ohB[:, :, :], axis=AX.X, op=ALU.add), ["ohB%d" % h for h in range(8)], ["sel3j"])
        p.add("dve", lambda e: e.reciprocal(out=sme[:, :], in_=sme[:, :]), ["sme%d" % h for h in range(8)], ["rs"])
        s.tt("dve", sel3[:, 2, :].rearrange("p (h k) -> p h k", h=8), sme[:, :].unsqueeze(2).to_broadcast([128, 8, 16]), eg[:, :, :], ALU.mult,
             ["rs"] + ["eg%d" % h for h in range(8)], ["sel3g"])

        def trS(e):
            ins = None
            for a in range(3):
                ins = e.transpose(out=psT3[:, a, :], in_=sel3[:, a, :], identity=identf[:, :])
            return ins
        s.mm(trS, ["sel3i", "sel3j", "sel3g", "identf"], ["psT3"])
        s.cp("act", selT[:, :, :], psT3[:, 0:3, :], ["psT3"], [STK])

    def phase3(n):
        nonlocal bi
        b = n % 2
        r0 = n * 128
        selT = selTs[b]
        STK = "selT%d" % b
        for tq in range(128 // TN):
            k = bi % 2
            bi += 1
            tsl = slice(tq * TN, (tq + 1) * TN)
            s.tt("dve", Bblk[k][:, :, :], selT[:, 1, tsl].unsqueeze(2).to_broadcast([128, TN, 128]), iota3[:, :, :], ALU.is_equal,
                 ["iota3", STK], ["B%d" % k])
            s.tt("dve", Ablk[k][:, :, :], selT[:, 0, tsl].unsqueeze(2).to_broadcast([128, TN, 128]), iota3[:, :, :], ALU.is_equal,
                 ["iota3", STK], ["A%d" % k])
            s.tt("pool", Ablk[k][:, :, :], Ablk[k][:, :, :], selT[:, 2, tsl].unsqueeze(2).to_broadcast([128, TN, 128]), ALU.mult,
                 ["A%d" % k, STK], ["A%d" % k])
            for q in range(TN // 4):
                gq_ = tq * (TN // 4) + q
                gb = gq_ % 3

                def mmG(e, k=k, q=q, gb=gb):
                    ins = None
                    for a in range(4):
                        tl = q * 4 + a
                        ins = e.matmul(psG[gb][:, :, a], lhsT=Bblk[k][:, tl, :], rhs=Ablk[k][:, tl, :], start=True, stop=True)
                    return ins
                s.mm(mmG, ["A%d" % k, "B%d" % k], ["psG%d" % gb])
                s.cp("act", Gs[:, :, gq_ * 4:(gq_ + 1) * 4], psG[gb][:, :, :], ["psG%d" % gb], ["Gs"])
        p.dma(gv[:, :, r0:r0 + 128], Gs[:, :, :], ["Gs"], ["Gd"])


    phase1(0)
    for n in range(NTL):
        phase2(n)
        if n + 1 < NTL:
            phase1(n + 1)
        phase3(n)
    s.done()


def stage_peer_ut(p, nc, g, l):
    s = Stage(p, nc)
    identb, identf = load_ident(s, g)
    Uc = [s.sb("Uc%d" % i, [128, D], BF16) for i in range(3)]
    UcT = [s.sb("UcT%d" % i, [128, 32, 128], BF16) for i in range(2)]
    psU = [s.ps("psU%d" % i, [128, 8, 128], BF16) for i in range(4)]
    ui = 0
    for c in range(128):
        b3 = c % 3
        b = c % 2
        p.dma(Uc[b3][:, :], g.w["peer_u"][l][c * 128:(c + 1) * 128, :], [], ["Uc%d" % b3], q="pool")
        for q4 in range(4):
            ub = ui % 4
            ui += 1

            def trU(e, b3=b3, q4=q4, ub=ub):
                ins = None
                for a in range(8):
                    kc = q4 * 8 + a
                    ins = e.transpose(out=psU[ub][:, a, :], in_=Uc[b3][:, kc * 128:(kc + 1) * 128], identity=identb[:, :])
                return ins
            s.mm(trU, ["Uc%d" % b3, "ident"], ["psU%d" % ub])
            s.cp("act" if q4 % 2 == 0 else "dve", UcT[b][:, q4 * 8:(q4 + 1) * 8, :], psU[ub][:, :, :], ["psU%d" % ub], ["UcT%d" % b])
        p.dma(g.UTd[c], UcT[b][:, :, :], ["UcT%d" % b], ["UTd"])
    s.done()


def stage_peer_a(p, nc, g, l):
    T = g.T
    TB = min(T, 1024)
    NH = max(1, TB // 512)
    TW = min(512, TB)
    s = Stage(p, nc)
    NBUF = 4
    PF = 3
    hT = s.sb("hT", [128, 32, TB], BF16)
    UcT = [s.sb("UcT%d" % i, [128, 32, 128], BF16) for i in range(NBUF)]
    Gc = [s.sb("Gc%d" % i, [128, TB], BF16) for i in range(NBUF)]
    ga = [s.sb("ga%d" % i, [128, 512]) for i in range(2)]
    Wc = [s.sb("Wc%d" % i, [128, TB], BF16) for i in range(3)]
    psA = [s.ps("psA%d" % i, [128, 512]) for i in range(6)]
    hv = g.h2T.rearrange("kc d t -> d kc t")
    its = [(nb, c) for nb in range(T // TB) for c in range(128)]

    def load(i):
        nb, c = its[i]
        k = i % NBUF
        tsl = slice(nb * TB, (nb + 1) * TB)
        p.dma(UcT[k][:, :, :], g.UTd[c], ["UTd"], ["UcT%d" % k])
        p.dma(Gc[k][:, :], g.Gd[c, :, tsl], ["Gd"], ["Gc%d" % k])
    ai = 0
    for i in range(min(PF, len(its))):
        load(i)
    for i, (nb, c) in enumerate(its):
        tsl = slice(nb * TB, (nb + 1) * TB)
        if c == 0:
            p.dma(hT[:, :, :], hv[:, :, tsl], ["h2T"], ["hT"], q="act")
        if i + PF < len(its):
            load(i + PF)
        k = i % NBUF
        wb_ = i % 3
        for th in range(NH):
            pk = ai % 6
            gk = ai % 2
            ai += 1

            def mmA(e, k=k, th=th, pk=pk):
                ins = None
                for kc in range(32):
                    ins = e.matmul(psA[pk][:, 0:TW], lhsT=UcT[k][:, kc, :], rhs=hT[:, kc, th * 512:th * 512 + TW], start=(kc == 0), stop=(kc == 31))
                return ins
            s.mm(mmA, ["UcT%d" % k, "hT"], ["psA%d" % pk])
            s.act(ga[gk][:, 0:TW], psA[pk][:, 0:TW], AF.Gelu, ["psA%d" % pk], ["ga%d" % gk])
            s.tt("dve" if gk == 0 else "pool", Wc[wb_][:, th * 512:th * 512 + TW], ga[gk][:, 0:TW], Gc[k][:, th * 512:th * 512 + TW], ALU.mult,
                 ["ga%d" % gk, "Gc%d" % k], ["Wc%d" % wb_])
        p.dma(g.Wd[c, :, tsl], Wc[wb_][:, :], ["Wc%d" % wb_], ["Wd"], q="act")
    s.done()


def stage_peer_b(p, nc, g, l, xsrc, xdst):
    T = g.T
    TB = min(T, 1024)
    NT = TB // 128
    s = Stage(p, nc)
    g2 = s.sb("g2", [128, D])
    p.dma(g2[:, :], bcast_rows(g.mod[l:l + 1, 5 * D:6 * D]), ["mod"], ["g2"])
    NR = 4
    PF = 3
    Vc = [s.sb("Vc%d" % i, [128, 4, 512], BF16) for i in range(NR)]
    Wc = [s.sb("Wc%d" % i, [128, 4, TB], BF16) for i in range(NR)]
    xs = [s.sb("xs%d" % i, [128, 512]) for i in range(2)]
    tm = [s.sb("tm%d" % i, [128, 512]) for i in range(2)]
    ob = [s.sb("ob%d" % i, [128, 512]) for i in range(2)]
    psB = [s.ps("psB%d" % i, [128, 512]) for i in range(NT)]
    vv = g.w["peer_v"][l].rearrange("(c4 a e) d -> c4 e a d", a=4, e=128)
    wv = g.Wd.rearrange("(c4 a) e t -> c4 e a t", a=4)
    its = [(nb, r, c4) for nb in range(T // TB) for r in range(8) for c4 in range(32)]

    def load(i):
        nb, r, c4 = its[i]
        k = i % NR
        p.dma(Vc[k][:, :, :], vv[c4][:, :, r * 512:(r + 1) * 512], [], ["Vc%d" % k], q="pool")
        p.dma(Wc[k][:, :, :], wv[c4][:, :, nb * TB:(nb + 1) * TB], ["Wd"], ["Wc%d" % k])
    for i in range(min(PF, len(its))):
        load(i)
    oi = 0
    for i, (nb, r, c4) in enumerate(its):
        if i + PF < len(its):
            load(i + PF)
        k = i % NR
        cols = slice(r * 512, (r + 1) * 512)

        def mmB(e, k=k, c4=c4):
            ins = None
            for a in range(4):
                for tt in range(NT):
                    ins = e.matmul(psB[tt][:, :], lhsT=Wc[k][:, a, tt * 128:(tt + 1) * 128], rhs=Vc[k][:, a, :],
                                   start=(c4 == 0 and a == 0), stop=(c4 == 31 and a == 3))
            return ins
        s.mm(mmB, ["Vc%d" % k, "Wc%d" % k], ["psB"])
        if c4 == 31:
            for tt in range(NT):
                o_i = oi % 2
                oi += 1
                r0 = nb * TB + tt * 128
                p.dma(xs[o_i][:, :], xsrc[r0:r0 + 128, cols], ["xsrc"], ["xs%d" % o_i], q="act")
                s.tt("dve", tm[o_i][:, :], psB[tt][:, :], g2[:, cols], ALU.mult, ["psB", "g2"], ["tm%d" % o_i])
                s.tt("pool", ob[o_i][:, :], tm[o_i][:, :], xs[o_i][:, :], ALU.add, ["tm%d" % o_i, "xs%d" % o_i], ["ob%d" % o_i])
                p.dma(xdst[r0:r0 + 128, cols], ob[o_i][:, :], ["ob%d" % o_i], ["xdst"], q="act")
    s.done()


def build_all(p, nc, g):
    stage_ada(p, nc, g)
    stage_rot(p, nc, g)
    xs = g.x
    for l in range(g.L):
        xo = g.out if l == g.L - 1 else g.x2
        stage_proj(p, nc, g, l, xs)
        stage_ret(p, nc, g, l)
        stage_ssm_scan(p, nc, g, l)
        stage_ssm_glu(p, nc, g, l)
        stage_swa(p, nc, g, l)
        stage_pool(p, nc, g, l)
        stage_wout(p, nc, g, l, xs, g.x1)
        stage_peer_h(p, nc, g, l, g.x1)
        stage_peer_q(p, nc, g, l)
        stage_peer_g(p, nc, g, l)
        stage_peer_ut(p, nc, g, l)
        stage_peer_a(p, nc, g, l)
        stage_peer_b(p, nc, g, l, g.x1, xo)
        xs = xo


from concourse.bass_utils import run_bass_kernel_spmd

SEQ = 4096
NCORES = 4
_NC_CACHE = {}


def _build(T, L):
    key = (T, L)
    if key not in _NC_CACHE:
        nc = bass.Bass("TRN2", target_bir_lowering=False)
        g = declare(nc, T, L)
        with ExitStack() as st:
            p = Prog(nc, st)
            build_all(p, nc, g)
        _NC_CACHE[key] = nc
    return _NC_CACHE[key]


def _in_map(b, inputs, consts, L):
    m = {}
    m["x"] = np.ascontiguousarray(inputs["x"][b])
    m["c"] = np.ascontiguousarray(inputs["c"][b:b + 1])
    m["pos"] = np.ascontiguousarray(inputs["positions"][b].astype(np.int32).reshape(-1, 1))
    for k, shp in WEIGHT_SHAPES.items():
        a = np.asarray(inputs[k])
        if len(shp) == 1:
            a = a.reshape(L, 1, shp[0])
        m[k] = a
    for k, shp in CONST_SHAPES.items():
        m["k_" + k] = np.ascontiguousarray(consts[k], dtype=np.float32).reshape(shp)
    return m


def kernel(**inputs):
    inputs = {k: np.asarray(v) for k, v in inputs.items()}
    B, S, _ = inputs["x"].shape
    L = inputs["ada_w"].shape[0]
    nc = _build(S, L)
    consts = host_consts_cache()
    maps = [_in_map(i % B, inputs, consts, L) for i in range(NCORES)]
    res = run_bass_kernel_spmd(nc, maps, core_ids=list(range(NCORES)))
    out = np.stack([np.asarray(res.results[b]["out"]) for b in range(B)], axis=0)
    return out.astype(np.float32)
```
